# Optimizing a Trainium2 kernel written in Bass

```python
import math
import jax, jax.numpy as jnp
from jax import lax
import numpy as np

D_MODEL = 1024
BATCH = 8
SEQ = 2048
DEPTH = 2

N_MIXERS = 2
N_RET_LAYERS = (DEPTH + 1) // 2
N_MLA_LAYERS = DEPTH // 2
DN_ALPHA = (2.0 * DEPTH) ** 0.25
DN_BETA = (8.0 * DEPTH) ** -0.25
LN_EPS = 1e-5
ROPE_BASE = 10000.0

RET_HEADS = 4
RET_DK = D_MODEL // RET_HEADS
RET_DV = 2 * RET_DK
RET_HK = RET_HEADS * RET_DK
RET_HV = RET_HEADS * RET_DV
RET_IN = 2 * RET_HK + 2 * RET_HV
RET_CHUNK = 128

MLA_HEADS = 8
MLA_NOPE = 128
MLA_ROPE = 64
MLA_VDIM = 128
MLA_Q_RANK = 384
MLA_KV_RANK = 256
MLA_IN = MLA_Q_RANK + MLA_KV_RANK + MLA_ROPE
MLA_QBLOCK = 128

PEER_HEADS = 8
PEER_NKEYS = 128
PEER_EXPERTS = PEER_NKEYS * PEER_NKEYS
PEER_DKEY = 256
PEER_HALF = PEER_DKEY // 2
PEER_TOPK = 16
PEER_TOKEN_BLOCK = 128

kernel_name = 'hybrid_retention_mla_peer_deepnorm'


def layer_norm(x, g, b):
    xf = x.astype(jnp.float32)
    mu = jnp.mean(xf, axis=-1, keepdims=True)
    var = jnp.mean(jnp.square(xf - mu), axis=-1, keepdims=True)
    return ((xf - mu) * lax.rsqrt(var + LN_EPS) * g.astype(jnp.float32) + b.astype(jnp.float32)).astype(x.dtype)


def rms_norm(x, g):
    xf = x.astype(jnp.float32)
    ms = jnp.mean(jnp.square(xf), axis=-1, keepdims=True)
    return (xf * lax.rsqrt(ms + LN_EPS) * g.astype(jnp.float32)).astype(x.dtype)


def rotary(x, positions):
    d = x.shape[-1]
    inv_freq = ROPE_BASE ** (-jnp.arange(0, d, 2, dtype=jnp.float32) / d)
    ang = positions.astype(jnp.float32)[..., None] * inv_freq
    cos = jnp.cos(ang)[:, :, None, :]
    sin = jnp.sin(ang)[:, :, None, :]
    x1, x2 = jnp.split(x.astype(jnp.float32), 2, axis=-1)
    return jnp.concatenate([x1 * cos - x2 * sin, x1 * sin + x2 * cos], axis=-1).astype(x.dtype)


def retention_scan(q, k, v, log_gamma, strict):
    B, S, H, dk = q.shape
    dv = v.shape[-1]
    C = RET_CHUNK
    n_chunks = S // C
    dt = q.dtype
    idx = jnp.arange(C, dtype=jnp.float32)
    diff = idx[:, None] - idx[None, :]
    keep = (diff > 0) if strict else (diff >= 0)
    lg = log_gamma[:, None, None]
    decay = jnp.where(keep[None], jnp.exp(jnp.maximum(diff, 0.0)[None] * lg), 0.0).astype(dt)
    xi = jnp.exp((idx + 1.0)[None, :] * log_gamma[:, None]).astype(dt)
    zeta = jnp.exp((C - 1.0 - idx)[None, :] * log_gamma[:, None]).astype(dt)
    gamma_c = jnp.exp(C * log_gamma).astype(dt)[None, :, None, None]

    def to_chunks(t):
        return t.reshape(B, n_chunks, C, H, t.shape[-1]).transpose(1, 0, 3, 2, 4)

    def step(state, inp):
        qi, ki, vi = inp
        inner = jnp.einsum('bhnm,bhmv->bhnv', jnp.einsum('bhnd,bhmd->bhnm', qi, ki) * decay[None], vi)
        cross = jnp.einsum('bhnd,bhdv->bhnv', qi, state) * xi[None, :, :, None]
        state = gamma_c * state + jnp.einsum('bhmd,bhmv->bhdv', ki * zeta[None, :, :, None], vi)
        return state, inner + cross

    state0 = jnp.zeros((B, H, dk, dv), dt)
    _, out = lax.scan(step, state0, (to_chunks(q), to_chunks(k), to_chunks(v)))
    return out.transpose(1, 0, 3, 2, 4).reshape(B, S, H, dv)


def retention_mixer(x, positions, w_in, log1m_decay, gn_g, gn_b, w_out):
    B, S, _ = x.shape
    proj = x @ w_in
    q, k, v, g = jnp.split(proj, [RET_HK, 2 * RET_HK, 2 * RET_HK + RET_HV], axis=-1)
    q = rotary(q.reshape(B, S, RET_HEADS, RET_DK), positions)
    k = rotary(k.reshape(B, S, RET_HEADS, RET_DK), positions) * (RET_DK ** -0.5)
    v = v.reshape(B, S, RET_HEADS, RET_DV)
    log_gamma = jnp.log1p(-jnp.exp(log1m_decay.astype(jnp.float32)))
    fwd = retention_scan(q, k, v, log_gamma[0], strict=False)
    bwd = jnp.flip(retention_scan(jnp.flip(q, 1), jnp.flip(k, 1), jnp.flip(v, 1), log_gamma[1], strict=True), 1)
    y = (fwd + bwd).astype(jnp.float32)
    mu = jnp.mean(y, axis=-1, keepdims=True)
    var = jnp.mean(jnp.square(y - mu), axis=-1, keepdims=True)
    yn = ((y - mu) * lax.rsqrt(var + LN_EPS)).reshape(B, S, RET_HV)
    yn = (yn * gn_g.astype(jnp.float32) + gn_b.astype(jnp.float32)).astype(x.dtype)
    return (jax.nn.silu(g) * yn) @ w_out


def mla_mixer(x, positions, w_in, q_norm_g, kv_norm_g, w_uq, w_ukv, w_out):
    B, S, _ = x.shape
    c_q, c_kv, k_rope = jnp.split(x @ w_in, [MLA_Q_RANK, MLA_Q_RANK + MLA_KV_RANK], axis=-1)
    q = (rms_norm(c_q, q_norm_g) @ w_uq).reshape(B, S, MLA_HEADS, MLA_NOPE + MLA_ROPE)
    q_nope, q_rope = jnp.split(q, [MLA_NOPE], axis=-1)
    q_rope = rotary(q_rope, positions)
    kv = (rms_norm(c_kv, kv_norm_g) @ w_ukv).reshape(B, S, MLA_HEADS, MLA_NOPE + MLA_VDIM)
    k_nope, v = jnp.split(kv, [MLA_NOPE], axis=-1)
    k_rope = rotary(k_rope[:, :, None, :], positions)[:, :, 0, :]
    scale = (MLA_NOPE + MLA_ROPE) ** -0.5
    nb = S // MLA_QBLOCK

    def blocks(t):
        return t.reshape(B, nb, MLA_QBLOCK, MLA_HEADS, t.shape[-1]).transpose(1, 0, 2, 3, 4)

    def attend(blk):
        qn, qr = blk
        s = jnp.einsum('bqhd,bkhd->bhqk', qn, k_nope) + jnp.einsum('bqhd,bkd->bhqk', qr, k_rope)
        p = jax.nn.softmax(s.astype(jnp.float32) * scale, axis=-1).astype(v.dtype)
        return jnp.einsum('bhqk,bkhd->bqhd', p, v)

    o = lax.map(attend, (blocks(q_nope), blocks(q_rope)))
    o = o.transpose(1, 0, 2, 3, 4).reshape(B, S, MLA_HEADS * MLA_VDIM)
    return o @ w_out


def peer_ffn(x, w_q, sub_keys, u, v):
    B, S, D = x.shape
    T = B * S
    xt = x.reshape(T // PEER_TOKEN_BLOCK, PEER_TOKEN_BLOCK, D)

    def block(xb):
        q = (xb @ w_q).reshape(PEER_TOKEN_BLOCK, PEER_HEADS, 2, PEER_HALF)
        s = jnp.einsum('thcd,hckd->thck', q, sub_keys).astype(jnp.float32)
        s1, i1 = lax.top_k(s[:, :, 0], PEER_TOPK)
        s2, i2 = lax.top_k(s[:, :, 1], PEER_TOPK)
        cand_s = (s1[..., :, None] + s2[..., None, :]).reshape(PEER_TOKEN_BLOCK, PEER_HEADS, PEER_TOPK * PEER_TOPK)
        cand_i = (i1[..., :, None] * PEER_NKEYS + i2[..., None, :]).reshape(PEER_TOKEN_BLOCK, PEER_HEADS, PEER_TOPK * PEER_TOPK)
        top_s, pos = lax.top_k(cand_s, PEER_TOPK)
        eidx = jnp.take_along_axis(cand_i, pos, axis=-1)
        gate = jax.nn.softmax(top_s, axis=-1).astype(xb.dtype)
        h = jax.nn.gelu(jnp.einsum('thkd,td->thk', u[eidx], xb), approximate=False)
        return jnp.einsum('thk,thkd->td', gate * h, v[eidx])

    return lax.map(block, xt).reshape(B, S, D)


def setup_inputs(seed: int = 0) -> dict:
    key = jax.random.key(seed)
    ks = jax.random.split(key, 24)
    f32 = jnp.float32
    nrm = lambda k, shape, s: jax.random.normal(k, shape, f32) * s
    x = jax.random.normal(ks[0], (BATCH, SEQ, D_MODEL), f32)
    offsets = jax.random.randint(ks[1], (BATCH, 1), 0, 512, dtype=jnp.int32)
    positions = (offsets + jnp.arange(SEQ, dtype=jnp.int32)[None, :]).astype(jnp.int32)
    ret_w_in = nrm(ks[2], (N_RET_LAYERS, D_MODEL, RET_IN), D_MODEL ** -0.5)
    base = (-5.0 - jnp.arange(RET_HEADS, dtype=f32)) * math.log(2.0)
    ret_log1m_decay = base[None, None, :] + nrm(ks[3], (N_RET_LAYERS, 2, RET_HEADS), 0.05)
    ret_gn_g = 1.0 + nrm(ks[4], (N_RET_LAYERS, RET_HV), 0.02)
    ret_gn_b = nrm(ks[5], (N_RET_LAYERS, RET_HV), 0.02)
    ret_w_out = nrm(ks[6], (N_RET_LAYERS, RET_HV, D_MODEL), DN_BETA * RET_HV ** -0.5)
    mla_w_in = nrm(ks[7], (N_MLA_LAYERS, D_MODEL, MLA_IN), D_MODEL ** -0.5)
    mla_q_norm = 1.0 + nrm(ks[8], (N_MLA_LAYERS, MLA_Q_RANK), 0.02)
    mla_kv_norm = 1.0 + nrm(ks[9], (N_MLA_LAYERS, MLA_KV_RANK), 0.02)
    mla_w_uq = nrm(ks[10], (N_MLA_LAYERS, MLA_Q_RANK, MLA_HEADS * (MLA_NOPE + MLA_ROPE)), MLA_Q_RANK ** -0.5)
    mla_w_ukv = nrm(ks[11], (N_MLA_LAYERS, MLA_KV_RANK, MLA_HEADS * (MLA_NOPE + MLA_VDIM)), MLA_KV_RANK ** -0.5)
    mla_w_out = nrm(ks[12], (N_MLA_LAYERS, MLA_HEADS * MLA_VDIM, D_MODEL), DN_BETA * (MLA_HEADS * MLA_VDIM) ** -0.5)
    peer_w_q = nrm(ks[13], (DEPTH, D_MODEL, PEER_HEADS * PEER_DKEY), D_MODEL ** -0.5)
    peer_sub_keys = nrm(ks[14], (DEPTH, PEER_HEADS, 2, PEER_NKEYS, PEER_HALF), PEER_HALF ** -0.5)
    peer_u = nrm(ks[15], (DEPTH, PEER_EXPERTS, D_MODEL), D_MODEL ** -0.5)
    peer_v = nrm(ks[16], (DEPTH, PEER_EXPERTS, D_MODEL), DN_BETA * (PEER_HEADS * PEER_TOPK) ** -0.5)
    ln_mix_g = 1.0 + nrm(ks[17], (DEPTH, D_MODEL), 0.02)
    ln_mix_b = nrm(ks[18], (DEPTH, D_MODEL), 0.02)
    ln_ffn_g = 1.0 + nrm(ks[19], (DEPTH, D_MODEL), 0.02)
    ln_ffn_b = nrm(ks[20], (DEPTH, D_MODEL), 0.02)
    return {'x': x, 'positions': positions,
            'ret_w_in': ret_w_in, 'ret_log1m_decay': ret_log1m_decay, 'ret_gn_g': ret_gn_g, 'ret_gn_b': ret_gn_b, 'ret_w_out': ret_w_out,
            'mla_w_in': mla_w_in, 'mla_q_norm': mla_q_norm, 'mla_kv_norm': mla_kv_norm, 'mla_w_uq': mla_w_uq, 'mla_w_ukv': mla_w_ukv, 'mla_w_out': mla_w_out,
            'peer_w_q': peer_w_q, 'peer_sub_keys': peer_sub_keys, 'peer_u': peer_u, 'peer_v': peer_v,
            'ln_mix_g': ln_mix_g, 'ln_mix_b': ln_mix_b, 'ln_ffn_g': ln_ffn_g, 'ln_ffn_b': ln_ffn_b}


def reference(x, positions,
              ret_w_in, ret_log1m_decay, ret_gn_g, ret_gn_b, ret_w_out,
              mla_w_in, mla_q_norm, mla_kv_norm, mla_w_uq, mla_w_ukv, mla_w_out,
              peer_w_q, peer_sub_keys, peer_u, peer_v,
              ln_mix_g, ln_mix_b, ln_ffn_g, ln_ffn_b):
    h = x
    for i in range(DEPTH):
        j = i // N_MIXERS
        if i % N_MIXERS == 0:
            mix = retention_mixer(h, positions, ret_w_in[j], ret_log1m_decay[j], ret_gn_g[j], ret_gn_b[j], ret_w_out[j])
        else:
            mix = mla_mixer(h, positions, mla_w_in[j], mla_q_norm[j], mla_kv_norm[j], mla_w_uq[j], mla_w_ukv[j], mla_w_out[j])
        h = layer_norm(DN_ALPHA * h + mix, ln_mix_g[i], ln_mix_b[i])
        ffn = peer_ffn(h, peer_w_q[i], peer_sub_keys[i], peer_u[i], peer_v[i])
        h = layer_norm(DN_ALPHA * h + ffn, ln_ffn_g[i], ln_ffn_b[i])
    return h
```

```python
import math, os, contextlib
import numpy as np
import concourse.bass as bass
import concourse.mybir as mybir
from concourse.bass_utils import run_bass_kernel_spmd

F32 = mybir.dt.float32
BF16 = mybir.dt.bfloat16
I32 = mybir.dt.int32
U32 = mybir.dt.uint32
ALU = mybir.AluOpType
AF = mybir.ActivationFunctionType
AX = mybir.AxisListType

ENGS = ["tensor", "vector", "scalar", "gpsimd", "sync"]
MAIN_MULT_ENG = os.environ.get("MAIN_MULT_ENG", "vector")
SELF_WAR = bool(int(os.environ.get("SELF_WAR", "1")))

S = 2048
D = 1024
NTB = 16
ALPHA = (2.0 * 2) ** 0.25
EPS = 1e-5
TWO_PI = 2.0 * math.pi
C1 = float(np.float32(TWO_PI))
C2 = TWO_PI - C1
PI_LO = 3.1415925


class Tr:
    __slots__ = ("name", "w", "r", "sem", "cnt")

    def __init__(self, name=""):
        self.name = name
        self.w = {}
        self.r = {}
        self.sem = None
        self.cnt = 0


class Op:
    __slots__ = ("eng", "fn", "deps", "dma", "sig", "semtr", "val", "raw", "post")


class Prog:
    def __init__(self, nc):
        self.nc = nc
        self.q = {e: [] for e in ENGS}
        self.all = []
        self._ord = {}

    def op(self, eng, fn, reads=(), writes=(), dma=False, post=None):
        o = Op()
        o.eng = eng; o.fn = fn; o.dma = dma; o.sig = False; o.val = None; o.post = post
        o.semtr = writes[0] if dma else None
        key = ("d", id(o.semtr)) if dma else ("e", eng)
        deps = []
        raw = set()
        for t in reads:
            for d in t.w.values():
                deps.append(d); raw.add(id(d))
        for t in writes:
            deps.extend(t.r.values())
            deps.extend(t.w.values())
        seen = set(); d2 = []
        for d in deps:
            if id(d) not in seen and d is not o:
                seen.add(id(d)); d2.append(d)
        o.deps = d2
        o.raw = raw
        rset = set(id(t) for t in reads)
        for t in writes:
            if t.r or (id(t) in rset):
                t.w = {key: o}; t.r = {}
            else:
                t.w[key] = o
        wset = set(id(t) for t in writes)
        for t in reads:
            if id(t) not in wset:
                t.r[key] = o
        self.q[eng].append(o)
        self._ord[id(o)] = len(self.all)
        self.all.append(o)
        return o

    def alias(self, new_trs, old_trs):
        merged = {}
        for t in old_trs:
            for dct in (t.w, t.r):
                for k, o in dct.items():
                    b = merged.get(k)
                    if b is None or self._ord[id(o)] > self._ord[id(b)]:
                        merged[k] = o
        for t in new_trs:
            t.w = {}
            t.r = dict(merged)

    def finalize_and_emit(self, final_waits=()):
        nc = self.nc
        for o in self.all:
            best = {}
            for d in o.deps:
                if d.dma:
                    k = ("d", id(d.semtr))
                else:
                    if d.eng == o.eng and (o.eng == "tensor" or (id(d) not in o.raw and not SELF_WAR)):
                        continue
                    k = ("e", d.eng)
                b = best.get(k)
                if b is None or self._ord[id(d)] > self._ord[id(b)]:
                    best[k] = d
            o.deps = list(best.values())
            for d in o.deps:
                d.sig = True
        for o in final_waits:
            o.sig = True
        for o in self.all:
            if o.dma:
                o.sig = o.sig or (os.environ.get("DMASIG","1")=="1")
        stack = contextlib.ExitStack()
        engsem = {}
        for e in ENGS:
            engsem[e] = stack.enter_context(nc.semaphore("s_" + e))
        engcnt = {e: 0 for e in ENGS}
        nsem = 0
        for o in self.all:
            if not o.sig:
                continue
            if o.dma:
                t = o.semtr
                if t.sem is None:
                    t.sem = stack.enter_context(nc.semaphore("d%d" % nsem))
                    nsem += 1
                t.cnt += 16
                o.val = (t.sem, t.cnt)
            else:
                engcnt[o.eng] += 1
                o.val = (engsem[o.eng], engcnt[o.eng])
        self.nsem = nsem
        block = stack.enter_context(nc.Block())
        prog = self

        def emit(engname, engobj):
            waited = {}
            for o in prog.q[engname]:
                for d in o.deps:
                    s, v = d.val
                    k = id(s)
                    if waited.get(k, 0) >= v:
                        continue
                    waited[k] = v
                    engobj.wait_ge(s, v)
                ins = o.fn(engobj)
                if o.post is not None and o.sig:
                    ins = o.post(engobj)
                if o.sig:
                    s, v = o.val
                    ins.then_inc(s, 16 if o.dma else 1)
            if engname == "sync":
                for o in final_waits:
                    s, v = o.val
                    engobj.wait_ge(s, v)

        @block.tensor
        def _(e):
            emit("tensor", e)

        @block.vector
        def _(e):
            emit("vector", e)

        @block.scalar
        def _(e):
            emit("scalar", e)

        @block.gpsimd
        def _(e):
            emit("gpsimd", e)

        @block.sync
        def _(e):
            emit("sync", e)

        stack.close()


class K:
    def __init__(self, nc, stop_after=None):
        self.nc = nc
        self.P = Prog(nc)
        self.stop_after = stop_after
        self.st = contextlib.ExitStack()

    def v(self, eng, fn, r=(), w=(), post=None):
        return self.P.op(eng, fn, reads=list(r), writes=list(w), post=post)

    def dma(self, eng, out, in_, r=(), w=(), **kw):
        return self.P.op(eng, lambda e: e.dma_start(out=out, in_=in_, **kw), reads=list(r), writes=list(w), dma=True)

    def mm(self, out, lhsT, rhs, start, stop, r=(), w=()):
        return self.P.op("tensor", lambda e: e.matmul(out, lhsT=lhsT, rhs=rhs, start=start, stop=stop),
                         reads=list(r), writes=list(w))

    def tp(self, out, in_, ident, r=(), w=()):
        return self.P.op("tensor", lambda e: e.transpose(out=out, in_=in_, identity=ident), reads=list(r), writes=list(w))

    def act(self, out, in_, func, r=(), w=(), **kw):
        return self.P.op("scalar", lambda e: e.activation(out=out, in_=in_, func=func, **kw), reads=list(r), writes=list(w))

    def tt(self, eng, out, in0, in1, op, r=(), w=()):
        return self.P.op(eng, lambda e: e.tensor_tensor(out=out, in0=in0, in1=in1, op=op), reads=list(r), writes=list(w))

    def ts(self, eng, out, in0, s1, s2, op0, op1=None, r=(), w=()):
        if op1 is None:
            return self.P.op(eng, lambda e: e.tensor_scalar(out=out, in0=in0, scalar1=s1, scalar2=None, op0=op0),
                             reads=list(r), writes=list(w))
        return self.P.op(eng, lambda e: e.tensor_scalar(out=out, in0=in0, scalar1=s1, scalar2=s2, op0=op0, op1=op1),
                         reads=list(r), writes=list(w))

    def stt(self, eng, out, in0, scalar, in1, op0, op1, r=(), w=()):
        return self.P.op(eng, lambda e: e.scalar_tensor_tensor(out=out, in0=in0, scalar=scalar, in1=in1, op0=op0, op1=op1),
                         reads=list(r), writes=list(w))

    def cp(self, eng, out, in_, r=(), w=()):
        if eng == "scalar":
            return self.P.op(eng, lambda e: e.copy(out=out, in_=in_), reads=list(r), writes=list(w))
        return self.P.op(eng, lambda e: e.tensor_copy(out=out, in_=in_), reads=list(r), writes=list(w))

    def sbuf(self, name, shape, dt):
        return self.st.enter_context(self.nc.sbuf_tensor(name, shape, dt))

    def psum(self, name, shape, dt):
        return self.st.enter_context(self.nc.psum_tensor(name, shape, dt))

    @staticmethod
    def carve(region, off_bytes, shape, dt):
        n = 1
        for s_ in shape[1:]:
            n *= s_
        esz = 4 if dt in (F32, I32, U32) else 2
        nb = n * esz
        assert off_bytes % 4 == 0 and nb % 4 == 0
        ap = region[0:shape[0], off_bytes // 4:(off_bytes + nb) // 4]
        if dt != F32:
            ap = ap.bitcast(dt)
        if len(shape) == 3:
            ap = ap.rearrange("p (a b) -> p a b", a=shape[1])
        elif len(shape) == 4:
            ap = ap.rearrange("p (a b c) -> p a b c", a=shape[1], b=shape[2])
        return ap

    def build(self):
        nc = self.nc
        dt_ = nc.dram_tensor
        I = {}

        def din(name, shape, dt=F32):
            I[name] = dt_(name, shape, dt, kind="ExternalInput").ap()

        din("x", [S, D]); din("pos", [1, S], I32)
        din("rwin", [D, 6144]); din("rl1d", [1, 8]); din("rgng", [1, 2048]); din("rgnb", [1, 2048]); din("rwout", [2048, D])
        din("mwin", [D, 704]); din("mqn", [128, 3]); din("mkvn", [128, 2]); din("mwuq", [384, 1536]); din("mwukv", [256, 2048])
        din("mwout", [D, D])
        din("pwq", [2, D, 2048]); din("pskT", [2, 128, 16, 128]); din("puT", [2, 128, 128, 1024]); din("pv", [2, 16384, D])
        din("lnmg", [2, D]); din("lnmb", [2, D]); din("lnfg", [2, D]); din("lnfb", [2, D])
        din("cinvf", [128, 4])
        self.I = I
        self.out = dt_("out", [S, D], F32, kind="ExternalOutput").ap()
        self.ubf = dt_("ubf", [2, 128 * 128, 1024], BF16, kind="Internal").ap()
        self.vbf = dt_("vbf", [2, 16384, 1024], BF16, kind="Internal").ap()
        self.tUbf = [Tr("ubf0"), Tr("ubf1")]
        self.tVbf = [Tr("vbf0"), Tr("vbf1")]

        self.RH = self.sbuf("RH", [128, NTB, D], F32)
        self.RHT = self.sbuf("RHT", [128, 8192], F32)
        self.RG = self.sbuf("RG", [128, 16384], F32)
        self.RX = self.sbuf("RX", [128, 10752], F32)
        self.RC = self.sbuf("RC", [128, 1408], F32)
        self.tH = [Tr("H%d" % i) for i in range(NTB)]
        self.tHT = [Tr("HT%d" % i) for i in range(NTB)]
        self.HT = self.carve(self.RHT, 0, [128, 8, S], BF16)
        self.tRG = [Tr("RG")]
        self.tRHT_alias = []
        self.PB = [self.psum("pb%d" % i, [128, 512], F32) for i in range(8)]
        self.tPB = [Tr("pb%d" % i) for i in range(8)]

        RC = self.RC
        self.identf = RC[:, 0:128]
        self.iotaf = RC[:, 128:256]
        self.identb = RC[:, 256:320].bitcast(BF16)
        self.invf = RC[:, 320:324]
        self.lg = RC[:, 324:332]
        self.iota16 = RC[:, 332:348]
        self.onesf = RC[:, 384:512]
        self.dmy = RC[:, 584:640]
        self._dmy_i = 0
        self.iotab = RC[:, 520:584].bitcast(BF16)
        self.tC = Tr("consts")
        tC = self.tC
        tmpi = self.carve(self.RX, 0, [128, 128], I32)
        ttmp = Tr("tmpi")
        self.v("gpsimd", lambda e: e.iota(tmpi, pattern=[[1, 128]], base=0, channel_multiplier=0), w=[ttmp])
        self.cp("vector", self.iotaf, tmpi, r=[ttmp], w=[tC])
        self.cp("vector", self.iota16, tmpi[:, 0:16], r=[ttmp], w=[tC])
        self.cp("vector", self.iotab, tmpi, r=[ttmp], w=[tC])
        self.v("gpsimd", lambda e: e.iota(tmpi, pattern=[[1, 128]], base=0, channel_multiplier=-1), w=[ttmp])
        tmpf = self.carve(self.RX, 512, [128, 128], F32)
        ttmpf = Tr("tmpf")
        self.cp("vector", tmpf, tmpi, r=[ttmp], w=[ttmpf])
        self.v("vector", lambda e: e.tensor_single_scalar(out=self.identf, in_=tmpf, scalar=0.0, op=ALU.is_equal), r=[ttmpf], w=[tC])
        self.v("vector", lambda e: e.tensor_single_scalar(out=self.identb, in_=tmpf, scalar=0.0, op=ALU.is_equal), r=[ttmpf], w=[tC])
        self.v("vector", lambda e: e.memset(self.onesf, 1.0), w=[tC])
        self.dma("sync", self.invf, I["cinvf"], w=[tC])
        self.tXtmp = [ttmp, ttmpf]

        for tb in range(NTB):
            self.dma("sync", self.RH[:, tb, :], I["x"][tb * 128:(tb + 1) * 128, :], w=[self.tH[tb]])

        self.fin = []
        self.puT_flat = I["puT"].rearrange("l c p e -> l (c p) e")
        self.retention()
        if self.stop_after == "ret":
            return self.finish()
        self.peer(0)
        if self.stop_after == "peer0":
            return self.finish()
        self.mla()
        if self.stop_after == "mla":
            return self.finish()
        self.peer(1)
        return self.finish()

    def convert_tables(self, l, part, nparts):
        n = 32 // nparts
        for i in range(part * n, (part + 1) * n):
            rs = slice(i * 512, (i + 1) * 512)
            self.dma("gpsimd", self.ubf[l, rs, :], self.puT_flat[l, rs, :], w=[self.tUbf[l]])
            self.dma("gpsimd", self.vbf[l, rs, :], self.I["pv"][l, rs, :], w=[self.tVbf[l]])

    def finish(self):
        for tb in range(NTB):
            o = self.dma("sync", self.out[tb * 128:(tb + 1) * 128, :], self.RH[:, tb, :], r=[self.tH[tb]], w=[Tr("out%d" % tb)])
            self.fin.append(o)
        self.P.finalize_and_emit(final_waits=self.fin)
        self.st.close()

    def build_HT(self):
        hb = self.carve(self.RX, 0, [128, D], BF16)
        thb = Tr("hb")
        self.P.alias([thb], self.tXtmp)
        pbT = self.PB[6][:].bitcast(BF16).rearrange("p (a b) -> p a b", a=8)
        for tb in range(NTB):
            self.cp("scalar", hb, self.RH[:, tb, :], r=[self.tH[tb]], w=[thb])
            for kc in range(8):
                self.tp(pbT[:, kc, :], hb[:, kc * 128:(kc + 1) * 128], self.identb, r=[thb, self.tC], w=[self.tPB[6]])
            self.cp("vector", self.HT[:, :, tb * 128:(tb + 1) * 128], pbT, r=[self.tPB[6]], w=[self.tHT[tb]])
        self.tXtmp = [thb]

    def trig_tables(self, col, nparts, cosT, sinT, tcs, sgn_col=None):
        I = self.I
        RX = self.RX
        posi = self.carve(self.RG, 0, [128, S], I32)
        ang = self.carve(self.RG, 8192, [128, S], F32)
        ki = self.carve(self.RG, 16384, [128, S], I32)
        kf = self.carve(self.RG, 24576, [128, S], F32)
        r = self.carve(self.RG, 32768, [128, S], F32)
        rc = self.carve(self.RG, 40960, [128, S], F32)
        tt_ = [Tr("trig%d" % i) for i in range(6)]
        self.P.alias(tt_, self.tRG)
        tpos, tang, tki, tkf, tr_, trc = tt_
        n = nparts
        self.dma("sync", posi, I["pos"].partition_broadcast(128), w=[tpos])
        self.cp("vector", ang, posi, r=[tpos], w=[tang])
        self.ts("vector", ang[0:n], ang[0:n], self.invf[0:n, col:col + 1], None, ALU.mult, r=[tang, self.tC], w=[tang])
        self.ts("vector", ki[0:n], ang[0:n], 1.0 / TWO_PI, None, ALU.mult, r=[tang], w=[tki])
        self.cp("vector", kf[0:n], ki[0:n], r=[tki], w=[tkf])
        self.stt("vector", r[0:n], kf[0:n], -C1, ang[0:n], ALU.mult, ALU.add, r=[tkf, tang], w=[tr_])
        self.stt("vector", r[0:n], kf[0:n], -C2, r[0:n], ALU.mult, ALU.add, r=[tkf, tr_], w=[tr_])
        self.ts("vector", rc[0:n], r[0:n], math.pi / 2, None, ALU.add, r=[tr_], w=[trc])
        self.ts("vector", kf[0:n], rc[0:n], math.pi, -TWO_PI, ALU.is_gt, ALU.mult, r=[trc], w=[tkf])
        self.tt("vector", rc[0:n], rc[0:n], kf[0:n], ALU.add, r=[trc, tkf], w=[trc])
        self.ts("vector", r[0:n], r[0:n], PI_LO, -PI_LO, ALU.min, ALU.max, r=[tr_], w=[tr_])
        self.ts("vector", rc[0:n], rc[0:n], PI_LO, -PI_LO, ALU.min, ALU.max, r=[trc], w=[trc])
        self.act(sinT[0:n], r[0:n], AF.Sin, r=[tr_], w=[tcs])
        self.act(cosT[0:n], rc[0:n], AF.Sin, r=[trc], w=[tcs])
        if sgn_col is not None:
            self.ts("vector", sinT[0:n], sinT[0:n], self.invf[0:n, sgn_col:sgn_col + 1], None, ALU.mult, r=[tcs, self.tC], w=[tcs])
        self.tRG = tt_

    def layer_norm(self, tb, g_ap, b_ap, tgb, scr):
        Hb = self.RH[:, tb, :]
        tH = self.tH[tb]
        st6, mv, rstd = scr["st6"], scr["mv"], scr["rstd"]
        tS = scr["t"]
        self.v("vector", lambda e: e.bn_stats(out=st6[:, 0:6], in_=Hb[:, 0:512]), r=[tH], w=[tS])
        self.v("vector", lambda e: e.bn_stats(out=st6[:, 6:12], in_=Hb[:, 512:1024]), r=[tH], w=[tS])
        self.v("vector", lambda e: e.bn_aggr(out=mv, in_=st6), r=[tS], w=[tS])
        self.act(rstd, mv[:, 1:2], AF.Sqrt, r=[tS], w=[tS], bias=EPS, scale=1.0)
        self.v("vector", lambda e: e.reciprocal(out=rstd, in_=rstd), r=[tS], w=[tS])
        self.ts("vector", Hb, Hb, mv[:, 0:1], rstd, ALU.subtract, ALU.mult, r=[tH, tS], w=[tH])
        self.tt("vector", Hb, Hb, g_ap, ALU.mult, r=[tH, tgb], w=[tH])
        self.tt("vector", Hb, Hb, b_ap, ALU.add, r=[tH, tgb], w=[tH])

    def load_ln(self, gsrc, bsrc, off):
        g = self.carve(self.RX, off, [128, D], F32)
        b = self.carve(self.RX, off + 4096, [128, D], F32)
        t = Tr("lngb")
        self.P.alias([t], self.tLN if hasattr(self, "tLN") else [])
        self.dma("sync", g, gsrc.partition_broadcast(128), w=[t])
        self.dma("sync", b, bsrc.partition_broadcast(128), w=[t])
        self.tLN = [t]
        return g, b, t

    def retention(self):
        I = self.I
        RX, RG = self.RX, self.RG
        cosT = self.carve(RX, 2048, [128, S], F32)
        sinT = self.carve(RX, 2048 + 8192, [128, S], F32)
        tcs = Tr("cossin")
        self.trig_tables(0, 128, cosT, sinT, tcs)
        tlg = self.tC
        l1 = self.carve(RX, 40000, [128, 8], F32)
        tl1 = Tr("l1")
        self.dma("sync", l1, I["rl1d"].partition_broadcast(128), w=[tl1])
        self.act(l1, l1, AF.Exp, r=[tl1], w=[tl1])
        self.act(self.lg, l1, AF.Ln, r=[tl1], w=[tlg], scale=-1.0, bias=1.0)
        self.build_HT()

        tmp = [self.carve(RX, 18432 + i * 2048, [128, 512], F32) for i in range(4)]
        ttmp = [Tr("rtmp%d" % i) for i in range(4)]
        gng = self.carve(RX, 26624, [128, 512], F32)
        gnb = self.carve(RX, 26624 + 2048, [128, 512], F32)
        tgn = Tr("gn")
        ZT = self.carve(RX, 30720, [128, 4, 128], BF16)
        tZT = Tr("ZT")
        small = self.carve(RX, 40064, [128, 32], F32)
        tsm = Tr("small")
        scr = {"st6": small[:, 0:12], "mv": small[:, 12:14], "rstd": small[:, 14:15], "t": tsm}
        lng, lnb, tln = self.load_ln(I["lnmg"][0:1, :], I["lnmb"][0:1, :], 31744)

        QT = self.carve(RG, 0, [128, 2, S], BF16)
        KT = self.carve(RG, 8192, [128, 2, S], BF16)
        VH = self.carve(RG, 16384, [128, NTB, 512], BF16)
        PT = self.carve(RG, 32768, [128, NTB, 256], BF16)
        Dtab = self.carve(RG, 40960, [128, 31, 128], BF16)
        W1 = self.carve(RG, 49152, [128, 8, 512], BF16)
        W2 = self.carve(RG, 57344, [128, 8, 512], BF16)
        W2o = self.carve(RG, 57344, [128, 4, 1024], BF16)
        tQT, tKT, tVH, tPT, tD, tW1, tW2 = [Tr(n) for n in ("QT", "KT", "VH", "PT", "Dtab", "W1", "W2")]
        self.P.alias([tQT, tKT, tVH, tPT, tD, tW1, tW2], self.tRG)
        Zb2 = [self.carve(RX, i * 1024, [128, 512], BF16) for i in range(2)]
        tZb2 = [Tr("Zb0"), Tr("Zb1")]
        self.P.alias(tZb2, self.tXtmp)
        tsm2 = [Tr("small0"), Tr("small1")]
        scr2 = [{"st6": small[:, 16 * i:16 * i + 12], "mv": small[:, 16 * i + 12:16 * i + 14], "rstd": small[:, 16 * i + 14:16 * i + 15], "t": tsm2[i]}
                for i in range(2)]
        iot = tmp[3].bitcast(I32)

        PB, tPB = self.PB, self.tPB
        HT, tHT = self.HT, self.tHT
        rwin = I["rwin"]
        for h in range(4):
            self.dma("gpsimd", W1[:, :, 0:256], rwin[:, h * 256:(h + 1) * 256].rearrange("(k p) c -> p k c", p=128), w=[tW1])
            self.dma("gpsimd", W1[:, :, 256:512], rwin[:, 1024 + h * 256:1024 + (h + 1) * 256].rearrange("(k p) c -> p k c", p=128), w=[tW1])
            self.dma("gpsimd", W2, rwin[:, 2048 + h * 512:2048 + (h + 1) * 512].rearrange("(k p) c -> p k c", p=128), w=[tW2])
            self.dma("sync", gng, I["rgng"][0:1, h * 512:(h + 1) * 512].partition_broadcast(128), w=[tgn])
            self.dma("sync", gnb, I["rgnb"][0:1, h * 512:(h + 1) * 512].partition_broadcast(128), w=[tgn])
            dflat = Dtab.rearrange("p a b -> p (a b)")
            for pc in range(8):
                r0 = pc * 4
                nr = min(4, 31 - r0)
                w_ = nr * 128
                io_v = iot[:, 0:w_]
                self.v("gpsimd", lambda e, io_v=io_v, nr=nr, r0=r0: e.iota(io_v, pattern=[[128, nr], [1, 128]], base=128 * (r0 - 15), channel_multiplier=-1),
                       w=[ttmp[3]])
                self.ts("vector", tmp[0][:, 0:w_], iot[:, 0:w_], 0.0, self.lg[:, h:h + 1], ALU.max, ALU.mult, r=[ttmp[3], self.tC], w=[ttmp[0]])
                self.ts("vector", tmp[1][:, 0:w_], iot[:, 0:w_], 0.0, self.lg[:, 4 + h:5 + h], ALU.min, ALU.mult, r=[ttmp[3], self.tC], w=[ttmp[1]])
                self.tt("vector", tmp[0][:, 0:w_], tmp[0][:, 0:w_], tmp[1][:, 0:w_], ALU.subtract, r=[ttmp[0], ttmp[1]], w=[ttmp[0]])
                self.act(dflat[:, r0 * 128:r0 * 128 + w_], tmp[0][:, 0:w_], AF.Exp, r=[ttmp[0]], w=[tD], bias=-math.log(16.0), scale=1.0)
            for which, dst, tdst in ((0, QT, tQT), (1, KT, tKT)):
                for tq in range(4):
                    tsl = slice(tq * 512, (tq + 1) * 512)
                    hts = [tHT[tq * 4 + i] for i in range(4)]
                    for c in range(2):
                        for kc in range(8):
                            self.mm(PB[c][:], W1[:, kc, which * 256 + c * 128: which * 256 + (c + 1) * 128], HT[:, kc, tsl],
                                    kc == 0, kc == 7, r=[tW1] + hts, w=[tPB[c]])
                    self.tt("vector", tmp[0], PB[0][:], cosT[:, tsl], ALU.mult, r=[tPB[0], tcs], w=[ttmp[0]])
                    self.tt("vector", tmp[1], PB[1][:], sinT[:, tsl], ALU.mult, r=[tPB[1], tcs], w=[ttmp[1]])
                    self.tt("vector", tmp[2], PB[0][:], sinT[:, tsl], ALU.mult, r=[tPB[0], tcs], w=[ttmp[2]])
                    self.tt("vector", tmp[3], PB[1][:], cosT[:, tsl], ALU.mult, r=[tPB[1], tcs], w=[ttmp[3]])
                    self.tt("gpsimd", dst[:, 0, tsl], tmp[0], tmp[1], ALU.subtract, r=[ttmp[0], ttmp[1]], w=[tdst])
                    self.tt("gpsimd", dst[:, 1, tsl], tmp[2], tmp[3], ALU.add, r=[ttmp[2], ttmp[3]], w=[tdst])
            for tb in range(NTB):
                pb = tb % 2
                for kc in range(8):
                    self.mm(PB[pb][:], HT[:, kc, tb * 128:(tb + 1) * 128], W2[:, kc, :], kc == 0, kc == 7, r=[tW2, tHT[tb]], w=[tPB[pb]])
                self.cp("scalar", VH[:, tb, :], PB[pb][:], r=[tPB[pb]], w=[tVH])
            self.dma("gpsimd", W1, rwin[:, 4096 + h * 512:4096 + (h + 1) * 512].rearrange("(k p) c -> p k c", p=128), w=[tW1])
            self.dma("gpsimd", W2o, I["rwout"][h * 512:(h + 1) * 512, :].rearrange("(k p) c -> p k c", p=128), w=[tW2])
            self.convert_tables(0, h, 4)
            pending = None
            for nb2 in range(8):
                nsl = slice(nb2 * 256, (nb2 + 1) * 256)
                for mb in range(NTB):
                    sb_ = (2, 3, 1)[mb % 3]
                    for kc in range(2):
                        self.mm(PB[sb_][:, 0:256], KT[:, kc, mb * 128:(mb + 1) * 128], QT[:, kc, nsl], kc == 0, kc == 1,
                                r=[tKT, tQT], w=[tPB[sb_]])
                    rel0 = 2 * nb2 - mb + 15
                    self.tt("vector", PT[:, mb, :], PB[sb_][:, 0:256], Dtab[:, rel0:rel0 + 2, :].rearrange("p a b -> p (a b)"), ALU.mult,
                            r=[tPB[sb_], tD], w=[tPT])
                for sub in range(2):
                    nb = nb2 * 2 + sub
                    ob = 4 + (nb % 2)
                    par = nb % 2
                    t_y, t_g = tmp[2 * par], tmp[2 * par + 1]
                    tt_y, tt_g = ttmp[2 * par], ttmp[2 * par + 1]
                    for mb in range(NTB):
                        self.mm(PB[ob][:], PT[:, mb, sub * 128:(sub + 1) * 128], VH[:, mb, :], mb == 0, mb == NTB - 1,
                                r=[tPT, tVH], w=[tPB[ob]])
                    for kc in range(8):
                        self.mm(PB[0][:], HT[:, kc, nb * 128:(nb + 1) * 128], W1[:, kc, :], kc == 0, kc == 7, r=[tW1, tHT[nb]], w=[tPB[0]])
                    if pending is not None:
                        pending()
                    self.act(t_g, PB[0][:], AF.Silu, r=[tPB[0]], w=[tt_g])
                    sc = scr2[par]
                    st6, mv, rstd, tsm_ = sc["st6"], sc["mv"], sc["rstd"], sc["t"]
                    self.v("vector", lambda e, ob=ob, st6=st6: e.bn_stats(out=st6[:, 0:6], in_=PB[ob][:]), r=[tPB[ob]], w=[tsm_])
                    self.v("vector", lambda e, st6=st6, mv=mv: e.bn_aggr(out=mv, in_=st6[:, 0:6]), r=[tsm_], w=[tsm_])
                    self.act(rstd, mv[:, 1:2], AF.Sqrt, r=[tsm_], w=[tsm_], bias=EPS, scale=1.0)
                    self.v("vector", lambda e, rstd=rstd: e.reciprocal(out=rstd, in_=rstd), r=[tsm_], w=[tsm_])
                    self.ts("vector", t_y, PB[ob][:], mv[:, 0:1], rstd, ALU.subtract, ALU.mult, r=[tPB[ob], tsm_], w=[tt_y])
                    self.tt("vector", t_y, t_y, gng, ALU.mult, r=[tt_y, tgn], w=[tt_y])
                    self.tt("vector", t_y, t_y, gnb, ALU.add, r=[tt_y, tgn], w=[tt_y])
                    self.tt("vector", Zb2[par], t_y, t_g, ALU.mult, r=[tt_y, tt_g], w=[tZb2[par]])

                    def tail(nb=nb, par=par):
                        pbT = PB[6][:].bitcast(BF16).rearrange("p (a b) -> p a b", a=8)
                        for fc in range(4):
                            self.tp(pbT[:, fc, :], Zb2[par][:, fc * 128:(fc + 1) * 128], self.identb, r=[tZb2[par], self.tC], w=[tPB[6]])
                        self.cp("scalar", ZT, pbT[:, 0:4, :], r=[tPB[6]], w=[tZT])
                        for half in range(2):
                            wb = 7
                            hs = slice(half * 512, (half + 1) * 512)
                            for fc in range(4):
                                self.mm(PB[wb][:], ZT[:, fc, :], W2o[:, fc, hs], fc == 0, fc == 3, r=[tZT, tW2], w=[tPB[wb]])
                            if h == 0:
                                self.stt("vector", self.RH[:, nb, hs], self.RH[:, nb, hs], ALPHA, PB[wb][:], ALU.mult, ALU.add,
                                         r=[self.tH[nb], tPB[wb]], w=[self.tH[nb]])
                            else:
                                self.tt("vector", self.RH[:, nb, hs], self.RH[:, nb, hs], PB[wb][:], ALU.add,
                                        r=[self.tH[nb], tPB[wb]], w=[self.tH[nb]])
                    pending = tail
            pending()
        for tb in range(NTB):
            self.layer_norm(tb, lng, lnb, tln, scr2[0])
        self.tRG = [tQT, tKT, tVH, tPT, tD, tW1, tW2]
        self.tXtmp = tZb2
        self.tXall = ttmp + [tgn, tZT, tsm, tcs, tl1] + tsm2

    def peer(self, l):
        I = self.I
        RX, RG, RHT = self.RX, self.RG, self.RHT
        PB, tPB = self.PB, self.tPB
        P = self.P
        lng, lnb, tln = self.load_ln(I["lnfg"][l:l + 1, :], I["lnfb"][l:l + 1, :], 31744)
        Hs = [self.carve(RX, i * 512, [128, 256], BF16) for i in range(2)]
        Wt = [self.carve(RX, 1024 + i * 512, [128, 256], BF16) for i in range(2)]
        tHs = [Tr("Hs%d" % i) for i in range(2)]
        tWt = [Tr("Wt%d" % i) for i in range(2)]
        P.alias(tHs + tWt, self.tXtmp)
        NS = 3
        UB = [self.carve(RX, 2048 + i * 4096, [128, 8, 128], BF16) for i in range(NS)]
        VB = [self.carve(RX, 2048 + i * 4096 + 2048, [128, D], BF16) for i in range(NS)]
        tU = [Tr("U%d" % i) for i in range(NS)]
        tV = [Tr("V%d" % i) for i in range(NS)]
        xT = [self.carve(RX, 14336 + i * 4096, [128, 8, 256], BF16) for i in range(2)]
        txT = [Tr("xT0"), Tr("xT1")]
        Ab = [self.carve(RX, 22528 + i * 2048, [128, 8, 128], BF16) for i in range(2)]
        Bb = [self.carve(RX, 26624 + i * 2048, [128, 8, 128], BF16) for i in range(2)]
        tA = [Tr("A%d" % i) for i in range(2)]
        tB = [Tr("B%d" % i) for i in range(2)]
        o0 = 22528
        tv = self.carve(RX, o0, [128, 16, 16], F32)
        ti = self.carve(RX, o0 + 1024, [128, 16, 16], U32)
        tif = self.carve(RX, o0 + 1024, [128, 16, 16], F32)
        cv = self.carve(RX, o0 + 2048, [128, 8, 16], F32)
        ci = self.carve(RX, o0 + 2560, [128, 8, 16], U32)
        ee = self.carve(RX, o0 + 3072, [128, 8, 16], F32)
        r1u = self.carve(RX, o0 + 3584, [128, 8, 16], U32)
        r1f = self.carve(RX, o0 + 3584, [128, 8, 16], F32)
        r2u = self.carve(RX, o0 + 4096, [128, 8, 16], U32)
        r2f = self.carve(RX, o0 + 4096, [128, 8, 16], F32)
        abw = [[self.carve(RX, o0 + 4608 + (tb2 * 3 + k_) * 512, [128, 8, 16], F32) for k_ in range(3)] for tb2 in range(2)]
        tsmall = [Tr("sm%d" % i) for i in range(13)]
        ttv0, tti0, tcv0, tci0, tee, tr1, tr2, tabw, ttv1, tti1, tcv1, tci1, ts2b = tsmall
        ttv, tti, tcv, tci = [ttv0, ttv1], [tti0, tti1], [tcv0, tcv1], [tci0, tci1]
        aT = self.RC[:, 640:768].bitcast(BF16)
        bT = self.RC[:, 768:896].bitcast(BF16)
        wT = self.RC[:, 896:1152]
        taT = Tr("abwT")
        small = self.carve(RX, 40064, [128, 64], F32)
        tsm = Tr("small")
        scr = {"st6": small[:, 0:12], "mv": small[:, 12:14], "rstd": small[:, 14:15], "t": tsm}
        Zs = small[:, 16:24]
        rz = small[:, 24:32]
        tZs = Tr("Zs")
        newX = tU + tV + txT + tA + tB + [taT, tsm, tZs]
        P.alias(newX, self.tXall)
        Gs = self.carve(RG, 0, [128, 256, 128], BF16)
        tGs = Tr("Gs")
        P.alias([tGs], self.tRG)
        qT = self.carve(RHT, 0, [128, 16, 256], BF16)
        skT = self.carve(RHT, 8192, [128, 16, 128], BF16)
        wqc = [self.carve(RHT, 12288 + i * 2048, [128, 8, 128], BF16) for i in range(2)]
        s_ = self.carve(RHT, 16384, [128, 16, 128], F32)
        s2 = self.carve(RHT, 24576, [128, 16, 128], F32)
        cand = self.carve(RHT, 16384, [128, 8, 256], F32)
        cand2 = self.carve(RHT, 24576, [128, 8, 256], F32)
        eq = self.carve(RHT, 16384, [128, 8, 16, 16], F32)
        tq_, tsk, ts_, ts2 = [Tr(n) for n in ("qT", "skT", "s", "s2")]
        twq = [Tr("wqc0"), Tr("wqc1")]
        scratch_trs = [tq_, tsk, ts_, ts2] + twq
        P.alias(scratch_trs + [ts2b], self.tHT)
        def post(e):
            i_ = self._dmy_i
            self._dmy_i = (i_ + 1) % 28
            return e.memset(self.dmy[:, 2 * i_:2 * i_ + 2], 0.0)
        tvv = tv.rearrange("p (h c) r -> p h c r", c=2)
        tifv = tif.rearrange("p (h c) r -> p h c r", c=2)
        cand4 = cand.rearrange("p h (a b) -> p h a b", a=16)
        iota16 = self.iota16
        B4 = [128, 8, 16, 16]
        B3 = [128, 8, 16]
        B3b = [128, 8, 128]
        pwq = I["pwq"]

        def top16(src, src2, tsrc, tsrc2, vals, idxs, tvals, tidxs, n):
            hn = n // 2
            for g0 in range(hn):
                pair = ((g0, 0), (g0 + hn, 1))
                for g_, ln in pair:
                    self.v("vector", lambda e, g_=g_: e.max(out=vals[:, g_, 0:8], in_=src[:, g_, :]), r=[tsrc], w=[tvals[ln]])
                for g_, ln in pair:
                    self.v("vector", lambda e, g_=g_: e.max_index(out=idxs[:, g_, 0:8], in_max=vals[:, g_, 0:8], in_values=src[:, g_, :]),
                           r=[tsrc, tvals[ln]], w=[tidxs[ln]], post=post)
                for g_, ln in pair:
                    self.v("vector", lambda e, g_=g_: e.match_replace(out=src2[:, g_, :], in_to_replace=vals[:, g_, 0:8], in_values=src[:, g_, :], imm_value=-1e30),
                           r=[tsrc, tvals[ln]], w=[tsrc2[ln]], post=post)
                yield
                for g_, ln in pair:
                    self.v("vector", lambda e, g_=g_: e.max(out=vals[:, g_, 8:16], in_=src2[:, g_, :]), r=[tsrc2[ln]], w=[tvals[ln]])
                for g_, ln in pair:
                    self.v("vector", lambda e, g_=g_: e.max_index(out=idxs[:, g_, 8:16], in_max=vals[:, g_, 8:16], in_values=src2[:, g_, :]),
                           r=[tsrc2[ln], tvals[ln]], w=[tidxs[ln]], post=post)
                yield

        def phase1a(g, buf):
            for tb2 in range(2):
                tb = 2 * g + tb2
                for kq in range(2):
                    pb = 6 + kq
                    for j in range(4):
                        kc = kq * 4 + j
                        self.tp(PB[pb][:, j * 128:(j + 1) * 128], self.RH[:, tb, kc * 128:(kc + 1) * 128], self.identf,
                                r=[self.tH[tb], self.tC], w=[tPB[pb]])
                    self.cp("scalar", xT[buf][:, kq * 4:(kq + 1) * 4, tb2 * 128:(tb2 + 1) * 128],
                            PB[pb][:].rearrange("p (a b) -> p a b", a=4), r=[tPB[pb]], w=[txT[buf]])
            yield
            for i in range(2):
                self.dma("gpsimd", skT[:, i * 8:(i + 1) * 8, :], I["pskT"][l, :, i * 8:(i + 1) * 8, :], w=[tsk])
            for gq in range(16):
                wb = gq % 2
                self.dma("gpsimd", wqc[wb], pwq[l, :, gq * 128:(gq + 1) * 128].rearrange("(k p) c -> p k c", p=128), w=[twq[wb]])
                pb = 6 + gq % 2
                for kc in range(8):
                    self.mm(PB[pb][:, 0:256], wqc[wb][:, kc, :], xT[buf][:, kc, :], kc == 0, kc == 7, r=[twq[wb], txT[buf]], w=[tPB[pb]])
                self.cp("scalar", qT[:, gq, :], PB[pb][:, 0:256], r=[tPB[pb]], w=[tq_])
                yield
            for tb2 in range(2):
                a_, b_, w_ = abw[tb2]
                for q4 in range(4):
                    pb = 6 + q4 % 2
                    for j in range(4):
                        gq = q4 * 4 + j
                        self.mm(PB[pb][:, j * 128:(j + 1) * 128], qT[:, gq, tb2 * 128:(tb2 + 1) * 128], skT[:, gq, :], True, True,
                                r=[tq_, tsk], w=[tPB[pb]])
                    self.cp("scalar", s_[:, q4 * 4:(q4 + 1) * 4, :], PB[pb][:].rearrange("p (a b) -> p a b", a=4), r=[tPB[pb]], w=[ts_])
                yield
                if tb2 == 0:
                    P.alias(tsmall, tA + tB)
                yield from top16(s_, s2, ts_, [ts2, ts2b], tv, ti, ttv, tti, 16)
                self.cp("vector", tif, ti, r=tti, w=tti)
                self.tt("vector", cand4, tvv[:, :, 0, :].unsqueeze(3).broadcast_to(B4), tvv[:, :, 1, :].unsqueeze(2).broadcast_to(B4), ALU.add,
                        r=ttv, w=[ts_])
                yield
                yield from top16(cand, cand2, ts_, [ts2, ts2b], cv, ci, tcv, tci, 8)
                self.tt("vector", ee, cv, cv[:, :, 0:1].broadcast_to(B3), ALU.subtract, r=tcv, w=[tee])
                self.act(ee, ee, AF.Exp, r=[tee], w=[tee])
                self.v("vector", lambda e: e.reduce_sum(out=Zs, in_=ee, axis=AX.X), r=[tee], w=[tZs])
                self.v("vector", lambda e: e.reciprocal(out=rz, in_=Zs), r=[tZs], w=[tZs])
                self.tt("vector", w_, ee, rz.unsqueeze(2).broadcast_to(B3), ALU.mult, r=[tee, tZs], w=[tabw])
                yield
                self.v("vector", lambda e: e.tensor_single_scalar(out=r1u, in_=ci, scalar=4, op=ALU.logical_shift_right), r=tci, w=[tr1])
                self.v("vector", lambda e: e.tensor_single_scalar(out=r2u, in_=ci, scalar=15, op=ALU.bitwise_and), r=tci, w=[tr2])
                self.cp("vector", r1f, r1u, r=[tr1], w=[tr1])
                self.cp("vector", r2f, r2u, r=[tr2], w=[tr2])
                yield
                io4 = iota16.unsqueeze(1).unsqueeze(1).broadcast_to(B4)
                for (rf, trf, cc, dst) in ((r1f, tr1, 0, a_), (r2f, tr2, 1, b_)):
                    self.tt("vector", eq, rf.unsqueeze(3).broadcast_to(B4), io4, ALU.is_equal, r=[trf, self.tC], w=[ts_])
                    self.tt("vector", eq, eq, tifv[:, :, cc, :].unsqueeze(2).broadcast_to(B4), ALU.mult, r=[ts_] + tti, w=[ts_])
                    self.v("vector", lambda e, dst=dst: e.tensor_reduce(out=dst, in_=eq, axis=AX.X, op=ALU.add), r=[ts_], w=[tabw])
                    yield
            yield "TAIL"
            for tb2 in range(2):
                a_, b_, w_ = abw[tb2]
                pb = 6 + tb2
                for k_, src in enumerate((a_, b_, w_)):
                    self.tp(PB[pb][:, k_ * 128:(k_ + 1) * 128], src.rearrange("p a b -> p (a b)"), self.identf, r=[tabw, self.tC], w=[tPB[pb]])
                for k_, dstT in enumerate((aT, bT, wT)):
                    self.cp("scalar", dstT[:, tb2 * 128:(tb2 + 1) * 128], PB[pb][:, k_ * 128:(k_ + 1) * 128], r=[tPB[pb]], w=[taT])
            P.alias(tA + tB, tsmall)

        def build(g):
            for t8 in range(32):
                sl = t8 % 2
                tsl8 = slice(t8 * 8, (t8 + 1) * 8)
                iob = self.iotab.unsqueeze(1).broadcast_to(B3b)
                self.tt("vector", Ab[sl], iob, aT[:, tsl8].unsqueeze(2).broadcast_to(B3b), ALU.is_equal, r=[self.tC, taT], w=[tA[sl]])
                self.tt("vector", Ab[sl], Ab[sl], wT[:, tsl8].unsqueeze(2).broadcast_to(B3b), ALU.mult, r=[tA[sl], taT], w=[tA[sl]])
                self.tt("vector", Bb[sl], iob, bT[:, tsl8].unsqueeze(2).broadcast_to(B3b), ALU.is_equal, r=[self.tC, taT], w=[tB[sl]])
                for q in range(2):
                    pb = 6 + q
                    for j in range(4):
                        jj = q * 4 + j
                        self.mm(PB[pb][:, j * 128:(j + 1) * 128], Bb[sl][:, jj, :], Ab[sl][:, jj, :], True, True, r=[tA[sl], tB[sl]], w=[tPB[pb]])
                    t0_ = t8 * 8 + q * 4
                    self.cp("scalar", Gs[:, t0_:t0_ + 4, :], PB[pb][:].rearrange("p (a b) -> p a b", a=4), r=[tPB[pb]], w=[tGs])

        gen = phase1a(0, 0)
        for _ in gen:
            pass
        for g in range(8):
            buf = g % 2
            gen = phase1a(g + 1, 1 - buf) if g + 1 < 8 else iter(())
            gen_tail = False
            for _ in range(0):
                try:
                    next(gen)
                except StopIteration:
                    break
            build(g)

            def pull(n):
                nonlocal gen_tail
                if gen_tail:
                    return
                for _ in range(n):
                    try:
                        r_ = next(gen)
                    except StopIteration:
                        gen_tail = True
                        return
                    if r_ == "TAIL":
                        gen_tail = True
                        return

            def load(c):
                sl = c % NS
                self.dma("sync", UB[sl].rearrange("p k e -> p (k e)"), self.ubf[l, c * 128:(c + 1) * 128, :], r=[self.tUbf[l]], w=[tU[sl]])
                self.dma("sync", VB[sl], self.vbf[l, c * 128:(c + 1) * 128, :], r=[self.tVbf[l]], w=[tV[sl]])

            def umm(c):
                pp = 4 + c % 2
                sl = c % NS
                for kc in range(8):
                    self.mm(PB[pp][:, 0:256], UB[sl][:, kc, :], xT[buf][:, kc, :], kc == 0, kc == 7, r=[tU[sl], txT[buf]], w=[tPB[pp]])

            def mid(c):
                pp = 4 + c % 2
                self.act(Hs[c % 2], PB[pp][:, 0:256], AF.Gelu, r=[tPB[pp]], w=[tHs[c % 2]])
                self.tt(MAIN_MULT_ENG, Wt[c % 2], Hs[c % 2], Gs[:, :, c], ALU.mult, r=[tHs[c % 2], tGs], w=[tWt[c % 2]])

            def vmm(c):
                sl = c % NS
                for tb2 in range(2):
                    for half in range(2):
                        bk = tb2 * 2 + half
                        self.mm(PB[bk][:], Wt[c % 2][:, tb2 * 128:(tb2 + 1) * 128], VB[sl][:, half * 512:(half + 1) * 512], c == 0, c == 127,
                                r=[tWt[c % 2], tV[sl]], w=[tPB[bk]])

            for c in range(NS - 1):
                load(c)
            umm(0); mid(0)
            for c in range(128):
                if c + NS - 1 < 128:
                    load(c + NS - 1)
                if c + 1 < 128:
                    umm(c + 1); mid(c + 1)
                vmm(c)
                pull(1)
            while True:
                try:
                    next(gen)
                except StopIteration:
                    break
            for tb2 in range(2):
                tb = 2 * g + tb2
                for half in range(2):
                    hs = slice(half * 512, (half + 1) * 512)
                    bk = tb2 * 2 + half
                    self.stt("vector", self.RH[:, tb, hs], self.RH[:, tb, hs], ALPHA, PB[bk][:], ALU.mult, ALU.add,
                             r=[self.tH[tb], tPB[bk]], w=[self.tH[tb]])
                self.layer_norm(tb, lng, lnb, tln, scr)
        P.alias(self.tHT, scratch_trs + [ts2b])
        self.tRG = [tGs]
        self.tXtmp = tHs + tWt
        self.tXall = newX + tsmall

    def mla(self):
        I = self.I
        RX, RG = self.RX, self.RG
        PB, tPB = self.PB, self.tPB
        HT, tHT = self.HT, self.tHT
        P = self.P
        SCALE = 192.0 ** -0.5
        lng, lnb, tln = self.load_ln(I["lnmg"][1:2, :], I["lnmb"][1:2, :], 31744)
        self.build_HT()
        cos64 = self.carve(RX, 2048, [128, S], F32)
        sin64 = self.carve(RX, 2048 + 8192, [128, S], F32)
        tcs = Tr("cs64")
        tmp = [self.carve(RX, 18432 + i * 2048, [128, 512], F32) for i in range(4)]
        ttmp = [Tr("mtmp%d" % i) for i in range(4)]
        mwsw = self.carve(RX, 26624, [128, 8, 64], BF16)
        wuqsw = self.carve(RX, 27648, [128, 3, 8, 64], BF16)
        tsw = Tr("sw")
        gains = self.carve(RX, 39936, [128, 8], F32)
        tgain = Tr("gains")
        small = self.carve(RX, 40064, [128, 64], F32)
        tsm = Tr("small")
        scr = {"st6": small[:, 0:12], "mv": small[:, 12:14], "rstd": small[:, 14:15], "t": tsm}
        newX = [tcs, tsw, tgain, tsm] + ttmp
        P.alias(newX, self.tXall)
        Ob2 = [self.carve(RX, 512 + i * 256, [128, 128], BF16) for i in range(2)]
        OT = self.carve(RX, 256, [128, 128], BF16)
        tOb2, tOT = [Tr("Ob0"), Tr("Ob1")], Tr("OT")
        P.alias(tOb2 + [tOT], self.tXtmp)
        rz2 = [small[:, 16:17], small[:, 17:18]]
        trz2 = [Tr("rz0"), Tr("rz1")]
        P.alias(trz2, self.tXall)
        self.trig_tables(1, 64, cos64, sin64, tcs, sgn_col=2)
        cnT = self.carve(RG, 0, [128, 5, S], BF16)
        kropeT = self.carve(RG, 20480, [128, S], BF16)
        wuq = self.carve(RG, 24576, [128, 3, 1536], BF16)
        wukv = self.carve(RG, 33792, [128, 2, 2048], BF16)
        wout = self.carve(RG, 41984, [128, 8, 1024], BF16)
        mwin = self.carve(RG, 41984, [128, 8, 704], BF16)
        craw = self.carve(RG, 58368, [128, 3, 512], F32)
        tcn, tkr, twuq, twukv, twout, tcraw = [Tr(n) for n in ("cnT", "kropeT", "wuq", "wukv", "wout", "craw")]
        P.alias([tcn, tkr, twuq, twukv, twout, tcraw], self.tRG)
        self.dma("gpsimd", mwin, I["mwin"].rearrange("(k p) c -> p k c", p=128), w=[twout])
        for kc in range(3):
            self.dma("gpsimd", wuq[:, kc, :], I["mwuq"][kc * 128:(kc + 1) * 128, :], w=[twuq])
        for kc in range(2):
            for hh in range(2):
                self.dma("gpsimd", wukv[:, kc, hh * 1024:(hh + 1) * 1024], I["mwukv"][kc * 128:(kc + 1) * 128, hh * 1024:(hh + 1) * 1024], w=[twukv])
        self.dma("sync", gains[:, 0:3], I["mqn"], w=[tgain])
        self.dma("sync", gains[:, 3:5], I["mkvn"], w=[tgain])
        self.cp("vector", mwsw[:, :, 0:32], mwin[:, :, 672:704], r=[twout], w=[tsw])
        self.cp("vector", mwsw[:, :, 32:64], mwin[:, :, 640:672], r=[twout], w=[tsw])
        wuqv = wuq.rearrange("p k (h c) -> p k h c", h=8)
        self.cp("vector", wuqsw[:, :, :, 0:32], wuqv[:, :, :, 160:192], r=[twuq], w=[tsw])
        self.cp("vector", wuqsw[:, :, :, 32:64], wuqv[:, :, :, 128:160], r=[twuq], w=[tsw])
        self.v("gpsimd", lambda e: e.memset(kropeT[64:128, :], 0.0), w=[tkr])
        MS = int(os.environ.get("MLA_STOP", "99"))
        if MS <= 1:
            return
        for tq in range(4):
            tsl = slice(tq * 512, (tq + 1) * 512)
            hts = [tHT[tq * 4 + i] for i in range(4)]
            for (fcs, sumbank, nfeat, goff) in (((0, 1, 2), 2, 384.0, 0), ((3, 4), 3, 256.0, 3)):
                for i_, fc in enumerate(fcs):
                    pb = fc % 2
                    for kc in range(8):
                        self.mm(PB[pb][:], mwin[:, kc, fc * 128:(fc + 1) * 128], HT[:, kc, tsl], kc == 0, kc == 7, r=[twout] + hts, w=[tPB[pb]])
                    self.cp("vector", craw[:, i_, :], PB[pb][:], r=[tPB[pb]], w=[tcraw])
                    if "norm" in os.environ.get("MLA_SKIP", ""):
                        continue
                    self.tt("vector", tmp[pb], craw[:, i_, :], PB[pb][:], ALU.mult, r=[tPB[pb], tcraw], w=[ttmp[pb]])
                    if "ones" not in os.environ.get("MLA_SKIP", ""):
                        self.mm(PB[sumbank][:], self.onesf, tmp[pb], i_ == 0, i_ == len(fcs) - 1, r=[self.tC, ttmp[pb]], w=[tPB[sumbank]])
                if "norm" in os.environ.get("MLA_SKIP", "") or "sqrt" in os.environ.get("MLA_SKIP", ""):
                    continue
                self.act(tmp[2], PB[sumbank][:], AF.Sqrt, r=[tPB[sumbank]], w=[ttmp[2]], scale=1.0 / nfeat, bias=EPS)
                self.v("vector", lambda e: e.reciprocal(out=tmp[2], in_=tmp[2]), r=[ttmp[2]], w=[ttmp[2]])
                for i_, fc in enumerate(fcs):
                    self.stt("vector", cnT[:, fc, tsl], craw[:, i_, :], gains[:, goff + i_:goff + i_ + 1], tmp[2], ALU.mult, ALU.mult,
                             r=[tcraw, tgain, ttmp[2]], w=[tcn])
            if "rope" in os.environ.get("MLA_SKIP", ""):
                continue
            for kc in range(8):
                self.mm(PB[6][0:64, :], mwin[:, kc, 640:704], HT[:, kc, tsl], kc == 0, kc == 7, r=[twout] + hts, w=[tPB[6]])
            for kc in range(8):
                self.mm(PB[7][0:64, :], mwsw[:, kc, :], HT[:, kc, tsl], kc == 0, kc == 7, r=[tsw] + hts, w=[tPB[7]])
            self.tt("vector", tmp[0][0:64], PB[6][0:64, :], cos64[0:64, tsl], ALU.mult, r=[tPB[6], tcs], w=[ttmp[0]])
            self.tt("vector", tmp[1][0:64], PB[7][0:64, :], sin64[0:64, tsl], ALU.mult, r=[tPB[7], tcs], w=[ttmp[1]])
            self.tt("gpsimd", kropeT[0:64, tsl], tmp[0][0:64], tmp[1][0:64], ALU.add, r=[ttmp[0], ttmp[1]], w=[tkr])
        if MS <= 2:
            return
        self.dma("gpsimd", wout, I["mwout"].rearrange("(h p) c -> p h c", p=128), w=[twout])
        RHT = self.RHT
        qnT = self.carve(RHT, 0, [128, S], BF16)
        qrT = self.carve(RHT, 4096, [128, S], BF16)
        knT = self.carve(RHT, 8192, [128, S], BF16)
        vh = self.carve(RHT, 12288, [128, NTB, 132], BF16)
        PT = self.carve(RHT, 16512, [128, NTB, 256], BF16)
        tqn, tqr, tkn, tvh, tPT = [Tr(n) for n in ("qnT", "qrT", "knT", "vh", "PTm")]
        P.alias([tqn, tqr, tkn, tvh, tPT], tHT)
        self.v("vector", lambda e: e.memset(vh[:, :, 128:129], 1.0), w=[tvh])
        self.v("gpsimd", lambda e: e.memset(qrT[64:128, :], 0.0), w=[tqr])
        for h in range(8):
            self.convert_tables(1, h, 8)
            for tq in range(4):
                tsl = slice(tq * 512, (tq + 1) * 512)
                for kc in range(3):
                    self.mm(PB[0][:], wuq[:, kc, 192 * h:192 * h + 128], cnT[:, kc, tsl], kc == 0, kc == 2, r=[twuq, tcn], w=[tPB[0]])
                self.cp("scalar", qnT[:, tsl], PB[0][:], r=[tPB[0]], w=[tqn])
                for kc in range(2):
                    self.mm(PB[1][:], wukv[:, kc, 256 * h:256 * h + 128], cnT[:, 3 + kc, tsl], kc == 0, kc == 1, r=[twukv, tcn], w=[tPB[1]])
                self.cp("scalar", knT[:, tsl], PB[1][:], r=[tPB[1]], w=[tkn])
                for kc in range(3):
                    self.mm(PB[6][0:64, :], wuq[:, kc, 192 * h + 128:192 * h + 192], cnT[:, kc, tsl], kc == 0, kc == 2, r=[twuq, tcn], w=[tPB[6]])
                for kc in range(3):
                    self.mm(PB[7][0:64, :], wuqsw[:, kc, h, :], cnT[:, kc, tsl], kc == 0, kc == 2, r=[tsw, tcn], w=[tPB[7]])
                self.tt("vector", tmp[0][0:64], PB[6][0:64, :], cos64[0:64, tsl], ALU.mult, r=[tPB[6], tcs], w=[ttmp[0]])
                self.tt("vector", tmp[1][0:64], PB[7][0:64, :], sin64[0:64, tsl], ALU.mult, r=[tPB[7], tcs], w=[ttmp[1]])
                self.tt("gpsimd", qrT[0:64, tsl], tmp[0][0:64], tmp[1][0:64], ALU.add, r=[ttmp[0], ttmp[1]], w=[tqr])
            if MS <= 3:
                return
            for tb in range(NTB):
                pb = tb % 2
                for kc in range(2):
                    self.mm(PB[pb][:, 0:128], cnT[:, 3 + kc, tb * 128:(tb + 1) * 128], wukv[:, kc, 256 * h + 128:256 * h + 256], kc == 0, kc == 1,
                            r=[twukv, tcn], w=[tPB[pb]])
                self.cp("scalar", vh[:, tb, 0:128], PB[pb][:, 0:128], r=[tPB[pb]], w=[tvh])
            if MS <= 4:
                return
            pending = None
            for nb2 in range(8):
                nsl = slice(nb2 * 256, (nb2 + 1) * 256)
                for mb in range(NTB):
                    sb_ = (2, 3, 0)[mb % 3]
                    msl = slice(mb * 128, (mb + 1) * 128)
                    self.mm(PB[sb_][:, 0:256], knT[:, msl], qnT[:, nsl], True, False, r=[tkn, tqn], w=[tPB[sb_]])
                    self.mm(PB[sb_][:, 0:256], kropeT[:, msl], qrT[:, nsl], False, True, r=[tkr, tqr], w=[tPB[sb_]])
                    self.act(PT[:, mb, :], PB[sb_][:, 0:256], AF.Exp, r=[tPB[sb_]], w=[tPT], scale=SCALE)
                for sub in range(2):
                    nb = nb2 * 2 + sub
                    ob = 4 + (nb % 2)
                    par = nb % 2
                    for mb in range(NTB):
                        self.mm(PB[ob][:, 0:129], PT[:, mb, sub * 128:(sub + 1) * 128], vh[:, mb, 0:129], mb == 0, mb == NTB - 1,
                                r=[tPT, tvh], w=[tPB[ob]])
                    if pending is not None:
                        pending()

                    def tail(nb=nb, ob=ob, par=par, h=h):
                        rz_ = rz2[par]
                        self.v("vector", lambda e: e.reciprocal(out=rz_, in_=PB[ob][:, 128:129]), r=[tPB[ob]], w=[trz2[par]])
                        self.ts("vector", Ob2[par], PB[ob][:, 0:128], rz_, None, ALU.mult, r=[tPB[ob], trz2[par]], w=[tOb2[par]])
                        pbT = PB[6][:].bitcast(BF16)
                        self.tp(pbT[:, 0:128], Ob2[par], self.identb, r=[tOb2[par], self.tC], w=[tPB[6]])
                        self.cp("scalar", OT, pbT[:, 0:128], r=[tPB[6]], w=[tOT])
                        for half in range(2):
                            wb = 7 if half == 0 else 1
                            hs = slice(half * 512, (half + 1) * 512)
                            self.mm(PB[wb][:], OT, wout[:, h, hs], True, True, r=[tOT, twout], w=[tPB[wb]])
                            if h == 0:
                                self.stt("vector", self.RH[:, nb, hs], self.RH[:, nb, hs], ALPHA, PB[wb][:], ALU.mult, ALU.add,
                                         r=[self.tH[nb], tPB[wb]], w=[self.tH[nb]])
                            else:
                                self.tt("vector", self.RH[:, nb, hs], self.RH[:, nb, hs], PB[wb][:], ALU.add,
                                        r=[self.tH[nb], tPB[wb]], w=[self.tH[nb]])
                    pending = tail
            pending()
            pending = None
        for tb in range(NTB):
            self.layer_norm(tb, lng, lnb, tln, scr)
        P.alias(tHT, [tqn, tqr, tkn, tvh, tPT])
        self.tRG = [tcn, tkr, twuq, twukv, twout, tcraw]
        self.tXtmp = tOb2 + [tOT]
        self.tXall = newX + trz2


def build_nc(stop_after=None):
    nc = bass.Bass("TRN2", target_bir_lowering=False)
    k = K(nc, stop_after=stop_after)
    k.build()
    return nc


_CACHE = {}


def host_consts():
    invf = np.zeros((128, 4), np.float32)
    invf[:, 0] = (10000.0 ** (-np.arange(0, 256, 2, dtype=np.float32) / np.float32(256))).astype(np.float32)
    f32 = (10000.0 ** (-np.arange(0, 64, 2, dtype=np.float32) / np.float32(64))).astype(np.float32)
    invf[0:32, 1] = f32
    invf[32:64, 1] = f32
    invf[0:32, 2] = -1.0
    invf[32:64, 2] = 1.0
    return invf


def make_in_maps(inp):
    c = np.ascontiguousarray
    f = lambda k: np.asarray(inp[k], dtype=np.float32)
    shared = {
        "rwin": c(f("ret_w_in")[0]), "rl1d": c(f("ret_log1m_decay")[0].reshape(1, 8)),
        "rgng": c(f("ret_gn_g")[0].reshape(1, 2048)), "rgnb": c(f("ret_gn_b")[0].reshape(1, 2048)),
        "rwout": c(f("ret_w_out")[0]),
        "mwin": c(f("mla_w_in")[0]), "mqn": c(f("mla_q_norm")[0].reshape(3, 128).T),
        "mkvn": c(f("mla_kv_norm")[0].reshape(2, 128).T), "mwuq": c(f("mla_w_uq")[0]),
        "mwukv": c(f("mla_w_ukv")[0]), "mwout": c(f("mla_w_out")[0]),
        "pwq": c(f("peer_w_q")),
        "pskT": c(f("peer_sub_keys").reshape(2, 16, 128, 128).transpose(0, 3, 1, 2)),
        "puT": c(f("peer_u").reshape(2, 128, 128, 8, 128).transpose(0, 1, 4, 3, 2).reshape(2, 128, 128, 1024)),
        "pv": c(f("peer_v")),
        "lnmg": c(f("ln_mix_g")), "lnmb": c(f("ln_mix_b")), "lnfg": c(f("ln_ffn_g")), "lnfb": c(f("ln_ffn_b")),
        "cinvf": host_consts(),
    }
    x = f("x")
    pos = np.asarray(inp["positions"], dtype=np.int32)
    maps = []
    for b in range(8):
        m = dict(shared)
        m["x"] = c(x[b])
        m["pos"] = c(pos[b].reshape(1, S))
        maps.append(m)
    return maps


def kernel(**inputs):
    if "nc" not in _CACHE:
        _CACHE["nc"] = build_nc()
    nc = _CACHE["nc"]
    maps = make_in_maps(inputs)
    res = run_bass_kernel_spmd(nc, maps, core_ids=list(range(8)))
    out = np.stack([np.asarray(r["out"], dtype=np.float32) for r in res.results], axis=0)
    return out
```

```python
import math, os, contextlib
import numpy as np
import concourse.bass as bass
import concourse.mybir as mybir
from concourse.bass_utils import run_bass_kernel_spmd

F32 = mybir.dt.float32
BF16 = mybir.dt.bfloat16
I32 = mybir.dt.int32
U32 = mybir.dt.uint32
ALU = mybir.AluOpType
AF = mybir.ActivationFunctionType
AX = mybir.AxisListType

ENGS = ["tensor", "vector", "scalar", "gpsimd", "sync"]
MAIN_MULT_ENG = os.environ.get("MAIN_MULT_ENG", "vector")
SELF_WAR = bool(int(os.environ.get("SELF_WAR", "1")))

S = 2048
D = 1024
NTB = 16
ALPHA = (2.0 * 2) ** 0.25
EPS = 1e-5
TWO_PI = 2.0 * math.pi
C1 = float(np.float32(TWO_PI))
C2 = TWO_PI - C1
PI_LO = 3.1415925


class Tr:
    __slots__ = ("name", "w", "r", "sem", "cnt")

    def __init__(self, name=""):
        self.name = name
        self.w = {}
        self.r = {}
        self.sem = None
        self.cnt = 0


class Op:
    __slots__ = ("eng", "fn", "deps", "dma", "sig", "semtr", "val", "raw", "post")


class Prog:
    def __init__(self, nc):
        self.nc = nc
        self.q = {e: [] for e in ENGS}
        self.all = []
        self._ord = {}

    def op(self, eng, fn, reads=(), writes=(), dma=False, post=None):
        o = Op()
        o.eng = eng; o.fn = fn; o.dma = dma; o.sig = False; o.val = None; o.post = post
        o.semtr = writes[0] if dma else None
        key = ("d", id(o.semtr)) if dma else ("e", eng)
        deps = []
        raw = set()
        for t in reads:
            for d in t.w.values():
                deps.append(d); raw.add(id(d))
        for t in writes:
            deps.extend(t.r.values())
            deps.extend(t.w.values())
        seen = set(); d2 = []
        for d in deps:
            if id(d) not in seen and d is not o:
                seen.add(id(d)); d2.append(d)
        o.deps = d2
        o.raw = raw
        rset = set(id(t) for t in reads)
        for t in writes:
            if t.r or (id(t) in rset):
                t.w = {key: o}; t.r = {}
            else:
                t.w[key] = o
        wset = set(id(t) for t in writes)
        for t in reads:
            if id(t) not in wset:
                t.r[key] = o
        self.q[eng].append(o)
        self._ord[id(o)] = len(self.all)
        self.all.append(o)
        return o

    def alias(self, new_trs, old_trs):
        merged = {}
        for t in old_trs:
            for dct in (t.w, t.r):
                for k, o in dct.items():
                    b = merged.get(k)
                    if b is None or self._ord[id(o)] > self._ord[id(b)]:
                        merged[k] = o
        for t in new_trs:
            t.w = {}
            t.r = dict(merged)

    def finalize_and_emit(self, final_waits=()):
        nc = self.nc
        for o in self.all:
            best = {}
            for d in o.deps:
                if d.dma:
                    k = ("d", id(d.semtr))
                else:
                    if d.eng == o.eng and (o.eng == "tensor" or (id(d) not in o.raw and not SELF_WAR)):
                        continue
                    k = ("e", d.eng)
                b = best.get(k)
                if b is None or self._ord[id(d)] > self._ord[id(b)]:
                    best[k] = d
            o.deps = list(best.values())
            for d in o.deps:
                d.sig = True
        for o in final_waits:
            o.sig = True
        for o in self.all:
            if o.dma:
                o.sig = o.sig or (os.environ.get("DMASIG","1")=="1")
        stack = contextlib.ExitStack()
        engsem = {}
        for e in ENGS:
            engsem[e] = stack.enter_context(nc.semaphore("s_" + e))
        engcnt = {e: 0 for e in ENGS}
        nsem = 0
        for o in self.all:
            if not o.sig:
                continue
            if o.dma:
                t = o.semtr
                if t.sem is None:
                    t.sem = stack.enter_context(nc.semaphore("d%d" % nsem))
                    nsem += 1
                t.cnt += 16
                o.val = (t.sem, t.cnt)
            else:
                engcnt[o.eng] += 1
                o.val = (engsem[o.eng], engcnt[o.eng])
        self.nsem = nsem
        block = stack.enter_context(nc.Block())
        prog = self

        def emit(engname, engobj):
            waited = {}
            for o in prog.q[engname]:
                for d in o.deps:
                    s, v = d.val
                    k = id(s)
                    if waited.get(k, 0) >= v:
                        continue
                    waited[k] = v
                    engobj.wait_ge(s, v)
                ins = o.fn(engobj)
                if o.post is not None and o.sig:
                    ins = o.post(engobj)
                if o.sig:
                    s, v = o.val
                    ins.then_inc(s, 16 if o.dma else 1)
            if engname == "sync":
                for o in final_waits:
                    s, v = o.val
                    engobj.wait_ge(s, v)

        @block.tensor
        def _(e):
            emit("tensor", e)

        @block.vector
        def _(e):
            emit("vector", e)

        @block.scalar
        def _(e):
            emit("scalar", e)

        @block.gpsimd
        def _(e):
            emit("gpsimd", e)

        @block.sync
        def _(e):
            emit("sync", e)

        stack.close()


class K:
    def __init__(self, nc, stop_after=None):
        self.nc = nc
        self.P = Prog(nc)
        self.stop_after = stop_after
        self.st = contextlib.ExitStack()

    def v(self, eng, fn, r=(), w=(), post=None):
        return self.P.op(eng, fn, reads=list(r), writes=list(w), post=post)

    def dma(self, eng, out, in_, r=(), w=(), **kw):
        return self.P.op(eng, lambda e: e.dma_start(out=out, in_=in_, **kw), reads=list(r), writes=list(w), dma=True)

    def mm(self, out, lhsT, rhs, start, stop, r=(), w=()):
        return self.P.op("tensor", lambda e: e.matmul(out, lhsT=lhsT, rhs=rhs, start=start, stop=stop),
                         reads=list(r), writes=list(w))

    def tp(self, out, in_, ident, r=(), w=()):
        return self.P.op("tensor", lambda e: e.transpose(out=out, in_=in_, identity=ident), reads=list(r), writes=list(w))

    def act(self, out, in_, func, r=(), w=(), **kw):
        return self.P.op("scalar", lambda e: e.activation(out=out, in_=in_, func=func, **kw), reads=list(r), writes=list(w))

    def tt(self, eng, out, in0, in1, op, r=(), w=()):
        return self.P.op(eng, lambda e: e.tensor_tensor(out=out, in0=in0, in1=in1, op=op), reads=list(r), writes=list(w))

    def ts(self, eng, out, in0, s1, s2, op0, op1=None, r=(), w=()):
        if op1 is None:
            return self.P.op(eng, lambda e: e.tensor_scalar(out=out, in0=in0, scalar1=s1, scalar2=None, op0=op0),
                             reads=list(r), writes=list(w))
        return self.P.op(eng, lambda e: e.tensor_scalar(out=out, in0=in0, scalar1=s1, scalar2=s2, op0=op0, op1=op1),
                         reads=list(r), writes=list(w))

    def stt(self, eng, out, in0, scalar, in1, op0, op1, r=(), w=()):
        return self.P.op(eng, lambda e: e.scalar_tensor_tensor(out=out, in0=in0, scalar=scalar, in1=in1, op0=op0, op1=op1),
                         reads=list(r), writes=list(w))

    def cp(self, eng, out, in_, r=(), w=()):
        if eng == "scalar":
            return self.P.op(eng, lambda e: e.copy(out=out, in_=in_), reads=list(r), writes=list(w))
        return self.P.op(eng, lambda e: e.tensor_copy(out=out, in_=in_), reads=list(r), writes=list(w))

    def sbuf(self, name, shape, dt):
        return self.st.enter_context(self.nc.sbuf_tensor(name, shape, dt))

    def psum(self, name, shape, dt):
        return self.st.enter_context(self.nc.psum_tensor(name, shape, dt))

    @staticmethod
    def carve(region, off_bytes, shape, dt):
        n = 1
        for s_ in shape[1:]:
            n *= s_
        esz = 4 if dt in (F32, I32, U32) else 2
        nb = n * esz
        assert off_bytes % 4 == 0 and nb % 4 == 0
        ap = region[0:shape[0], off_bytes // 4:(off_bytes + nb) // 4]
        if dt != F32:
            ap = ap.bitcast(dt)
        if len(shape) == 3:
            ap = ap.rearrange("p (a b) -> p a b", a=shape[1])
        elif len(shape) == 4:
            ap = ap.rearrange("p (a b c) -> p a b c", a=shape[1], b=shape[2])
        return ap

    def build(self):
        nc = self.nc
        dt_ = nc.dram_tensor
        I = {}

        def din(name, shape, dt=F32):
            I[name] = dt_(name, shape, dt, kind="ExternalInput").ap()

        din("x", [S, D]); din("pos", [1, S], I32)
        din("rwin", [D, 6144]); din("rl1d", [1, 8]); din("rgng", [1, 2048]); din("rgnb", [1, 2048]); din("rwout", [2048, D])
        din("mwin", [D, 704]); din("mqn", [128, 3]); din("mkvn", [128, 2]); din("mwuq", [384, 1536]); din("mwukv", [256, 2048])
        din("mwout", [D, D])
        din("pwq", [2, D, 2048]); din("pskT", [2, 128, 16, 128]); din("puT", [2, 128, 128, 1024]); din("pv", [2, 16384, D])
        din("lnmg", [2, D]); din("lnmb", [2, D]); din("lnfg", [2, D]); din("lnfb", [2, D])
        din("cinvf", [128, 4])
        self.I = I
        self.out = dt_("out", [S, D], F32, kind="ExternalOutput").ap()
        self.ubf = dt_("ubf", [2, 128 * 128, 1024], BF16, kind="Internal").ap()
        self.vbf = dt_("vbf", [2, 16384, 1024], BF16, kind="Internal").ap()
        self.tUbf = [Tr("ubf0"), Tr("ubf1")]
        self.tVbf = [Tr("vbf0"), Tr("vbf1")]

        self.RH = self.sbuf("RH", [128, NTB, D], F32)
        self.RHT = self.sbuf("RHT", [128, 8192], F32)
        self.RG = self.sbuf("RG", [128, 16384], F32)
        self.RX = self.sbuf("RX", [128, 10752], F32)
        self.RC = self.sbuf("RC", [128, 1408], F32)
        self.tH = [Tr("H%d" % i) for i in range(NTB)]
        self.tHT = [Tr("HT%d" % i) for i in range(NTB)]
        self.HT = self.carve(self.RHT, 0, [128, 8, S], BF16)
        self.tRG = [Tr("RG")]
        self.tRHT_alias = []
        self.PB = [self.psum("pb%d" % i, [128, 512], F32) for i in range(8)]
        self.tPB = [Tr("pb%d" % i) for i in range(8)]

        RC = self.RC
        self.identf = RC[:, 0:128]
        self.iotaf = RC[:, 128:256]
        self.identb = RC[:, 256:320].bitcast(BF16)
        self.invf = RC[:, 320:324]
        self.lg = RC[:, 324:332]
        self.iota16 = RC[:, 332:348]
        self.onesf = RC[:, 384:512]
        self.dmy = RC[:, 584:640]
        self._dmy_i = 0
        self.iotab = RC[:, 520:584].bitcast(BF16)
        self.tC = Tr("consts")
        tC = self.tC
        tmpi = self.carve(self.RX, 0, [128, 128], I32)
        ttmp = Tr("tmpi")
        self.v("gpsimd", lambda e: e.iota(tmpi, pattern=[[1, 128]], base=0, channel_multiplier=0), w=[ttmp])
        self.cp("vector", self.iotaf, tmpi, r=[ttmp], w=[tC])
        self.cp("vector", self.iota16, tmpi[:, 0:16], r=[ttmp], w=[tC])
        self.cp("vector", self.iotab, tmpi, r=[ttmp], w=[tC])
        self.v("gpsimd", lambda e: e.iota(tmpi, pattern=[[1, 128]], base=0, channel_multiplier=-1), w=[ttmp])
        tmpf = self.carve(self.RX, 512, [128, 128], F32)
        ttmpf = Tr("tmpf")
        self.cp("vector", tmpf, tmpi, r=[ttmp], w=[ttmpf])
        self.v("vector", lambda e: e.tensor_single_scalar(out=self.identf, in_=tmpf, scalar=0.0, op=ALU.is_equal), r=[ttmpf], w=[tC])
        self.v("vector", lambda e: e.tensor_single_scalar(out=self.identb, in_=tmpf, scalar=0.0, op=ALU.is_equal), r=[ttmpf], w=[tC])
        self.v("vector", lambda e: e.memset(self.onesf, 1.0), w=[tC])
        self.dma("sync", self.invf, I["cinvf"], w=[tC])
        self.tXtmp = [ttmp, ttmpf]

        for tb in range(NTB):
            self.dma("sync", self.RH[:, tb, :], I["x"][tb * 128:(tb + 1) * 128, :], w=[self.tH[tb]])

        self.fin = []
        self.puT_flat = I["puT"].rearrange("l c p e -> l (c p) e")
        self.retention()
        if self.stop_after == "ret":
            return self.finish()
        self.peer(0)
        if self.stop_after == "peer0":
            return self.finish()
        self.mla()
        if self.stop_after == "mla":
            return self.finish()
        self.peer(1)
        return self.finish()

    def convert_tables(self, l, part, nparts):
        n = 32 // nparts
        for i in range(part * n, (part + 1) * n):
            rs = slice(i * 512, (i + 1) * 512)
            self.dma("gpsimd", self.ubf[l, rs, :], self.puT_flat[l, rs, :], w=[self.tUbf[l]])
            self.dma("gpsimd", self.vbf[l, rs, :], self.I["pv"][l, rs, :], w=[self.tVbf[l]])

    def finish(self):
        for tb in range(NTB):
            o = self.dma("sync", self.out[tb * 128:(tb + 1) * 128, :], self.RH[:, tb, :], r=[self.tH[tb]], w=[Tr("out%d" % tb)])
            self.fin.append(o)
        self.P.finalize_and_emit(final_waits=self.fin)
        self.st.close()

    def build_HT(self):
        hb = self.carve(self.RX, 0, [128, D], BF16)
        thb = Tr("hb")
        self.P.alias([thb], self.tXtmp)
        pbT = self.PB[6][:].bitcast(BF16).rearrange("p (a b) -> p a b", a=8)
        for tb in range(NTB):
            self.cp("scalar", hb, self.RH[:, tb, :], r=[self.tH[tb]], w=[thb])
            for kc in range(8):
                self.tp(pbT[:, kc, :], hb[:, kc * 128:(kc + 1) * 128], self.identb, r=[thb, self.tC], w=[self.tPB[6]])
            self.cp("vector", self.HT[:, :, tb * 128:(tb + 1) * 128], pbT, r=[self.tPB[6]], w=[self.tHT[tb]])
        self.tXtmp = [thb]

    def trig_tables(self, col, nparts, cosT, sinT, tcs, sgn_col=None):
        I = self.I
        RX = self.RX
        posi = self.carve(self.RG, 0, [128, S], I32)
        ang = self.carve(self.RG, 8192, [128, S], F32)
        ki = self.carve(self.RG, 16384, [128, S], I32)
        kf = self.carve(self.RG, 24576, [128, S], F32)
        r = self.carve(self.RG, 32768, [128, S], F32)
        rc = self.carve(self.RG, 40960, [128, S], F32)
        tt_ = [Tr("trig%d" % i) for i in range(6)]
        self.P.alias(tt_, self.tRG)
        tpos, tang, tki, tkf, tr_, trc = tt_
        n = nparts
        self.dma("sync", posi, I["pos"].partition_broadcast(128), w=[tpos])
        self.cp("vector", ang, posi, r=[tpos], w=[tang])
        self.ts("vector", ang[0:n], ang[0:n], self.invf[0:n, col:col + 1], None, ALU.mult, r=[tang, self.tC], w=[tang])
        self.ts("vector", ki[0:n], ang[0:n], 1.0 / TWO_PI, None, ALU.mult, r=[tang], w=[tki])
        self.cp("vector", kf[0:n], ki[0:n], r=[tki], w=[tkf])
        self.stt("vector", r[0:n], kf[0:n], -C1, ang[0:n], ALU.mult, ALU.add, r=[tkf, tang], w=[tr_])
        self.stt("vector", r[0:n], kf[0:n], -C2, r[0:n], ALU.mult, ALU.add, r=[tkf, tr_], w=[tr_])
        self.ts("vector", rc[0:n], r[0:n], math.pi / 2, None, ALU.add, r=[tr_], w=[trc])
        self.ts("vector", kf[0:n], rc[0:n], math.pi, -TWO_PI, ALU.is_gt, ALU.mult, r=[trc], w=[tkf])
        self.tt("vector", rc[0:n], rc[0:n], kf[0:n], ALU.add, r=[trc, tkf], w=[trc])
        self.ts("vector", r[0:n], r[0:n], PI_LO, -PI_LO, ALU.min, ALU.max, r=[tr_], w=[tr_])
        self.ts("vector", rc[0:n], rc[0:n], PI_LO, -PI_LO, ALU.min, ALU.max, r=[trc], w=[trc])
        self.act(sinT[0:n], r[0:n], AF.Sin, r=[tr_], w=[tcs])
        self.act(cosT[0:n], rc[0:n], AF.Sin, r=[trc], w=[tcs])
        if sgn_col is not None:
            self.ts("vector", sinT[0:n], sinT[0:n], self.invf[0:n, sgn_col:sgn_col + 1], None, ALU.mult, r=[tcs, self.tC], w=[tcs])
        self.tRG = tt_

    def layer_norm(self, tb, g_ap, b_ap, tgb, scr):
        Hb = self.RH[:, tb, :]
        tH = self.tH[tb]
        st6, mv, rstd = scr["st6"], scr["mv"], scr["rstd"]
        tS = scr["t"]
        self.v("vector", lambda e: e.bn_stats(out=st6[:, 0:6], in_=Hb[:, 0:512]), r=[tH], w=[tS])
        self.v("vector", lambda e: e.bn_stats(out=st6[:, 6:12], in_=Hb[:, 512:1024]), r=[tH], w=[tS])
        self.v("vector", lambda e: e.bn_aggr(out=mv, in_=st6), r=[tS], w=[tS])
        self.act(rstd, mv[:, 1:2], AF.Sqrt, r=[tS], w=[tS], bias=EPS, scale=1.0)
        self.v("vector", lambda e: e.reciprocal(out=rstd, in_=rstd), r=[tS], w=[tS])
        self.ts("vector", Hb, Hb, mv[:, 0:1], rstd, ALU.subtract, ALU.mult, r=[tH, tS], w=[tH])
        self.tt("vector", Hb, Hb, g_ap, ALU.mult, r=[tH, tgb], w=[tH])
        self.tt("vector", Hb, Hb, b_ap, ALU.add, r=[tH, tgb], w=[tH])

    def load_ln(self, gsrc, bsrc, off):
        g = self.carve(self.RX, off, [128, D], F32)
        b = self.carve(self.RX, off + 4096, [128, D], F32)
        t = Tr("lngb")
        self.P.alias([t], self.tLN if hasattr(self, "tLN") else [])
        self.dma("sync", g, gsrc.partition_broadcast(128), w=[t])
        self.dma("sync", b, bsrc.partition_broadcast(128), w=[t])
        self.tLN = [t]
        return g, b, t

    def retention(self):
        I = self.I
        RX, RG = self.RX, self.RG
        cosT = self.carve(RX, 2048, [128, S], F32)
        sinT = self.carve(RX, 2048 + 8192, [128, S], F32)
        tcs = Tr("cossin")
        self.trig_tables(0, 128, cosT, sinT, tcs)
        tlg = self.tC
        l1 = self.carve(RX, 40000, [128, 8], F32)
        tl1 = Tr("l1")
        self.dma("sync", l1, I["rl1d"].partition_broadcast(128), w=[tl1])
        self.act(l1, l1, AF.Exp, r=[tl1], w=[tl1])
        self.act(self.lg, l1, AF.Ln, r=[tl1], w=[tlg], scale=-1.0, bias=1.0)
        self.build_HT()

        tmp = [self.carve(RX, 18432 + i * 2048, [128, 512], F32) for i in range(4)]
        ttmp = [Tr("rtmp%d" % i) for i in range(4)]
        gng = self.carve(RX, 26624, [128, 512], F32)
        gnb = self.carve(RX, 26624 + 2048, [128, 512], F32)
        tgn = Tr("gn")
        ZT = self.carve(RX, 30720, [128, 4, 128], BF16)
        tZT = Tr("ZT")
        small = self.carve(RX, 40064, [128, 32], F32)
        tsm = Tr("small")
        scr = {"st6": small[:, 0:12], "mv": small[:, 12:14], "rstd": small[:, 14:15], "t": tsm}
        lng, lnb, tln = self.load_ln(I["lnmg"][0:1, :], I["lnmb"][0:1, :], 31744)

        QT = self.carve(RG, 0, [128, 2, S], BF16)
        KT = self.carve(RG, 8192, [128, 2, S], BF16)
        VH = self.carve(RG, 16384, [128, NTB, 512], BF16)
        PT = self.carve(RG, 32768, [128, NTB, 256], BF16)
        Dtab = self.carve(RG, 40960, [128, 31, 128], BF16)
        W1 = self.carve(RG, 49152, [128, 8, 512], BF16)
        W2 = self.carve(RG, 57344, [128, 8, 512], BF16)
        W2o = self.carve(RG, 57344, [128, 4, 1024], BF16)
        tQT, tKT, tVH, tPT, tD, tW1, tW2 = [Tr(n) for n in ("QT", "KT", "VH", "PT", "Dtab", "W1", "W2")]
        self.P.alias([tQT, tKT, tVH, tPT, tD, tW1, tW2], self.tRG)
        Zb2 = [self.carve(RX, i * 1024, [128, 512], BF16) for i in range(2)]
        tZb2 = [Tr("Zb0"), Tr("Zb1")]
        self.P.alias(tZb2, self.tXtmp)
        tsm2 = [Tr("small0"), Tr("small1")]
        scr2 = [{"st6": small[:, 16 * i:16 * i + 12], "mv": small[:, 16 * i + 12:16 * i + 14], "rstd": small[:, 16 * i + 14:16 * i + 15], "t": tsm2[i]}
                for i in range(2)]
        iot = tmp[3].bitcast(I32)

        PB, tPB = self.PB, self.tPB
        HT, tHT = self.HT, self.tHT
        rwin = I["rwin"]
        for h in range(4):
            self.dma("gpsimd", W1[:, :, 0:256], rwin[:, h * 256:(h + 1) * 256].rearrange("(k p) c -> p k c", p=128), w=[tW1])
            self.dma("gpsimd", W1[:, :, 256:512], rwin[:, 1024 + h * 256:1024 + (h + 1) * 256].rearrange("(k p) c -> p k c", p=128), w=[tW1])
            self.dma("gpsimd", W2, rwin[:, 2048 + h * 512:2048 + (h + 1) * 512].rearrange("(k p) c -> p k c", p=128), w=[tW2])
            self.dma("sync", gng, I["rgng"][0:1, h * 512:(h + 1) * 512].partition_broadcast(128), w=[tgn])
            self.dma("sync", gnb, I["rgnb"][0:1, h * 512:(h + 1) * 512].partition_broadcast(128), w=[tgn])
            dflat = Dtab.rearrange("p a b -> p (a b)")
            for pc in range(8):
                r0 = pc * 4
                nr = min(4, 31 - r0)
                w_ = nr * 128
                io_v = iot[:, 0:w_]
                self.v("gpsimd", lambda e, io_v=io_v, nr=nr, r0=r0: e.iota(io_v, pattern=[[128, nr], [1, 128]], base=128 * (r0 - 15), channel_multiplier=-1),
                       w=[ttmp[3]])
                self.ts("vector", tmp[0][:, 0:w_], iot[:, 0:w_], 0.0, self.lg[:, h:h + 1], ALU.max, ALU.mult, r=[ttmp[3], self.tC], w=[ttmp[0]])
                self.ts("vector", tmp[1][:, 0:w_], iot[:, 0:w_], 0.0, self.lg[:, 4 + h:5 + h], ALU.min, ALU.mult, r=[ttmp[3], self.tC], w=[ttmp[1]])
                self.tt("vector", tmp[0][:, 0:w_], tmp[0][:, 0:w_], tmp[1][:, 0:w_], ALU.subtract, r=[ttmp[0], ttmp[1]], w=[ttmp[0]])
                self.act(dflat[:, r0 * 128:r0 * 128 + w_], tmp[0][:, 0:w_], AF.Exp, r=[ttmp[0]], w=[tD], bias=-math.log(16.0), scale=1.0)
            for which, dst, tdst in ((0, QT, tQT), (1, KT, tKT)):
                for tq in range(4):
                    tsl = slice(tq * 512, (tq + 1) * 512)
                    hts = [tHT[tq * 4 + i] for i in range(4)]
                    for c in range(2):
                        for kc in range(8):
                            self.mm(PB[c][:], W1[:, kc, which * 256 + c * 128: which * 256 + (c + 1) * 128], HT[:, kc, tsl],
                                    kc == 0, kc == 7, r=[tW1] + hts, w=[tPB[c]])
                    self.tt("vector", tmp[0], PB[0][:], cosT[:, tsl], ALU.mult, r=[tPB[0], tcs], w=[ttmp[0]])
                    self.tt("vector", tmp[1], PB[1][:], sinT[:, tsl], ALU.mult, r=[tPB[1], tcs], w=[ttmp[1]])
                    self.tt("vector", tmp[2], PB[0][:], sinT[:, tsl], ALU.mult, r=[tPB[0], tcs], w=[ttmp[2]])
                    self.tt("vector", tmp[3], PB[1][:], cosT[:, tsl], ALU.mult, r=[tPB[1], tcs], w=[ttmp[3]])
                    self.tt("gpsimd", dst[:, 0, tsl], tmp[0], tmp[1], ALU.subtract, r=[ttmp[0], ttmp[1]], w=[tdst])
                    self.tt("gpsimd", dst[:, 1, tsl], tmp[2], tmp[3], ALU.add, r=[ttmp[2], ttmp[3]], w=[tdst])
            for tb in range(NTB):
                pb = tb % 2
                for kc in range(8):
                    self.mm(PB[pb][:], HT[:, kc, tb * 128:(tb + 1) * 128], W2[:, kc, :], kc == 0, kc == 7, r=[tW2, tHT[tb]], w=[tPB[pb]])
                self.cp("scalar", VH[:, tb, :], PB[pb][:], r=[tPB[pb]], w=[tVH])
            self.dma("gpsimd", W1, rwin[:, 4096 + h * 512:4096 + (h + 1) * 512].rearrange("(k p) c -> p k c", p=128), w=[tW1])
            self.dma("gpsimd", W2o, I["rwout"][h * 512:(h + 1) * 512, :].rearrange("(k p) c -> p k c", p=128), w=[tW2])
            self.convert_tables(0, h, 4)
            pending = None
            for nb2 in range(8):
                nsl = slice(nb2 * 256, (nb2 + 1) * 256)
                for mb in range(NTB):
                    sb_ = (2, 3, 1)[mb % 3]
                    for kc in range(2):
                        self.mm(PB[sb_][:, 0:256], KT[:, kc, mb * 128:(mb + 1) * 128], QT[:, kc, nsl], kc == 0, kc == 1,
                                r=[tKT, tQT], w=[tPB[sb_]])
                    rel0 = 2 * nb2 - mb + 15
                    self.tt("vector", PT[:, mb, :], PB[sb_][:, 0:256], Dtab[:, rel0:rel0 + 2, :].rearrange("p a b -> p (a b)"), ALU.mult,
                            r=[tPB[sb_], tD], w=[tPT])
                for sub in range(2):
                    nb = nb2 * 2 + sub
                    ob = 4 + (nb % 2)
                    par = nb % 2
                    t_y, t_g = tmp[2 * par], tmp[2 * par + 1]
                    tt_y, tt_g = ttmp[2 * par], ttmp[2 * par + 1]
                    for mb in range(NTB):
                        self.mm(PB[ob][:], PT[:, mb, sub * 128:(sub + 1) * 128], VH[:, mb, :], mb == 0, mb == NTB - 1,
                                r=[tPT, tVH], w=[tPB[ob]])
                    for kc in range(8):
                        self.mm(PB[0][:], HT[:, kc, nb * 128:(nb + 1) * 128], W1[:, kc, :], kc == 0, kc == 7, r=[tW1, tHT[nb]], w=[tPB[0]])
                    if pending is not None:
                        pending()
                    self.act(t_g, PB[0][:], AF.Silu, r=[tPB[0]], w=[tt_g])
                    sc = scr2[par]
                    st6, mv, rstd, tsm_ = sc["st6"], sc["mv"], sc["rstd"], sc["t"]
                    self.v("vector", lambda e, ob=ob, st6=st6: e.bn_stats(out=st6[:, 0:6], in_=PB[ob][:]), r=[tPB[ob]], w=[tsm_])
                    self.v("vector", lambda e, st6=st6, mv=mv: e.bn_aggr(out=mv, in_=st6[:, 0:6]), r=[tsm_], w=[tsm_])
                    self.act(rstd, mv[:, 1:2], AF.Sqrt, r=[tsm_], w=[tsm_], bias=EPS, scale=1.0)
                    self.v("vector", lambda e, rstd=rstd: e.reciprocal(out=rstd, in_=rstd), r=[tsm_], w=[tsm_])
                    self.ts("vector", t_y, PB[ob][:], mv[:, 0:1], rstd, ALU.subtract, ALU.mult, r=[tPB[ob], tsm_], w=[tt_y])
                    self.tt("vector", t_y, t_y, gng, ALU.mult, r=[tt_y, tgn], w=[tt_y])
                    self.tt("vector", t_y, t_y, gnb, ALU.add, r=[tt_y, tgn], w=[tt_y])
                    self.tt("vector", Zb2[par], t_y, t_g, ALU.mult, r=[tt_y, tt_g], w=[tZb2[par]])

                    def tail(nb=nb, par=par):
                        pbT = PB[6][:].bitcast(BF16).rearrange("p (a b) -> p a b", a=8)
                        for fc in range(4):
                            self.tp(pbT[:, fc, :], Zb2[par][:, fc * 128:(fc + 1) * 128], self.identb, r=[tZb2[par], self.tC], w=[tPB[6]])
                        self.cp("scalar", ZT, pbT[:, 0:4, :], r=[tPB[6]], w=[tZT])
                        for half in range(2):
                            wb = 7
                            hs = slice(half * 512, (half + 1) * 512)
                            for fc in range(4):
                                self.mm(PB[wb][:], ZT[:, fc, :], W2o[:, fc, hs], fc == 0, fc == 3, r=[tZT, tW2], w=[tPB[wb]])
                            if h == 0:
                                self.stt("vector", self.RH[:, nb, hs], self.RH[:, nb, hs], ALPHA, PB[wb][:], ALU.mult, ALU.add,
                                         r=[self.tH[nb], tPB[wb]], w=[self.tH[nb]])
                            else:
                                self.tt("vector", self.RH[:, nb, hs], self.RH[:, nb, hs], PB[wb][:], ALU.add,
                                        r=[self.tH[nb], tPB[wb]], w=[self.tH[nb]])
                    pending = tail
            pending()
        for tb in range(NTB):
            self.layer_norm(tb, lng, lnb, tln, scr2[0])
        self.tRG = [tQT, tKT, tVH, tPT, tD, tW1, tW2]
        self.tXtmp = tZb2
        self.tXall = ttmp + [tgn, tZT, tsm, tcs, tl1] + tsm2

    def peer(self, l):
        I = self.I
        RX, RG, RHT = self.RX, self.RG, self.RHT
        PB, tPB = self.PB, self.tPB
        P = self.P
        lng, lnb, tln = self.load_ln(I["lnfg"][l:l + 1, :], I["lnfb"][l:l + 1, :], 31744)
        Hs = [self.carve(RX, i * 512, [128, 256], BF16) for i in range(2)]
        Wt = [self.carve(RX, 1024 + i * 512, [128, 256], BF16) for i in range(2)]
        tHs = [Tr("Hs%d" % i) for i in range(2)]
        tWt = [Tr("Wt%d" % i) for i in range(2)]
        P.alias(tHs + tWt, self.tXtmp)
        NS = 3
        UB = [self.carve(RX, 2048 + i * 4096, [128, 8, 128], BF16) for i in range(NS)]
        VB = [self.carve(RX, 2048 + i * 4096 + 2048, [128, D], BF16) for i in range(NS)]
        tU = [Tr("U%d" % i) for i in range(NS)]
        tV = [Tr("V%d" % i) for i in range(NS)]
        xT = [self.carve(RX, 14336 + i * 4096, [128, 8, 256], BF16) for i in range(2)]
        txT = [Tr("xT0"), Tr("xT1")]
        Ab = [self.carve(RX, 22528 + i * 2048, [128, 8, 128], BF16) for i in range(2)]
        Bb = [self.carve(RX, 26624 + i * 2048, [128, 8, 128], BF16) for i in range(2)]
        tA = [Tr("A%d" % i) for i in range(2)]
        tB = [Tr("B%d" % i) for i in range(2)]
        o0 = 22528
        tv = self.carve(RX, o0, [128, 16, 16], F32)
        ti = self.carve(RX, o0 + 1024, [128, 16, 16], U32)
        tif = self.carve(RX, o0 + 1024, [128, 16, 16], F32)
        cv = self.carve(RX, o0 + 2048, [128, 8, 16], F32)
        ci = self.carve(RX, o0 + 2560, [128, 8, 16], U32)
        ee = self.carve(RX, o0 + 3072, [128, 8, 16], F32)
        r1u = self.carve(RX, o0 + 3584, [128, 8, 16], U32)
        r1f = self.carve(RX, o0 + 3584, [128, 8, 16], F32)
        r2u = self.carve(RX, o0 + 4096, [128, 8, 16], U32)
        r2f = self.carve(RX, o0 + 4096, [128, 8, 16], F32)
        abw = [[self.carve(RX, o0 + 4608 + (tb2 * 3 + k_) * 512, [128, 8, 16], F32) for k_ in range(3)] for tb2 in range(2)]
        tsmall = [Tr("sm%d" % i) for i in range(13)]
        ttv0, tti0, tcv0, tci0, tee, tr1, tr2, tabw, ttv1, tti1, tcv1, tci1, ts2b = tsmall
        ttv, tti, tcv, tci = [ttv0, ttv1], [tti0, tti1], [tcv0, tcv1], [tci0, tci1]
        aT = self.RC[:, 640:768].bitcast(BF16)
        bT = self.RC[:, 768:896].bitcast(BF16)
        wT = self.RC[:, 896:1152]
        taT = Tr("abwT")
        small = self.carve(RX, 40064, [128, 64], F32)
        tsm = Tr("small")
        scr = {"st6": small[:, 0:12], "mv": small[:, 12:14], "rstd": small[:, 14:15], "t": tsm}
        Zs = small[:, 16:24]
        rz = small[:, 24:32]
        tZs = Tr("Zs")
        newX = tU + tV + txT + tA + tB + [taT, tsm, tZs]
        P.alias(newX, self.tXall)
        Gs = self.carve(RG, 0, [128, 256, 128], BF16)
        tGs = Tr("Gs")
        P.alias([tGs], self.tRG)
        qT = self.carve(RHT, 0, [128, 16, 256], BF16)
        skT = self.carve(RHT, 8192, [128, 16, 128], BF16)
        wqc = [self.carve(RHT, 12288 + i * 2048, [128, 8, 128], BF16) for i in range(2)]
        s_ = self.carve(RHT, 16384, [128, 16, 128], F32)
        s2 = self.carve(RHT, 24576, [128, 16, 128], F32)
        cand = self.carve(RHT, 16384, [128, 8, 256], F32)
        cand2 = self.carve(RHT, 24576, [128, 8, 256], F32)
        eq = self.carve(RHT, 16384, [128, 8, 16, 16], F32)
        tq_, tsk, ts_, ts2 = [Tr(n) for n in ("qT", "skT", "s", "s2")]
        twq = [Tr("wqc0"), Tr("wqc1")]
        scratch_trs = [tq_, tsk, ts_, ts2] + twq
        P.alias(scratch_trs + [ts2b], self.tHT)
        def post(e):
            i_ = self._dmy_i
            self._dmy_i = (i_ + 1) % 28
            return e.memset(self.dmy[:, 2 * i_:2 * i_ + 2], 0.0)
        tvv = tv.rearrange("p (h c) r -> p h c r", c=2)
        tifv = tif.rearrange("p (h c) r -> p h c r", c=2)
        cand4 = cand.rearrange("p h (a b) -> p h a b", a=16)
        iota16 = self.iota16
        B4 = [128, 8, 16, 16]
        B3 = [128, 8, 16]
        B3b = [128, 8, 128]
        pwq = I["pwq"]

        def top16(src, src2, tsrc, tsrc2, vals, idxs, tvals, tidxs, n):
            hn = n // 2
            for g0 in range(hn):
                pair = ((g0, 0), (g0 + hn, 1))
                for g_, ln in pair:
                    self.v("vector", lambda e, g_=g_: e.max(out=vals[:, g_, 0:8], in_=src[:, g_, :]), r=[tsrc], w=[tvals[ln]])
                for g_, ln in pair:
                    self.v("vector", lambda e, g_=g_: e.max_index(out=idxs[:, g_, 0:8], in_max=vals[:, g_, 0:8], in_values=src[:, g_, :]),
                           r=[tsrc, tvals[ln]], w=[tidxs[ln]], post=post)
                for g_, ln in pair:
                    self.v("vector", lambda e, g_=g_: e.match_replace(out=src2[:, g_, :], in_to_replace=vals[:, g_, 0:8], in_values=src[:, g_, :], imm_value=-1e30),
                           r=[tsrc, tvals[ln]], w=[tsrc2[ln]], post=post)
                yield
                for g_, ln in pair:
                    self.v("vector", lambda e, g_=g_: e.max(out=vals[:, g_, 8:16], in_=src2[:, g_, :]), r=[tsrc2[ln]], w=[tvals[ln]])
                for g_, ln in pair:
                    self.v("vector", lambda e, g_=g_: e.max_index(out=idxs[:, g_, 8:16], in_max=vals[:, g_, 8:16], in_values=src2[:, g_, :]),
                           r=[tsrc2[ln], tvals[ln]], w=[tidxs[ln]], post=post)
                yield

        def phase1a(g, buf):
            for tb2 in range(2):
                tb = 2 * g + tb2
                for kq in range(2):
                    pb = 6 + kq
                    for j in range(4):
                        kc = kq * 4 + j
                        self.tp(PB[pb][:, j * 128:(j + 1) * 128], self.RH[:, tb, kc * 128:(kc + 1) * 128], self.identf,
                                r=[self.tH[tb], self.tC], w=[tPB[pb]])
                    self.cp("scalar", xT[buf][:, kq * 4:(kq + 1) * 4, tb2 * 128:(tb2 + 1) * 128],
                            PB[pb][:].rearrange("p (a b) -> p a b", a=4), r=[tPB[pb]], w=[txT[buf]])
            yield
            for i in range(2):
                self.dma("gpsimd", skT[:, i * 8:(i + 1) * 8, :], I["pskT"][l, :, i * 8:(i + 1) * 8, :], w=[tsk])
            for gq in range(16):
                wb = gq % 2
                self.dma("gpsimd", wqc[wb], pwq[l, :, gq * 128:(gq + 1) * 128].rearrange("(k p) c -> p k c", p=128), w=[twq[wb]])
                pb = 6 + gq % 2
                for kc in range(8):
                    self.mm(PB[pb][:, 0:256], wqc[wb][:, kc, :], xT[buf][:, kc, :], kc == 0, kc == 7, r=[twq[wb], txT[buf]], w=[tPB[pb]])
                self.cp("scalar", qT[:, gq, :], PB[pb][:, 0:256], r=[tPB[pb]], w=[tq_])
                yield
            for tb2 in range(2):
                a_, b_, w_ = abw[tb2]
                for q4 in range(4):
                    pb = 6 + q4 % 2
                    for j in range(4):
                        gq = q4 * 4 + j
                        self.mm(PB[pb][:, j * 128:(j + 1) * 128], qT[:, gq, tb2 * 128:(tb2 + 1) * 128], skT[:, gq, :], True, True,
                                r=[tq_, tsk], w=[tPB[pb]])
                    self.cp("scalar", s_[:, q4 * 4:(q4 + 1) * 4, :], PB[pb][:].rearrange("p (a b) -> p a b", a=4), r=[tPB[pb]], w=[ts_])
                yield
                if tb2 == 0:
                    P.alias(tsmall, tA + tB)
                yield from top16(s_, s2, ts_, [ts2, ts2b], tv, ti, ttv, tti, 16)
                self.cp("vector", tif, ti, r=tti, w=tti)
                self.tt("vector", cand4, tvv[:, :, 0, :].unsqueeze(3).broadcast_to(B4), tvv[:, :, 1, :].unsqueeze(2).broadcast_to(B4), ALU.add,
                        r=ttv, w=[ts_])
                yield
                yield from top16(cand, cand2, ts_, [ts2, ts2b], cv, ci, tcv, tci, 8)
                self.tt("vector", ee, cv, cv[:, :, 0:1].broadcast_to(B3), ALU.subtract, r=tcv, w=[tee])
                self.act(ee, ee, AF.Exp, r=[tee], w=[tee])
                self.v("vector", lambda e: e.reduce_sum(out=Zs, in_=ee, axis=AX.X), r=[tee], w=[tZs])
                self.v("vector", lambda e: e.reciprocal(out=rz, in_=Zs), r=[tZs], w=[tZs])
                self.tt("vector", w_, ee, rz.unsqueeze(2).broadcast_to(B3), ALU.mult, r=[tee, tZs], w=[tabw])
                yield
                self.v("vector", lambda e: e.tensor_single_scalar(out=r1u, in_=ci, scalar=4, op=ALU.logical_shift_right), r=tci, w=[tr1])
                self.v("vector", lambda e: e.tensor_single_scalar(out=r2u, in_=ci, scalar=15, op=ALU.bitwise_and), r=tci, w=[tr2])
                self.cp("vector", r1f, r1u, r=[tr1], w=[tr1])
                self.cp("vector", r2f, r2u, r=[tr2], w=[tr2])
                yield
                io4 = iota16.unsqueeze(1).unsqueeze(1).broadcast_to(B4)
                for (rf, trf, cc, dst) in ((r1f, tr1, 0, a_), (r2f, tr2, 1, b_)):
                    self.tt("vector", eq, rf.unsqueeze(3).broadcast_to(B4), io4, ALU.is_equal, r=[trf, self.tC], w=[ts_])
                    self.tt("vector", eq, eq, tifv[:, :, cc, :].unsqueeze(2).broadcast_to(B4), ALU.mult, r=[ts_] + tti, w=[ts_])
                    self.v("vector", lambda e, dst=dst: e.tensor_reduce(out=dst, in_=eq, axis=AX.X, op=ALU.add), r=[ts_], w=[tabw])
                    yield
            yield "TAIL"
            for tb2 in range(2):
                a_, b_, w_ = abw[tb2]
                pb = 6 + tb2
                for k_, src in enumerate((a_, b_, w_)):
                    self.tp(PB[pb][:, k_ * 128:(k_ + 1) * 128], src.rearrange("p a b -> p (a b)"), self.identf, r=[tabw, self.tC], w=[tPB[pb]])
                for k_, dstT in enumerate((aT, bT, wT)):
                    self.cp("scalar", dstT[:, tb2 * 128:(tb2 + 1) * 128], PB[pb][:, k_ * 128:(k_ + 1) * 128], r=[tPB[pb]], w=[taT])
            P.alias(tA + tB, tsmall)

        def build(g):
            for t8 in range(32):
                sl = t8 % 2
                tsl8 = slice(t8 * 8, (t8 + 1) * 8)
                iob = self.iotab.unsqueeze(1).broadcast_to(B3b)
                self.tt("vector", Ab[sl], iob, aT[:, tsl8].unsqueeze(2).broadcast_to(B3b), ALU.is_equal, r=[self.tC, taT], w=[tA[sl]])
                self.tt("vector", Bb[sl], iob, bT[:, tsl8].unsqueeze(2).broadcast_to(B3b), ALU.is_equal, r=[self.tC, taT], w=[tB[sl]])
                self.tt("vector", Ab[sl], Ab[sl], wT[:, tsl8].unsqueeze(2).broadcast_to(B3b), ALU.mult, r=[tA[sl], taT], w=[tA[sl]])
                for q in range(2):
                    pb = 6 + q
                    for j in range(4):
                        jj = q * 4 + j
                        self.mm(PB[pb][:, j * 128:(j + 1) * 128], Bb[sl][:, jj, :], Ab[sl][:, jj, :], True, True, r=[tA[sl], tB[sl]], w=[tPB[pb]])
                    t0_ = t8 * 8 + q * 4
                    self.cp("scalar", Gs[:, t0_:t0_ + 4, :], PB[pb][:].rearrange("p (a b) -> p a b", a=4), r=[tPB[pb]], w=[tGs])

        gen = phase1a(0, 0)
        for _ in gen:
            pass
        for g in range(8):
            buf = g % 2
            gen = phase1a(g + 1, 1 - buf) if g + 1 < 8 else iter(())
            gen_tail = False
            for _ in range(0):
                try:
                    next(gen)
                except StopIteration:
                    break
            build(g)

            def pull(n):
                nonlocal gen_tail
                if gen_tail:
                    return
                for _ in range(n):
                    try:
                        r_ = next(gen)
                    except StopIteration:
                        gen_tail = True
                        return
                    if r_ == "TAIL":
                        gen_tail = True
                        return

            def load(c):
                sl = c % NS
                self.dma("sync", UB[sl].rearrange("p k e -> p (k e)"), self.ubf[l, c * 128:(c + 1) * 128, :], r=[self.tUbf[l]], w=[tU[sl]])
                self.dma("sync", VB[sl], self.vbf[l, c * 128:(c + 1) * 128, :], r=[self.tVbf[l]], w=[tV[sl]])

            def umm(c):
                pp = 4 + c % 2
                sl = c % NS
                for kc in range(8):
                    self.mm(PB[pp][:, 0:256], UB[sl][:, kc, :], xT[buf][:, kc, :], kc == 0, kc == 7, r=[tU[sl], txT[buf]], w=[tPB[pp]])

            def mid(c):
                pp = 4 + c % 2
                self.act(Hs[c % 2], PB[pp][:, 0:256], AF.Gelu, r=[tPB[pp]], w=[tHs[c % 2]])
                self.tt(MAIN_MULT_ENG, Wt[c % 2], Hs[c % 2], Gs[:, :, c], ALU.mult, r=[tHs[c % 2], tGs], w=[tWt[c % 2]])

            def vmm(c):
                sl = c % NS
                for tb2 in range(2):
                    for half in range(2):
                        bk = tb2 * 2 + half
                        self.mm(PB[bk][:], Wt[c % 2][:, tb2 * 128:(tb2 + 1) * 128], VB[sl][:, half * 512:(half + 1) * 512], c == 0, c == 127,
                                r=[tWt[c % 2], tV[sl]], w=[tPB[bk]])

            for c in range(NS - 1):
                load(c)
            umm(0); mid(0)
            for c in range(128):
                if c + NS - 1 < 128:
                    load(c + NS - 1)
                if c + 1 < 128:
                    umm(c + 1); mid(c + 1)
                vmm(c)
                pull(1)
            while True:
                try:
                    next(gen)
                except StopIteration:
                    break
            for tb2 in range(2):
                tb = 2 * g + tb2
                for half in range(2):
                    hs = slice(half * 512, (half + 1) * 512)
                    bk = tb2 * 2 + half
                    self.stt("vector", self.RH[:, tb, hs], self.RH[:, tb, hs], ALPHA, PB[bk][:], ALU.mult, ALU.add,
                             r=[self.tH[tb], tPB[bk]], w=[self.tH[tb]])
                self.layer_norm(tb, lng, lnb, tln, scr)
        P.alias(self.tHT, scratch_trs + [ts2b])
        self.tRG = [tGs]
        self.tXtmp = tHs + tWt
        self.tXall = newX + tsmall

    def mla(self):
        I = self.I
        RX, RG = self.RX, self.RG
        PB, tPB = self.PB, self.tPB
        HT, tHT = self.HT, self.tHT
        P = self.P
        SCALE = 192.0 ** -0.5
        lng, lnb, tln = self.load_ln(I["lnmg"][1:2, :], I["lnmb"][1:2, :], 31744)
        self.build_HT()
        cos64 = self.carve(RX, 2048, [128, S], F32)
        sin64 = self.carve(RX, 2048 + 8192, [128, S], F32)
        tcs = Tr("cs64")
        tmp = [self.carve(RX, 18432 + i * 2048, [128, 512], F32) for i in range(4)]
        ttmp = [Tr("mtmp%d" % i) for i in range(4)]
        mwsw = self.carve(RX, 26624, [128, 8, 64], BF16)
        wuqsw = self.carve(RX, 27648, [128, 3, 8, 64], BF16)
        tsw = Tr("sw")
        gains = self.carve(RX, 39936, [128, 8], F32)
        tgain = Tr("gains")
        small = self.carve(RX, 40064, [128, 64], F32)
        tsm = Tr("small")
        scr = {"st6": small[:, 0:12], "mv": small[:, 12:14], "rstd": small[:, 14:15], "t": tsm}
        newX = [tcs, tsw, tgain, tsm] + ttmp
        P.alias(newX, self.tXall)
        Ob2 = [self.carve(RX, 512 + i * 256, [128, 128], BF16) for i in range(2)]
        OT = self.carve(RX, 256, [128, 128], BF16)
        tOb2, tOT = [Tr("Ob0"), Tr("Ob1")], Tr("OT")
        P.alias(tOb2 + [tOT], self.tXtmp)
        rz2 = [small[:, 16:17], small[:, 17:18]]
        trz2 = [Tr("rz0"), Tr("rz1")]
        P.alias(trz2, self.tXall)
        self.trig_tables(1, 64, cos64, sin64, tcs, sgn_col=2)
        cnT = self.carve(RG, 0, [128, 5, S], BF16)
        kropeT = self.carve(RG, 20480, [128, S], BF16)
        wuq = self.carve(RG, 24576, [128, 3, 1536], BF16)
        wukv = self.carve(RG, 33792, [128, 2, 2048], BF16)
        wout = self.carve(RG, 41984, [128, 8, 1024], BF16)
        mwin = self.carve(RG, 41984, [128, 8, 704], BF16)
        craw = self.carve(RG, 58368, [128, 3, 512], F32)
        tcn, tkr, twuq, twukv, twout, tcraw = [Tr(n) for n in ("cnT", "kropeT", "wuq", "wukv", "wout", "craw")]
        P.alias([tcn, tkr, twuq, twukv, twout, tcraw], self.tRG)
        self.dma("gpsimd", mwin, I["mwin"].rearrange("(k p) c -> p k c", p=128), w=[twout])
        for kc in range(3):
            self.dma("gpsimd", wuq[:, kc, :], I["mwuq"][kc * 128:(kc + 1) * 128, :], w=[twuq])
        for kc in range(2):
            for hh in range(2):
                self.dma("gpsimd", wukv[:, kc, hh * 1024:(hh + 1) * 1024], I["mwukv"][kc * 128:(kc + 1) * 128, hh * 1024:(hh + 1) * 1024], w=[twukv])
        self.dma("sync", gains[:, 0:3], I["mqn"], w=[tgain])
        self.dma("sync", gains[:, 3:5], I["mkvn"], w=[tgain])
        self.cp("vector", mwsw[:, :, 0:32], mwin[:, :, 672:704], r=[twout], w=[tsw])
        self.cp("vector", mwsw[:, :, 32:64], mwin[:, :, 640:672], r=[twout], w=[tsw])
        wuqv = wuq.rearrange("p k (h c) -> p k h c", h=8)
        self.cp("vector", wuqsw[:, :, :, 0:32], wuqv[:, :, :, 160:192], r=[twuq], w=[tsw])
        self.cp("vector", wuqsw[:, :, :, 32:64], wuqv[:, :, :, 128:160], r=[twuq], w=[tsw])
        self.v("gpsimd", lambda e: e.memset(kropeT[64:128, :], 0.0), w=[tkr])
        MS = int(os.environ.get("MLA_STOP", "99"))
        if MS <= 1:
            return
        for tq in range(4):
            tsl = slice(tq * 512, (tq + 1) * 512)
            hts = [tHT[tq * 4 + i] for i in range(4)]
            for (fcs, sumbank, nfeat, goff) in (((0, 1, 2), 2, 384.0, 0), ((3, 4), 3, 256.0, 3)):
                for i_, fc in enumerate(fcs):
                    pb = fc % 2
                    for kc in range(8):
                        self.mm(PB[pb][:], mwin[:, kc, fc * 128:(fc + 1) * 128], HT[:, kc, tsl], kc == 0, kc == 7, r=[twout] + hts, w=[tPB[pb]])
                    self.cp("vector", craw[:, i_, :], PB[pb][:], r=[tPB[pb]], w=[tcraw])
                    if "norm" in os.environ.get("MLA_SKIP", ""):
                        continue
                    self.tt("vector", tmp[pb], craw[:, i_, :], PB[pb][:], ALU.mult, r=[tPB[pb], tcraw], w=[ttmp[pb]])
                    if "ones" not in os.environ.get("MLA_SKIP", ""):
                        self.mm(PB[sumbank][:], self.onesf, tmp[pb], i_ == 0, i_ == len(fcs) - 1, r=[self.tC, ttmp[pb]], w=[tPB[sumbank]])
                if "norm" in os.environ.get("MLA_SKIP", "") or "sqrt" in os.environ.get("MLA_SKIP", ""):
                    continue
                self.act(tmp[2], PB[sumbank][:], AF.Sqrt, r=[tPB[sumbank]], w=[ttmp[2]], scale=1.0 / nfeat, bias=EPS)
                self.v("vector", lambda e: e.reciprocal(out=tmp[2], in_=tmp[2]), r=[ttmp[2]], w=[ttmp[2]])
                for i_, fc in enumerate(fcs):
                    self.stt("vector", cnT[:, fc, tsl], craw[:, i_, :], gains[:, goff + i_:goff + i_ + 1], tmp[2], ALU.mult, ALU.mult,
                             r=[tcraw, tgain, ttmp[2]], w=[tcn])
            if "rope" in os.environ.get("MLA_SKIP", ""):
                continue
            for kc in range(8):
                self.mm(PB[6][0:64, :], mwin[:, kc, 640:704], HT[:, kc, tsl], kc == 0, kc == 7, r=[twout] + hts, w=[tPB[6]])
            for kc in range(8):
                self.mm(PB[7][0:64, :], mwsw[:, kc, :], HT[:, kc, tsl], kc == 0, kc == 7, r=[tsw] + hts, w=[tPB[7]])
            self.tt("vector", tmp[0][0:64], PB[6][0:64, :], cos64[0:64, tsl], ALU.mult, r=[tPB[6], tcs], w=[ttmp[0]])
            self.tt("vector", tmp[1][0:64], PB[7][0:64, :], sin64[0:64, tsl], ALU.mult, r=[tPB[7], tcs], w=[ttmp[1]])
            self.tt("gpsimd", kropeT[0:64, tsl], tmp[0][0:64], tmp[1][0:64], ALU.add, r=[ttmp[0], ttmp[1]], w=[tkr])
        if MS <= 2:
            return
        self.dma("gpsimd", wout, I["mwout"].rearrange("(h p) c -> p h c", p=128), w=[twout])
        RHT = self.RHT
        qnT = self.carve(RHT, 0, [128, S], BF16)
        qrT = self.carve(RHT, 4096, [128, S], BF16)
        knT = self.carve(RHT, 8192, [128, S], BF16)
        vh = self.carve(RHT, 12288, [128, NTB, 132], BF16)
        PT = self.carve(RHT, 16512, [128, NTB, 256], BF16)
        tqn, tqr, tkn, tvh, tPT = [Tr(n) for n in ("qnT", "qrT", "knT", "vh", "PTm")]
        P.alias([tqn, tqr, tkn, tvh, tPT], tHT)
        self.v("vector", lambda e: e.memset(vh[:, :, 128:129], 1.0), w=[tvh])
        self.v("gpsimd", lambda e: e.memset(qrT[64:128, :], 0.0), w=[tqr])
        for h in range(8):
            self.convert_tables(1, h, 8)
            for tq in range(4):
                tsl = slice(tq * 512, (tq + 1) * 512)
                for kc in range(3):
                    self.mm(PB[0][:], wuq[:, kc, 192 * h:192 * h + 128], cnT[:, kc, tsl], kc == 0, kc == 2, r=[twuq, tcn], w=[tPB[0]])
                self.cp("scalar", qnT[:, tsl], PB[0][:], r=[tPB[0]], w=[tqn])
                for kc in range(2):
                    self.mm(PB[1][:], wukv[:, kc, 256 * h:256 * h + 128], cnT[:, 3 + kc, tsl], kc == 0, kc == 1, r=[twukv, tcn], w=[tPB[1]])
                self.cp("scalar", knT[:, tsl], PB[1][:], r=[tPB[1]], w=[tkn])
                for kc in range(3):
                    self.mm(PB[6][0:64, :], wuq[:, kc, 192 * h + 128:192 * h + 192], cnT[:, kc, tsl], kc == 0, kc == 2, r=[twuq, tcn], w=[tPB[6]])
                for kc in range(3):
                    self.mm(PB[7][0:64, :], wuqsw[:, kc, h, :], cnT[:, kc, tsl], kc == 0, kc == 2, r=[tsw, tcn], w=[tPB[7]])
                self.tt("vector", tmp[0][0:64], PB[6][0:64, :], cos64[0:64, tsl], ALU.mult, r=[tPB[6], tcs], w=[ttmp[0]])
                self.tt("vector", tmp[1][0:64], PB[7][0:64, :], sin64[0:64, tsl], ALU.mult, r=[tPB[7], tcs], w=[ttmp[1]])
                self.tt("gpsimd", qrT[0:64, tsl], tmp[0][0:64], tmp[1][0:64], ALU.add, r=[ttmp[0], ttmp[1]], w=[tqr])
            if MS <= 3:
                return
            for tb in range(NTB):
                pb = tb % 2
                for kc in range(2):
                    self.mm(PB[pb][:, 0:128], cnT[:, 3 + kc, tb * 128:(tb + 1) * 128], wukv[:, kc, 256 * h + 128:256 * h + 256], kc == 0, kc == 1,
                            r=[twukv, tcn], w=[tPB[pb]])
                self.cp("scalar", vh[:, tb, 0:128], PB[pb][:, 0:128], r=[tPB[pb]], w=[tvh])
            if MS <= 4:
                return
            pending = None
            for nb2 in range(8):
                nsl = slice(nb2 * 256, (nb2 + 1) * 256)
                for mb in range(NTB):
                    sb_ = (2, 3, 0)[mb % 3]
                    msl = slice(mb * 128, (mb + 1) * 128)
                    self.mm(PB[sb_][:, 0:256], knT[:, msl], qnT[:, nsl], True, False, r=[tkn, tqn], w=[tPB[sb_]])
                    self.mm(PB[sb_][:, 0:256], kropeT[:, msl], qrT[:, nsl], False, True, r=[tkr, tqr], w=[tPB[sb_]])
                    self.act(PT[:, mb, :], PB[sb_][:, 0:256], AF.Exp, r=[tPB[sb_]], w=[tPT], scale=SCALE)
                for sub in range(2):
                    nb = nb2 * 2 + sub
                    ob = 4 + (nb % 2)
                    par = nb % 2
                    for mb in range(NTB):
                        self.mm(PB[ob][:, 0:129], PT[:, mb, sub * 128:(sub + 1) * 128], vh[:, mb, 0:129], mb == 0, mb == NTB - 1,
                                r=[tPT, tvh], w=[tPB[ob]])
                    if pending is not None:
                        pending()

                    def tail(nb=nb, ob=ob, par=par, h=h):
                        rz_ = rz2[par]
                        self.v("vector", lambda e: e.reciprocal(out=rz_, in_=PB[ob][:, 128:129]), r=[tPB[ob]], w=[trz2[par]])
                        self.ts("vector", Ob2[par], PB[ob][:, 0:128], rz_, None, ALU.mult, r=[tPB[ob], trz2[par]], w=[tOb2[par]])
                        pbT = PB[6][:].bitcast(BF16)
                        self.tp(pbT[:, 0:128], Ob2[par], self.identb, r=[tOb2[par], self.tC], w=[tPB[6]])
                        self.cp("scalar", OT, pbT[:, 0:128], r=[tPB[6]], w=[tOT])
                        for half in range(2):
                            wb = 7 if half == 0 else 1
                            hs = slice(half * 512, (half + 1) * 512)
                            self.mm(PB[wb][:], OT, wout[:, h, hs], True, True, r=[tOT, twout], w=[tPB[wb]])
                            if h == 0:
                                self.stt("vector", self.RH[:, nb, hs], self.RH[:, nb, hs], ALPHA, PB[wb][:], ALU.mult, ALU.add,
                                         r=[self.tH[nb], tPB[wb]], w=[self.tH[nb]])
                            else:
                                self.tt("vector", self.RH[:, nb, hs], self.RH[:, nb, hs], PB[wb][:], ALU.add,
                                        r=[self.tH[nb], tPB[wb]], w=[self.tH[nb]])
                    pending = tail
            pending()
            pending = None
        for tb in range(NTB):
            self.layer_norm(tb, lng, lnb, tln, scr)
        P.alias(tHT, [tqn, tqr, tkn, tvh, tPT])
        self.tRG = [tcn, tkr, twuq, twukv, twout, tcraw]
        self.tXtmp = tOb2 + [tOT]
        self.tXall = newX + trz2


def build_nc(stop_after=None):
    nc = bass.Bass("TRN2", target_bir_lowering=False)
    k = K(nc, stop_after=stop_after)
    k.build()
    return nc


_CACHE = {}


def host_consts():
    invf = np.zeros((128, 4), np.float32)
    invf[:, 0] = (10000.0 ** (-np.arange(0, 256, 2, dtype=np.float32) / np.float32(256))).astype(np.float32)
    f32 = (10000.0 ** (-np.arange(0, 64, 2, dtype=np.float32) / np.float32(64))).astype(np.float32)
    invf[0:32, 1] = f32
    invf[32:64, 1] = f32
    invf[0:32, 2] = -1.0
    invf[32:64, 2] = 1.0
    return invf


def make_in_maps(inp):
    c = np.ascontiguousarray
    f = lambda k: np.asarray(inp[k], dtype=np.float32)
    shared = {
        "rwin": c(f("ret_w_in")[0]), "rl1d": c(f("ret_log1m_decay")[0].reshape(1, 8)),
        "rgng": c(f("ret_gn_g")[0].reshape(1, 2048)), "rgnb": c(f("ret_gn_b")[0].reshape(1, 2048)),
        "rwout": c(f("ret_w_out")[0]),
        "mwin": c(f("mla_w_in")[0]), "mqn": c(f("mla_q_norm")[0].reshape(3, 128).T),
        "mkvn": c(f("mla_kv_norm")[0].reshape(2, 128).T), "mwuq": c(f("mla_w_uq")[0]),
        "mwukv": c(f("mla_w_ukv")[0]), "mwout": c(f("mla_w_out")[0]),
        "pwq": c(f("peer_w_q")),
        "pskT": c(f("peer_sub_keys").reshape(2, 16, 128, 128).transpose(0, 3, 1, 2)),
        "puT": c(f("peer_u").reshape(2, 128, 128, 8, 128).transpose(0, 1, 4, 3, 2).reshape(2, 128, 128, 1024)),
        "pv": c(f("peer_v")),
        "lnmg": c(f("ln_mix_g")), "lnmb": c(f("ln_mix_b")), "lnfg": c(f("ln_ffn_g")), "lnfb": c(f("ln_ffn_b")),
        "cinvf": host_consts(),
    }
    x = f("x")
    pos = np.asarray(inp["positions"], dtype=np.int32)
    maps = []
    for b in range(8):
        m = dict(shared)
        m["x"] = c(x[b])
        m["pos"] = c(pos[b].reshape(1, S))
        maps.append(m)
    return maps


def kernel(**inputs):
    if "nc" not in _CACHE:
        _CACHE["nc"] = build_nc()
    nc = _CACHE["nc"]
    maps = make_in_maps(inputs)
    res = run_bass_kernel_spmd(nc, maps, core_ids=list(range(8)))
    out = np.stack([np.asarray(r["out"], dtype=np.float32) for r in res.results], axis=0)
    return out
```

```python
import math, os, contextlib
import numpy as np
import concourse.bass as bass
import concourse.mybir as mybir
from concourse.bass_utils import run_bass_kernel_spmd

F32 = mybir.dt.float32
BF16 = mybir.dt.bfloat16
I32 = mybir.dt.int32
U32 = mybir.dt.uint32
ALU = mybir.AluOpType
AF = mybir.ActivationFunctionType
AX = mybir.AxisListType

ENGS = ["tensor", "vector", "scalar", "gpsimd", "sync"]
MAIN_MULT_ENG = os.environ.get("MAIN_MULT_ENG", "vector")
SELF_WAR = bool(int(os.environ.get("SELF_WAR", "1")))

S = 2048
D = 1024
NTB = 16
ALPHA = (2.0 * 2) ** 0.25
EPS = 1e-5
TWO_PI = 2.0 * math.pi
C1 = float(np.float32(TWO_PI))
C2 = TWO_PI - C1
PI_LO = 3.1415925


class Tr:
    __slots__ = ("name", "w", "r", "sem", "cnt")

    def __init__(self, name=""):
        self.name = name
        self.w = {}
        self.r = {}
        self.sem = None
        self.cnt = 0


class Op:
    __slots__ = ("eng", "fn", "deps", "dma", "sig", "semtr", "val", "raw", "post")


class Prog:
    def __init__(self, nc):
        self.nc = nc
        self.q = {e: [] for e in ENGS}
        self.all = []
        self._ord = {}

    def op(self, eng, fn, reads=(), writes=(), dma=False, post=None):
        o = Op()
        o.eng = eng; o.fn = fn; o.dma = dma; o.sig = False; o.val = None; o.post = post
        o.semtr = writes[0] if dma else None
        key = ("d", id(o.semtr)) if dma else ("e", eng)
        deps = []
        raw = set()
        for t in reads:
            for d in t.w.values():
                deps.append(d); raw.add(id(d))
        for t in writes:
            deps.extend(t.r.values())
            deps.extend(t.w.values())
        seen = set(); d2 = []
        for d in deps:
            if id(d) not in seen and d is not o:
                seen.add(id(d)); d2.append(d)
        o.deps = d2
        o.raw = raw
        rset = set(id(t) for t in reads)
        for t in writes:
            if t.r or (id(t) in rset):
                t.w = {key: o}; t.r = {}
            else:
                t.w[key] = o
        wset = set(id(t) for t in writes)
        for t in reads:
            if id(t) not in wset:
                t.r[key] = o
        self.q[eng].append(o)
        self._ord[id(o)] = len(self.all)
        self.all.append(o)
        return o

    def alias(self, new_trs, old_trs):
        merged = {}
        for t in old_trs:
            for dct in (t.w, t.r):
                for k, o in dct.items():
                    b = merged.get(k)
                    if b is None or self._ord[id(o)] > self._ord[id(b)]:
                        merged[k] = o
        for t in new_trs:
            t.w = {}
            t.r = dict(merged)

    def finalize_and_emit(self, final_waits=()):
        nc = self.nc
        for o in self.all:
            best = {}
            for d in o.deps:
                if d.dma:
                    k = ("d", id(d.semtr))
                else:
                    if d.eng == o.eng and (o.eng == "tensor" or (id(d) not in o.raw and not SELF_WAR)):
                        continue
                    k = ("e", d.eng)
                b = best.get(k)
                if b is None or self._ord[id(d)] > self._ord[id(b)]:
                    best[k] = d
            o.deps = list(best.values())
            for d in o.deps:
                d.sig = True
        for o in final_waits:
            o.sig = True
        for o in self.all:
            if o.dma:
                o.sig = o.sig or (os.environ.get("DMASIG","1")=="1")
        stack = contextlib.ExitStack()
        engsem = {}
        for e in ENGS:
            engsem[e] = stack.enter_context(nc.semaphore("s_" + e))
        engcnt = {e: 0 for e in ENGS}
        nsem = 0
        for o in self.all:
            if not o.sig:
                continue
            if o.dma:
                t = o.semtr
                if t.sem is None:
                    t.sem = stack.enter_context(nc.semaphore("d%d" % nsem))
                    nsem += 1
                t.cnt += 16
                o.val = (t.sem, t.cnt)
            else:
                engcnt[o.eng] += 1
                o.val = (engsem[o.eng], engcnt[o.eng])
        self.nsem = nsem
        block = stack.enter_context(nc.Block())
        prog = self

        def emit(engname, engobj):
            waited = {}
            for o in prog.q[engname]:
                for d in o.deps:
                    s, v = d.val
                    k = id(s)
                    if waited.get(k, 0) >= v:
                        continue
                    waited[k] = v
                    engobj.wait_ge(s, v)
                ins = o.fn(engobj)
                if o.post is not None and o.sig:
                    ins = o.post(engobj)
                if o.sig:
                    s, v = o.val
                    ins.then_inc(s, 16 if o.dma else 1)
            if engname == "sync":
                for o in final_waits:
                    s, v = o.val
                    engobj.wait_ge(s, v)

        @block.tensor
        def _(e):
            emit("tensor", e)

        @block.vector
        def _(e):
            emit("vector", e)

        @block.scalar
        def _(e):
            emit("scalar", e)

        @block.gpsimd
        def _(e):
            emit("gpsimd", e)

        @block.sync
        def _(e):
            emit("sync", e)

        stack.close()


class K:
    def __init__(self, nc, stop_after=None):
        self.nc = nc
        self.P = Prog(nc)
        self.stop_after = stop_after
        self.st = contextlib.ExitStack()

    def v(self, eng, fn, r=(), w=(), post=None):
        return self.P.op(eng, fn, reads=list(r), writes=list(w), post=post)

    def dma(self, eng, out, in_, r=(), w=(), **kw):
        return self.P.op(eng, lambda e: e.dma_start(out=out, in_=in_, **kw), reads=list(r), writes=list(w), dma=True)

    def mm(self, out, lhsT, rhs, start, stop, r=(), w=()):
        return self.P.op("tensor", lambda e: e.matmul(out, lhsT=lhsT, rhs=rhs, start=start, stop=stop),
                         reads=list(r), writes=list(w))

    def tp(self, out, in_, ident, r=(), w=()):
        return self.P.op("tensor", lambda e: e.transpose(out=out, in_=in_, identity=ident), reads=list(r), writes=list(w))

    def act(self, out, in_, func, r=(), w=(), **kw):
        return self.P.op("scalar", lambda e: e.activation(out=out, in_=in_, func=func, **kw), reads=list(r), writes=list(w))

    def tt(self, eng, out, in0, in1, op, r=(), w=()):
        return self.P.op(eng, lambda e: e.tensor_tensor(out=out, in0=in0, in1=in1, op=op), reads=list(r), writes=list(w))

    def ts(self, eng, out, in0, s1, s2, op0, op1=None, r=(), w=()):
        if op1 is None:
            return self.P.op(eng, lambda e: e.tensor_scalar(out=out, in0=in0, scalar1=s1, scalar2=None, op0=op0),
                             reads=list(r), writes=list(w))
        return self.P.op(eng, lambda e: e.tensor_scalar(out=out, in0=in0, scalar1=s1, scalar2=s2, op0=op0, op1=op1),
                         reads=list(r), writes=list(w))

    def stt(self, eng, out, in0, scalar, in1, op0, op1, r=(), w=()):
        return self.P.op(eng, lambda e: e.scalar_tensor_tensor(out=out, in0=in0, scalar=scalar, in1=in1, op0=op0, op1=op1),
                         reads=list(r), writes=list(w))

    def cp(self, eng, out, in_, r=(), w=()):
        if eng == "scalar":
            return self.P.op(eng, lambda e: e.copy(out=out, in_=in_), reads=list(r), writes=list(w))
        return self.P.op(eng, lambda e: e.tensor_copy(out=out, in_=in_), reads=list(r), writes=list(w))

    def sbuf(self, name, shape, dt):
        return self.st.enter_context(self.nc.sbuf_tensor(name, shape, dt))

    def psum(self, name, shape, dt):
        return self.st.enter_context(self.nc.psum_tensor(name, shape, dt))

    @staticmethod
    def carve(region, off_bytes, shape, dt):
        n = 1
        for s_ in shape[1:]:
            n *= s_
        esz = 4 if dt in (F32, I32, U32) else 2
        nb = n * esz
        assert off_bytes % 4 == 0 and nb % 4 == 0
        ap = region[0:shape[0], off_bytes // 4:(off_bytes + nb) // 4]
        if dt != F32:
            ap = ap.bitcast(dt)
        if len(shape) == 3:
            ap = ap.rearrange("p (a b) -> p a b", a=shape[1])
        elif len(shape) == 4:
            ap = ap.rearrange("p (a b c) -> p a b c", a=shape[1], b=shape[2])
        return ap

    def build(self):
        nc = self.nc
        dt_ = nc.dram_tensor
        I = {}

        def din(name, shape, dt=F32):
            I[name] = dt_(name, shape, dt, kind="ExternalInput").ap()

        din("x", [S, D]); din("pos", [1, S], I32)
        din("rwin", [D, 6144]); din("rl1d", [1, 8]); din("rgng", [1, 2048]); din("rgnb", [1, 2048]); din("rwout", [2048, D])
        din("mwin", [D, 704]); din("mqn", [128, 3]); din("mkvn", [128, 2]); din("mwuq", [384, 1536]); din("mwukv", [256, 2048])
        din("mwout", [D, D])
        din("pwq", [2, D, 2048]); din("pskT", [2, 128, 16, 128]); din("puT", [2, 128, 128, 1024]); din("pv", [2, 16384, D])
        din("lnmg", [2, D]); din("lnmb", [2, D]); din("lnfg", [2, D]); din("lnfb", [2, D])
        din("cinvf", [128, 4])
        self.I = I
        self.out = dt_("out", [S, D], F32, kind="ExternalOutput").ap()
        self.ubf = dt_("ubf", [2, 128 * 128, 1024], BF16, kind="Internal").ap()
        self.vbf = dt_("vbf", [2, 16384, 1024], BF16, kind="Internal").ap()
        self.tUbf = [Tr("ubf0"), Tr("ubf1")]
        self.tVbf = [Tr("vbf0"), Tr("vbf1")]

        self.RH = self.sbuf("RH", [128, NTB, D], F32)
        self.RHT = self.sbuf("RHT", [128, 8192], F32)
        self.RG = self.sbuf("RG", [128, 16384], F32)
        self.RX = self.sbuf("RX", [128, 10752], F32)
        self.RC = self.sbuf("RC", [128, 1408], F32)
        self.tH = [Tr("H%d" % i) for i in range(NTB)]
        self.tHT = [Tr("HT%d" % i) for i in range(NTB)]
        self.HT = self.carve(self.RHT, 0, [128, 8, S], BF16)
        self.tRG = [Tr("RG")]
        self.tRHT_alias = []
        self.PB = [self.psum("pb%d" % i, [128, 512], F32) for i in range(8)]
        self.tPB = [Tr("pb%d" % i) for i in range(8)]

        RC = self.RC
        self.identf = RC[:, 0:128]
        self.iotaf = RC[:, 128:256]
        self.identb = RC[:, 256:320].bitcast(BF16)
        self.invf = RC[:, 320:324]
        self.lg = RC[:, 324:332]
        self.iota16 = RC[:, 332:348]
        self.onesf = RC[:, 384:512]
        self.dmy = RC[:, 584:640]
        self._dmy_i = 0
        self.iotab = RC[:, 520:584].bitcast(BF16)
        self.tC = Tr("consts")
        tC = self.tC
        tmpi = self.carve(self.RX, 0, [128, 128], I32)
        ttmp = Tr("tmpi")
        self.v("gpsimd", lambda e: e.iota(tmpi, pattern=[[1, 128]], base=0, channel_multiplier=0), w=[ttmp])
        self.cp("vector", self.iotaf, tmpi, r=[ttmp], w=[tC])
        self.cp("vector", self.iota16, tmpi[:, 0:16], r=[ttmp], w=[tC])
        self.cp("vector", self.iotab, tmpi, r=[ttmp], w=[tC])
        self.v("gpsimd", lambda e: e.iota(tmpi, pattern=[[1, 128]], base=0, channel_multiplier=-1), w=[ttmp])
        tmpf = self.carve(self.RX, 512, [128, 128], F32)
        ttmpf = Tr("tmpf")
        self.cp("vector", tmpf, tmpi, r=[ttmp], w=[ttmpf])
        self.v("vector", lambda e: e.tensor_single_scalar(out=self.identf, in_=tmpf, scalar=0.0, op=ALU.is_equal), r=[ttmpf], w=[tC])
        self.v("vector", lambda e: e.tensor_single_scalar(out=self.identb, in_=tmpf, scalar=0.0, op=ALU.is_equal), r=[ttmpf], w=[tC])
        self.v("vector", lambda e: e.memset(self.onesf, 1.0), w=[tC])
        self.dma("sync", self.invf, I["cinvf"], w=[tC])
        self.tXtmp = [ttmp, ttmpf]

        for tb in range(NTB):
            self.dma("sync", self.RH[:, tb, :], I["x"][tb * 128:(tb + 1) * 128, :], w=[self.tH[tb]])

        self.fin = []
        self.puT_flat = I["puT"].rearrange("l c p e -> l (c p) e")
        self.retention()
        if self.stop_after == "ret":
            return self.finish()
        self.peer(0)
        if self.stop_after == "peer0":
            return self.finish()
        self.mla()
        if self.stop_after == "mla":
            return self.finish()
        self.peer(1)
        return self.finish()

    def convert_tables(self, l, part, nparts):
        n = 32 // nparts
        for i in range(part * n, (part + 1) * n):
            rs = slice(i * 512, (i + 1) * 512)
            self.dma("gpsimd", self.ubf[l, rs, :], self.puT_flat[l, rs, :], w=[self.tUbf[l]])
            self.dma("gpsimd", self.vbf[l, rs, :], self.I["pv"][l, rs, :], w=[self.tVbf[l]])

    def finish(self):
        for tb in range(NTB):
            o = self.dma("sync", self.out[tb * 128:(tb + 1) * 128, :], self.RH[:, tb, :], r=[self.tH[tb]], w=[Tr("out%d" % tb)])
            self.fin.append(o)
        self.P.finalize_and_emit(final_waits=self.fin)
        self.st.close()

    def build_HT(self):
        hb = self.carve(self.RX, 0, [128, D], BF16)
        thb = Tr("hb")
        self.P.alias([thb], self.tXtmp)
        pbT = self.PB[6][:].bitcast(BF16).rearrange("p (a b) -> p a b", a=8)
        for tb in range(NTB):
            self.cp("scalar", hb, self.RH[:, tb, :], r=[self.tH[tb]], w=[thb])
            for kc in range(8):
                self.tp(pbT[:, kc, :], hb[:, kc * 128:(kc + 1) * 128], self.identb, r=[thb, self.tC], w=[self.tPB[6]])
            self.cp("vector", self.HT[:, :, tb * 128:(tb + 1) * 128], pbT, r=[self.tPB[6]], w=[self.tHT[tb]])
        self.tXtmp = [thb]

    def trig_tables(self, col, nparts, cosT, sinT, tcs, sgn_col=None):
        I = self.I
        RX = self.RX
        posi = self.carve(self.RG, 0, [128, S], I32)
        ang = self.carve(self.RG, 8192, [128, S], F32)
        ki = self.carve(self.RG, 16384, [128, S], I32)
        kf = self.carve(self.RG, 24576, [128, S], F32)
        r = self.carve(self.RG, 32768, [128, S], F32)
        rc = self.carve(self.RG, 40960, [128, S], F32)
        tt_ = [Tr("trig%d" % i) for i in range(6)]
        self.P.alias(tt_, self.tRG)
        tpos, tang, tki, tkf, tr_, trc = tt_
        n = nparts
        self.dma("sync", posi, I["pos"].partition_broadcast(128), w=[tpos])
        self.cp("vector", ang, posi, r=[tpos], w=[tang])
        self.ts("vector", ang[0:n], ang[0:n], self.invf[0:n, col:col + 1], None, ALU.mult, r=[tang, self.tC], w=[tang])
        self.ts("vector", ki[0:n], ang[0:n], 1.0 / TWO_PI, None, ALU.mult, r=[tang], w=[tki])
        self.cp("vector", kf[0:n], ki[0:n], r=[tki], w=[tkf])
        self.stt("vector", r[0:n], kf[0:n], -C1, ang[0:n], ALU.mult, ALU.add, r=[tkf, tang], w=[tr_])
        self.stt("vector", r[0:n], kf[0:n], -C2, r[0:n], ALU.mult, ALU.add, r=[tkf, tr_], w=[tr_])
        self.ts("vector", rc[0:n], r[0:n], math.pi / 2, None, ALU.add, r=[tr_], w=[trc])
        self.ts("vector", kf[0:n], rc[0:n], math.pi, -TWO_PI, ALU.is_gt, ALU.mult, r=[trc], w=[tkf])
        self.tt("vector", rc[0:n], rc[0:n], kf[0:n], ALU.add, r=[trc, tkf], w=[trc])
        self.ts("vector", r[0:n], r[0:n], PI_LO, -PI_LO, ALU.min, ALU.max, r=[tr_], w=[tr_])
        self.ts("vector", rc[0:n], rc[0:n], PI_LO, -PI_LO, ALU.min, ALU.max, r=[trc], w=[trc])
        self.act(sinT[0:n], r[0:n], AF.Sin, r=[tr_], w=[tcs])
        self.act(cosT[0:n], rc[0:n], AF.Sin, r=[trc], w=[tcs])
        if sgn_col is not None:
            self.ts("vector", sinT[0:n], sinT[0:n], self.invf[0:n, sgn_col:sgn_col + 1], None, ALU.mult, r=[tcs, self.tC], w=[tcs])
        self.tRG = tt_

    def layer_norm(self, tb, g_ap, b_ap, tgb, scr):
        self.layer_norm_multi([tb], g_ap, b_ap, tgb, [scr])

    def layer_norm_multi(self, tbs, g_ap, b_ap, tgb, scrs):
        Hb = [self.RH[:, tb, :] for tb in tbs]
        tH = [self.tH[tb] for tb in tbs]
        n = len(tbs)
        for i in range(n):
            sc = scrs[i]
            self.v("vector", lambda e, sc=sc, i=i: e.bn_stats(out=sc["st6"][:, 0:6], in_=Hb[i][:, 0:512]), r=[tH[i]], w=[sc["t"]])
            self.v("vector", lambda e, sc=sc, i=i: e.bn_stats(out=sc["st6"][:, 6:12], in_=Hb[i][:, 512:1024]), r=[tH[i]], w=[sc["t"]])
        for i in range(n):
            sc = scrs[i]
            self.v("vector", lambda e, sc=sc: e.bn_aggr(out=sc["mv"], in_=sc["st6"]), r=[sc["t"]], w=[sc["t"]])
        for i in range(n):
            sc = scrs[i]
            self.act(sc["rstd"], sc["mv"][:, 1:2], AF.Sqrt, r=[sc["t"]], w=[sc["t"]], bias=EPS, scale=1.0)
        for i in range(n):
            sc = scrs[i]
            self.v("vector", lambda e, sc=sc: e.reciprocal(out=sc["rstd"], in_=sc["rstd"]), r=[sc["t"]], w=[sc["t"]])
        for i in range(n):
            sc = scrs[i]
            self.ts("vector", Hb[i], Hb[i], sc["mv"][:, 0:1], sc["rstd"], ALU.subtract, ALU.mult, r=[tH[i], sc["t"]], w=[tH[i]])
        for i in range(n):
            self.tt("vector", Hb[i], Hb[i], g_ap, ALU.mult, r=[tH[i], tgb], w=[tH[i]])
        for i in range(n):
            self.tt("vector", Hb[i], Hb[i], b_ap, ALU.add, r=[tH[i], tgb], w=[tH[i]])

    def load_ln(self, gsrc, bsrc, off):
        g = self.carve(self.RX, off, [128, D], F32)
        b = self.carve(self.RX, off + 4096, [128, D], F32)
        t = Tr("lngb")
        self.P.alias([t], self.tLN if hasattr(self, "tLN") else [])
        self.dma("sync", g, gsrc.partition_broadcast(128), w=[t])
        self.dma("sync", b, bsrc.partition_broadcast(128), w=[t])
        self.tLN = [t]
        return g, b, t

    def retention(self):
        I = self.I
        RX, RG = self.RX, self.RG
        cosT = self.carve(RX, 2048, [128, S], F32)
        sinT = self.carve(RX, 2048 + 8192, [128, S], F32)
        tcs = Tr("cossin")
        self.trig_tables(0, 128, cosT, sinT, tcs)
        tlg = self.tC
        l1 = self.carve(RX, 40000, [128, 8], F32)
        tl1 = Tr("l1")
        self.dma("sync", l1, I["rl1d"].partition_broadcast(128), w=[tl1])
        self.act(l1, l1, AF.Exp, r=[tl1], w=[tl1])
        self.act(self.lg, l1, AF.Ln, r=[tl1], w=[tlg], scale=-1.0, bias=1.0)
        self.build_HT()

        tmp = [self.carve(RX, 18432 + i * 2048, [128, 512], F32) for i in range(4)]
        ttmp = [Tr("rtmp%d" % i) for i in range(4)]
        gng = self.carve(RX, 26624, [128, 512], F32)
        gnb = self.carve(RX, 26624 + 2048, [128, 512], F32)
        tgn = Tr("gn")
        ZT = self.carve(RX, 30720, [128, 4, 128], BF16)
        tZT = Tr("ZT")
        small = self.carve(RX, 40064, [128, 32], F32)
        tsm = Tr("small")
        scr = {"st6": small[:, 0:12], "mv": small[:, 12:14], "rstd": small[:, 14:15], "t": tsm}
        lng, lnb, tln = self.load_ln(I["lnmg"][0:1, :], I["lnmb"][0:1, :], 31744)

        QT = self.carve(RG, 0, [128, 2, S], BF16)
        KT = self.carve(RG, 8192, [128, 2, S], BF16)
        VH = self.carve(RG, 16384, [128, NTB, 512], BF16)
        PT = self.carve(RG, 32768, [128, NTB, 256], BF16)
        Dtab = self.carve(RG, 40960, [128, 31, 128], BF16)
        W1 = self.carve(RG, 49152, [128, 8, 512], BF16)
        W2 = self.carve(RG, 57344, [128, 8, 512], BF16)
        W2o = self.carve(RG, 57344, [128, 4, 1024], BF16)
        tQT, tKT, tVH, tPT, tD, tW1, tW2 = [Tr(n) for n in ("QT", "KT", "VH", "PT", "Dtab", "W1", "W2")]
        self.P.alias([tQT, tKT, tVH, tPT, tD, tW1, tW2], self.tRG)
        Zb2 = [self.carve(RX, i * 1024, [128, 512], BF16) for i in range(2)]
        tZb2 = [Tr("Zb0"), Tr("Zb1")]
        self.P.alias(tZb2, self.tXtmp)
        tsm2 = [Tr("small0"), Tr("small1")]
        scr2 = [{"st6": small[:, 16 * i:16 * i + 12], "mv": small[:, 16 * i + 12:16 * i + 14], "rstd": small[:, 16 * i + 14:16 * i + 15], "t": tsm2[i]}
                for i in range(2)]
        iot = tmp[3].bitcast(I32)

        PB, tPB = self.PB, self.tPB
        HT, tHT = self.HT, self.tHT
        rwin = I["rwin"]
        for h in range(4):
            self.dma("gpsimd", W1[:, :, 0:256], rwin[:, h * 256:(h + 1) * 256].rearrange("(k p) c -> p k c", p=128), w=[tW1])
            self.dma("gpsimd", W1[:, :, 256:512], rwin[:, 1024 + h * 256:1024 + (h + 1) * 256].rearrange("(k p) c -> p k c", p=128), w=[tW1])
            self.dma("gpsimd", W2, rwin[:, 2048 + h * 512:2048 + (h + 1) * 512].rearrange("(k p) c -> p k c", p=128), w=[tW2])
            self.dma("sync", gng, I["rgng"][0:1, h * 512:(h + 1) * 512].partition_broadcast(128), w=[tgn])
            self.dma("sync", gnb, I["rgnb"][0:1, h * 512:(h + 1) * 512].partition_broadcast(128), w=[tgn])
            dflat = Dtab.rearrange("p a b -> p (a b)")
            for pc in range(8):
                r0 = pc * 4
                nr = min(4, 31 - r0)
                w_ = nr * 128
                io_v = iot[:, 0:w_]
                self.v("gpsimd", lambda e, io_v=io_v, nr=nr, r0=r0: e.iota(io_v, pattern=[[128, nr], [1, 128]], base=128 * (r0 - 15), channel_multiplier=-1),
                       w=[ttmp[3]])
                self.ts("vector", tmp[0][:, 0:w_], iot[:, 0:w_], 0.0, self.lg[:, h:h + 1], ALU.max, ALU.mult, r=[ttmp[3], self.tC], w=[ttmp[0]])
                self.ts("vector", tmp[1][:, 0:w_], iot[:, 0:w_], 0.0, self.lg[:, 4 + h:5 + h], ALU.min, ALU.mult, r=[ttmp[3], self.tC], w=[ttmp[1]])
                self.tt("vector", tmp[0][:, 0:w_], tmp[0][:, 0:w_], tmp[1][:, 0:w_], ALU.subtract, r=[ttmp[0], ttmp[1]], w=[ttmp[0]])
                self.act(dflat[:, r0 * 128:r0 * 128 + w_], tmp[0][:, 0:w_], AF.Exp, r=[ttmp[0]], w=[tD], bias=-math.log(16.0), scale=1.0)
            for which, dst, tdst in ((0, QT, tQT), (1, KT, tKT)):
                for tq in range(4):
                    tsl = slice(tq * 512, (tq + 1) * 512)
                    hts = [tHT[tq * 4 + i] for i in range(4)]
                    for c in range(2):
                        for kc in range(8):
                            self.mm(PB[c][:], W1[:, kc, which * 256 + c * 128: which * 256 + (c + 1) * 128], HT[:, kc, tsl],
                                    kc == 0, kc == 7, r=[tW1] + hts, w=[tPB[c]])
                    self.tt("vector", tmp[0], PB[0][:], cosT[:, tsl], ALU.mult, r=[tPB[0], tcs], w=[ttmp[0]])
                    self.tt("vector", tmp[1], PB[1][:], sinT[:, tsl], ALU.mult, r=[tPB[1], tcs], w=[ttmp[1]])
                    self.tt("vector", tmp[2], PB[0][:], sinT[:, tsl], ALU.mult, r=[tPB[0], tcs], w=[ttmp[2]])
                    self.tt("vector", tmp[3], PB[1][:], cosT[:, tsl], ALU.mult, r=[tPB[1], tcs], w=[ttmp[3]])
                    self.tt("gpsimd", dst[:, 0, tsl], tmp[0], tmp[1], ALU.subtract, r=[ttmp[0], ttmp[1]], w=[tdst])
                    self.tt("gpsimd", dst[:, 1, tsl], tmp[2], tmp[3], ALU.add, r=[ttmp[2], ttmp[3]], w=[tdst])
            for tb in range(NTB):
                pb = tb % 2
                for kc in range(8):
                    self.mm(PB[pb][:], HT[:, kc, tb * 128:(tb + 1) * 128], W2[:, kc, :], kc == 0, kc == 7, r=[tW2, tHT[tb]], w=[tPB[pb]])
                self.cp("scalar", VH[:, tb, :], PB[pb][:], r=[tPB[pb]], w=[tVH])
            self.dma("gpsimd", W1, rwin[:, 4096 + h * 512:4096 + (h + 1) * 512].rearrange("(k p) c -> p k c", p=128), w=[tW1])
            self.dma("gpsimd", W2o, I["rwout"][h * 512:(h + 1) * 512, :].rearrange("(k p) c -> p k c", p=128), w=[tW2])
            self.convert_tables(0, h, 4)
            pending = None
            for nb2 in range(8):
                nsl = slice(nb2 * 256, (nb2 + 1) * 256)
                for mb in range(NTB):
                    sb_ = (2, 3, 1)[mb % 3]
                    for kc in range(2):
                        self.mm(PB[sb_][:, 0:256], KT[:, kc, mb * 128:(mb + 1) * 128], QT[:, kc, nsl], kc == 0, kc == 1,
                                r=[tKT, tQT], w=[tPB[sb_]])
                    rel0 = 2 * nb2 - mb + 15
                    self.tt("vector", PT[:, mb, :], PB[sb_][:, 0:256], Dtab[:, rel0:rel0 + 2, :].rearrange("p a b -> p (a b)"), ALU.mult,
                            r=[tPB[sb_], tD], w=[tPT])
                for sub in range(2):
                    nb = nb2 * 2 + sub
                    ob = 4 + (nb % 2)
                    par = nb % 2
                    t_y, t_g = tmp[2 * par], tmp[2 * par + 1]
                    tt_y, tt_g = ttmp[2 * par], ttmp[2 * par + 1]
                    for mb in range(NTB):
                        self.mm(PB[ob][:], PT[:, mb, sub * 128:(sub + 1) * 128], VH[:, mb, :], mb == 0, mb == NTB - 1,
                                r=[tPT, tVH], w=[tPB[ob]])
                    for kc in range(8):
                        self.mm(PB[0][:], HT[:, kc, nb * 128:(nb + 1) * 128], W1[:, kc, :], kc == 0, kc == 7, r=[tW1, tHT[nb]], w=[tPB[0]])
                    if pending is not None:
                        pending()
                    self.act(t_g, PB[0][:], AF.Silu, r=[tPB[0]], w=[tt_g])
                    sc = scr2[par]
                    st6, mv, rstd, tsm_ = sc["st6"], sc["mv"], sc["rstd"], sc["t"]
                    self.v("vector", lambda e, ob=ob, st6=st6: e.bn_stats(out=st6[:, 0:6], in_=PB[ob][:]), r=[tPB[ob]], w=[tsm_])
                    self.v("vector", lambda e, st6=st6, mv=mv: e.bn_aggr(out=mv, in_=st6[:, 0:6]), r=[tsm_], w=[tsm_])
                    self.act(rstd, mv[:, 1:2], AF.Sqrt, r=[tsm_], w=[tsm_], bias=EPS, scale=1.0)
                    self.v("vector", lambda e, rstd=rstd: e.reciprocal(out=rstd, in_=rstd), r=[tsm_], w=[tsm_])
                    self.ts("vector", t_y, PB[ob][:], mv[:, 0:1], rstd, ALU.subtract, ALU.mult, r=[tPB[ob], tsm_], w=[tt_y])
                    self.tt("vector", t_y, t_y, gng, ALU.mult, r=[tt_y, tgn], w=[tt_y])
                    self.tt("vector", t_y, t_y, gnb, ALU.add, r=[tt_y, tgn], w=[tt_y])
                    self.tt("vector", Zb2[par], t_y, t_g, ALU.mult, r=[tt_y, tt_g], w=[tZb2[par]])

                    def tail(nb=nb, par=par):
                        pbT = PB[6][:].bitcast(BF16).rearrange("p (a b) -> p a b", a=8)
                        for fc in range(4):
                            self.tp(pbT[:, fc, :], Zb2[par][:, fc * 128:(fc + 1) * 128], self.identb, r=[tZb2[par], self.tC], w=[tPB[6]])
                        self.cp("scalar", ZT, pbT[:, 0:4, :], r=[tPB[6]], w=[tZT])
                        for half in range(2):
                            wb = 7
                            hs = slice(half * 512, (half + 1) * 512)
                            for fc in range(4):
                                self.mm(PB[wb][:], ZT[:, fc, :], W2o[:, fc, hs], fc == 0, fc == 3, r=[tZT, tW2], w=[tPB[wb]])
                            if h == 0:
                                self.stt("vector", self.RH[:, nb, hs], self.RH[:, nb, hs], ALPHA, PB[wb][:], ALU.mult, ALU.add,
                                         r=[self.tH[nb], tPB[wb]], w=[self.tH[nb]])
                            else:
                                self.tt("vector", self.RH[:, nb, hs], self.RH[:, nb, hs], PB[wb][:], ALU.add,
                                        r=[self.tH[nb], tPB[wb]], w=[self.tH[nb]])
                    pending = tail
            pending()
        for tb in range(0, NTB, 2):
            self.layer_norm_multi([tb, tb + 1], lng, lnb, tln, scr2)
        self.tRG = [tQT, tKT, tVH, tPT, tD, tW1, tW2]
        self.tXtmp = tZb2
        self.tXall = ttmp + [tgn, tZT, tsm, tcs, tl1] + tsm2

    def peer(self, l):
        I = self.I
        RX, RG, RHT = self.RX, self.RG, self.RHT
        PB, tPB = self.PB, self.tPB
        P = self.P
        lng, lnb, tln = self.load_ln(I["lnfg"][l:l + 1, :], I["lnfb"][l:l + 1, :], 31744)
        Hs = [self.carve(RX, i * 512, [128, 256], BF16) for i in range(2)]
        Wt = [self.carve(RX, 1024 + i * 512, [128, 256], BF16) for i in range(2)]
        tHs = [Tr("Hs%d" % i) for i in range(2)]
        tWt = [Tr("Wt%d" % i) for i in range(2)]
        P.alias(tHs + tWt, self.tXtmp)
        NS = 3
        UB = [self.carve(RX, 2048 + i * 4096, [128, 8, 128], BF16) for i in range(NS)]
        VB = [self.carve(RX, 2048 + i * 4096 + 2048, [128, D], BF16) for i in range(NS)]
        tU = [Tr("U%d" % i) for i in range(NS)]
        tV = [Tr("V%d" % i) for i in range(NS)]
        xT = [self.carve(RX, 14336 + i * 4096, [128, 8, 256], BF16) for i in range(2)]
        txT = [Tr("xT0"), Tr("xT1")]
        Ab = [self.carve(RX, 22528 + i * 2048, [128, 8, 128], BF16) for i in range(2)]
        Bb = [self.carve(RX, 26624 + i * 2048, [128, 8, 128], BF16) for i in range(2)]
        tA = [Tr("A%d" % i) for i in range(2)]
        tB = [Tr("B%d" % i) for i in range(2)]
        o0 = 22528
        tv = self.carve(RX, o0, [128, 16, 16], F32)
        ti = self.carve(RX, o0 + 1024, [128, 16, 16], U32)
        tif = self.carve(RX, o0 + 1024, [128, 16, 16], F32)
        cv = self.carve(RX, o0 + 2048, [128, 8, 16], F32)
        ci = self.carve(RX, o0 + 2560, [128, 8, 16], U32)
        ee = self.carve(RX, o0 + 3072, [128, 8, 16], F32)
        r1u = self.carve(RX, o0 + 3584, [128, 8, 16], U32)
        r1f = self.carve(RX, o0 + 3584, [128, 8, 16], F32)
        r2u = self.carve(RX, o0 + 4096, [128, 8, 16], U32)
        r2f = self.carve(RX, o0 + 4096, [128, 8, 16], F32)
        abw = [[self.carve(RX, o0 + 4608 + (tb2 * 3 + k_) * 512, [128, 8, 16], F32) for k_ in range(3)] for tb2 in range(2)]
        tsmall = [Tr("sm%d" % i) for i in range(13)]
        ttv0, tti0, tcv0, tci0, tee, tr1, tr2, tabw, ttv1, tti1, tcv1, tci1, ts2b = tsmall
        ttv, tti, tcv, tci = [ttv0, ttv1], [tti0, tti1], [tcv0, tcv1], [tci0, tci1]
        aT = self.RC[:, 640:768].bitcast(BF16)
        bT = self.RC[:, 768:896].bitcast(BF16)
        wT = self.RC[:, 896:1152]
        taT = Tr("abwT")
        small = self.carve(RX, 40064, [128, 64], F32)
        tsm = Tr("small")
        scr = {"st6": small[:, 0:12], "mv": small[:, 12:14], "rstd": small[:, 14:15], "t": tsm}
        Zs = small[:, 16:24]
        rz = small[:, 24:32]
        tZs = Tr("Zs")
        tsmB = Tr("smallB")
        scrB = {"st6": small[:, 32:44], "mv": small[:, 44:46], "rstd": small[:, 46:47], "t": tsmB}
        newX = tU + tV + txT + tA + tB + [taT, tsm, tZs, tsmB]
        P.alias(newX, self.tXall)
        Gs = self.carve(RG, 0, [128, 256, 128], BF16)
        tGs = Tr("Gs")
        P.alias([tGs], self.tRG)
        qT = self.carve(RHT, 0, [128, 16, 256], BF16)
        skT = self.carve(RHT, 8192, [128, 16, 128], BF16)
        wqc = [self.carve(RHT, 12288 + i * 2048, [128, 8, 128], BF16) for i in range(2)]
        s_ = self.carve(RHT, 16384, [128, 16, 128], F32)
        s2 = self.carve(RHT, 24576, [128, 16, 128], F32)
        cand = self.carve(RHT, 16384, [128, 8, 256], F32)
        cand2 = self.carve(RHT, 24576, [128, 8, 256], F32)
        eq = self.carve(RHT, 16384, [128, 8, 16, 16], F32)
        tq_, tsk, ts_, ts2 = [Tr(n) for n in ("qT", "skT", "s", "s2")]
        twq = [Tr("wqc0"), Tr("wqc1")]
        scratch_trs = [tq_, tsk, ts_, ts2] + twq
        P.alias(scratch_trs + [ts2b], self.tHT)
        def post(e):
            i_ = self._dmy_i
            self._dmy_i = (i_ + 1) % 28
            return e.memset(self.dmy[:, 2 * i_:2 * i_ + 2], 0.0)
        tvv = tv.rearrange("p (h c) r -> p h c r", c=2)
        tifv = tif.rearrange("p (h c) r -> p h c r", c=2)
        cand4 = cand.rearrange("p h (a b) -> p h a b", a=16)
        iota16 = self.iota16
        B4 = [128, 8, 16, 16]
        B3 = [128, 8, 16]
        B3b = [128, 8, 128]
        pwq = I["pwq"]

        def top16(src, src2, tsrc, tsrc2, vals, idxs, tvals, tidxs, n):
            hn = n // 2
            for g0 in range(hn):
                pair = ((g0, 0), (g0 + hn, 1))
                for g_, ln in pair:
                    self.v("vector", lambda e, g_=g_: e.max(out=vals[:, g_, 0:8], in_=src[:, g_, :]), r=[tsrc], w=[tvals[ln]])
                for g_, ln in pair:
                    self.v("vector", lambda e, g_=g_: e.max_index(out=idxs[:, g_, 0:8], in_max=vals[:, g_, 0:8], in_values=src[:, g_, :]),
                           r=[tsrc, tvals[ln]], w=[tidxs[ln]], post=post)
                for g_, ln in pair:
                    self.v("vector", lambda e, g_=g_: e.match_replace(out=src2[:, g_, :], in_to_replace=vals[:, g_, 0:8], in_values=src[:, g_, :], imm_value=-1e30),
                           r=[tsrc, tvals[ln]], w=[tsrc2[ln]], post=post)
                yield
                for g_, ln in pair:
                    self.v("vector", lambda e, g_=g_: e.max(out=vals[:, g_, 8:16], in_=src2[:, g_, :]), r=[tsrc2[ln]], w=[tvals[ln]])
                for g_, ln in pair:
                    self.v("vector", lambda e, g_=g_: e.max_index(out=idxs[:, g_, 8:16], in_max=vals[:, g_, 8:16], in_values=src2[:, g_, :]),
                           r=[tsrc2[ln], tvals[ln]], w=[tidxs[ln]], post=post)
                yield

        def phase1a(g, buf):
            for tb2 in range(2):
                tb = 2 * g + tb2
                for kq in range(2):
                    pb = 6 + kq
                    for j in range(4):
                        kc = kq * 4 + j
                        self.tp(PB[pb][:, j * 128:(j + 1) * 128], self.RH[:, tb, kc * 128:(kc + 1) * 128], self.identf,
                                r=[self.tH[tb], self.tC], w=[tPB[pb]])
                    self.cp("scalar", xT[buf][:, kq * 4:(kq + 1) * 4, tb2 * 128:(tb2 + 1) * 128],
                            PB[pb][:].rearrange("p (a b) -> p a b", a=4), r=[tPB[pb]], w=[txT[buf]])
            yield
            for i in range(2):
                self.dma("gpsimd", skT[:, i * 8:(i + 1) * 8, :], I["pskT"][l, :, i * 8:(i + 1) * 8, :], w=[tsk])
            for gq in range(16):
                wb = gq % 2
                self.dma("gpsimd", wqc[wb], pwq[l, :, gq * 128:(gq + 1) * 128].rearrange("(k p) c -> p k c", p=128), w=[twq[wb]])
                pb = 6 + gq % 2
                for kc in range(8):
                    self.mm(PB[pb][:, 0:256], wqc[wb][:, kc, :], xT[buf][:, kc, :], kc == 0, kc == 7, r=[twq[wb], txT[buf]], w=[tPB[pb]])
                self.cp("scalar", qT[:, gq, :], PB[pb][:, 0:256], r=[tPB[pb]], w=[tq_])
                yield
            for tb2 in range(2):
                a_, b_, w_ = abw[tb2]
                for q4 in range(4):
                    pb = 6 + q4 % 2
                    for j in range(4):
                        gq = q4 * 4 + j
                        self.mm(PB[pb][:, j * 128:(j + 1) * 128], qT[:, gq, tb2 * 128:(tb2 + 1) * 128], skT[:, gq, :], True, True,
                                r=[tq_, tsk], w=[tPB[pb]])
                    self.cp("scalar", s_[:, q4 * 4:(q4 + 1) * 4, :], PB[pb][:].rearrange("p (a b) -> p a b", a=4), r=[tPB[pb]], w=[ts_])
                yield
                if tb2 == 0:
                    P.alias(tsmall, tA + tB)
                yield from top16(s_, s2, ts_, [ts2, ts2b], tv, ti, ttv, tti, 16)
                self.cp("vector", tif, ti, r=tti, w=tti)
                self.tt("vector", cand4, tvv[:, :, 0, :].unsqueeze(3).broadcast_to(B4), tvv[:, :, 1, :].unsqueeze(2).broadcast_to(B4), ALU.add,
                        r=ttv, w=[ts_])
                yield
                yield from top16(cand, cand2, ts_, [ts2, ts2b], cv, ci, tcv, tci, 8)
                self.tt("vector", ee, cv, cv[:, :, 0:1].broadcast_to(B3), ALU.subtract, r=tcv, w=[tee])
                self.act(ee, ee, AF.Exp, r=[tee], w=[tee])
                self.v("vector", lambda e: e.reduce_sum(out=Zs, in_=ee, axis=AX.X), r=[tee], w=[tZs])
                self.v("vector", lambda e: e.reciprocal(out=rz, in_=Zs), r=[tZs], w=[tZs])
                self.tt("vector", w_, ee, rz.unsqueeze(2).broadcast_to(B3), ALU.mult, r=[tee, tZs], w=[tabw])
                yield
                self.v("vector", lambda e: e.tensor_single_scalar(out=r1u, in_=ci, scalar=4, op=ALU.logical_shift_right), r=tci, w=[tr1])
                self.v("vector", lambda e: e.tensor_single_scalar(out=r2u, in_=ci, scalar=15, op=ALU.bitwise_and), r=tci, w=[tr2])
                self.cp("vector", r1f, r1u, r=[tr1], w=[tr1])
                self.cp("vector", r2f, r2u, r=[tr2], w=[tr2])
                yield
                io4 = iota16.unsqueeze(1).unsqueeze(1).broadcast_to(B4)
                for (rf, trf, cc, dst) in ((r1f, tr1, 0, a_), (r2f, tr2, 1, b_)):
                    self.tt("vector", eq, rf.unsqueeze(3).broadcast_to(B4), io4, ALU.is_equal, r=[trf, self.tC], w=[ts_])
                    self.tt("vector", eq, eq, tifv[:, :, cc, :].unsqueeze(2).broadcast_to(B4), ALU.mult, r=[ts_] + tti, w=[ts_])
                    self.v("vector", lambda e, dst=dst: e.tensor_reduce(out=dst, in_=eq, axis=AX.X, op=ALU.add), r=[ts_], w=[tabw])
                    yield
            yield "TAIL"
            for tb2 in range(2):
                a_, b_, w_ = abw[tb2]
                pb = 6 + tb2
                for k_, src in enumerate((a_, b_, w_)):
                    self.tp(PB[pb][:, k_ * 128:(k_ + 1) * 128], src.rearrange("p a b -> p (a b)"), self.identf, r=[tabw, self.tC], w=[tPB[pb]])
                for k_, dstT in enumerate((aT, bT, wT)):
                    self.cp("scalar", dstT[:, tb2 * 128:(tb2 + 1) * 128], PB[pb][:, k_ * 128:(k_ + 1) * 128], r=[tPB[pb]], w=[taT])
            P.alias(tA + tB, tsmall)

        def build(g):
            for t8 in range(32):
                sl = t8 % 2
                tsl8 = slice(t8 * 8, (t8 + 1) * 8)
                iob = self.iotab.unsqueeze(1).broadcast_to(B3b)
                self.tt("vector", Ab[sl], iob, aT[:, tsl8].unsqueeze(2).broadcast_to(B3b), ALU.is_equal, r=[self.tC, taT], w=[tA[sl]])
                self.tt("vector", Bb[sl], iob, bT[:, tsl8].unsqueeze(2).broadcast_to(B3b), ALU.is_equal, r=[self.tC, taT], w=[tB[sl]])
                self.tt("vector", Ab[sl], Ab[sl], wT[:, tsl8].unsqueeze(2).broadcast_to(B3b), ALU.mult, r=[tA[sl], taT], w=[tA[sl]])
                for q in range(2):
                    pb = 6 + q
                    for j in range(4):
                        jj = q * 4 + j
                        self.mm(PB[pb][:, j * 128:(j + 1) * 128], Bb[sl][:, jj, :], Ab[sl][:, jj, :], True, True, r=[tA[sl], tB[sl]], w=[tPB[pb]])
                    t0_ = t8 * 8 + q * 4
                    self.cp("scalar", Gs[:, t0_:t0_ + 4, :], PB[pb][:].rearrange("p (a b) -> p a b", a=4), r=[tPB[pb]], w=[tGs])

        gen = phase1a(0, 0)
        for _ in gen:
            pass
        for g in range(8):
            buf = g % 2
            gen = phase1a(g + 1, 1 - buf) if g + 1 < 8 else iter(())
            gen_tail = False
            for _ in range(0):
                try:
                    next(gen)
                except StopIteration:
                    break
            build(g)

            def pull(n):
                nonlocal gen_tail
                if gen_tail:
                    return
                for _ in range(n):
                    try:
                        r_ = next(gen)
                    except StopIteration:
                        gen_tail = True
                        return
                    if r_ == "TAIL":
                        gen_tail = True
                        return

            def load(c):
                sl = c % NS
                self.dma("sync", UB[sl].rearrange("p k e -> p (k e)"), self.ubf[l, c * 128:(c + 1) * 128, :], r=[self.tUbf[l]], w=[tU[sl]])
                self.dma("sync", VB[sl], self.vbf[l, c * 128:(c + 1) * 128, :], r=[self.tVbf[l]], w=[tV[sl]])

            def umm(c):
                pp = 4 + c % 2
                sl = c % NS
                for kc in range(8):
                    self.mm(PB[pp][:, 0:256], UB[sl][:, kc, :], xT[buf][:, kc, :], kc == 0, kc == 7, r=[tU[sl], txT[buf]], w=[tPB[pp]])

            def mid(c):
                pp = 4 + c % 2
                self.act(Hs[c % 2], PB[pp][:, 0:256], AF.Gelu, r=[tPB[pp]], w=[tHs[c % 2]])
                self.tt(MAIN_MULT_ENG, Wt[c % 2], Hs[c % 2], Gs[:, :, c], ALU.mult, r=[tHs[c % 2], tGs], w=[tWt[c % 2]])

            def vmm(c):
                sl = c % NS
                for tb2 in range(2):
                    for half in range(2):
                        bk = tb2 * 2 + half
                        self.mm(PB[bk][:], Wt[c % 2][:, tb2 * 128:(tb2 + 1) * 128], VB[sl][:, half * 512:(half + 1) * 512], c == 0, c == 127,
                                r=[tWt[c % 2], tV[sl]], w=[tPB[bk]])

            for c in range(NS - 1):
                load(c)
            umm(0); mid(0)
            for c in range(128):
                if c + NS - 1 < 128:
                    load(c + NS - 1)
                if c + 1 < 128:
                    umm(c + 1); mid(c + 1)
                vmm(c)
                pull(1)
            while True:
                try:
                    next(gen)
                except StopIteration:
                    break
            for tb2 in range(2):
                tb = 2 * g + tb2
                for half in range(2):
                    hs = slice(half * 512, (half + 1) * 512)
                    bk = tb2 * 2 + half
                    self.stt("vector", self.RH[:, tb, hs], self.RH[:, tb, hs], ALPHA, PB[bk][:], ALU.mult, ALU.add,
                             r=[self.tH[tb], tPB[bk]], w=[self.tH[tb]])
            self.layer_norm_multi([2 * g, 2 * g + 1], lng, lnb, tln, [scr, scrB])
        P.alias(self.tHT, scratch_trs + [ts2b])
        self.tRG = [tGs]
        self.tXtmp = tHs + tWt
        self.tXall = newX + tsmall

    def mla(self):
        I = self.I
        RX, RG = self.RX, self.RG
        PB, tPB = self.PB, self.tPB
        HT, tHT = self.HT, self.tHT
        P = self.P
        SCALE = 192.0 ** -0.5
        lng, lnb, tln = self.load_ln(I["lnmg"][1:2, :], I["lnmb"][1:2, :], 31744)
        self.build_HT()
        cos64 = self.carve(RX, 2048, [128, S], F32)
        sin64 = self.carve(RX, 2048 + 8192, [128, S], F32)
        tcs = Tr("cs64")
        tmp = [self.carve(RX, 18432 + i * 2048, [128, 512], F32) for i in range(4)]
        ttmp = [Tr("mtmp%d" % i) for i in range(4)]
        mwsw = self.carve(RX, 26624, [128, 8, 64], BF16)
        wuqsw = self.carve(RX, 27648, [128, 3, 8, 64], BF16)
        tsw = Tr("sw")
        gains = self.carve(RX, 39936, [128, 8], F32)
        tgain = Tr("gains")
        small = self.carve(RX, 40064, [128, 64], F32)
        tsm = Tr("small")
        scr = {"st6": small[:, 0:12], "mv": small[:, 12:14], "rstd": small[:, 14:15], "t": tsm}
        newX = [tcs, tsw, tgain, tsm] + ttmp
        P.alias(newX, self.tXall)
        Ob2 = [self.carve(RX, 512 + i * 256, [128, 128], BF16) for i in range(2)]
        OT = self.carve(RX, 256, [128, 128], BF16)
        tOb2, tOT = [Tr("Ob0"), Tr("Ob1")], Tr("OT")
        P.alias(tOb2 + [tOT], self.tXtmp)
        rz2 = [small[:, 16:17], small[:, 17:18]]
        trz2 = [Tr("rz0"), Tr("rz1")]
        P.alias(trz2, self.tXall)
        self.trig_tables(1, 64, cos64, sin64, tcs, sgn_col=2)
        cnT = self.carve(RG, 0, [128, 5, S], BF16)
        kropeT = self.carve(RG, 20480, [128, S], BF16)
        wuq = self.carve(RG, 24576, [128, 3, 1536], BF16)
        wukv = self.carve(RG, 33792, [128, 2, 2048], BF16)
        wout = self.carve(RG, 41984, [128, 8, 1024], BF16)
        mwin = self.carve(RG, 41984, [128, 8, 704], BF16)
        craw = self.carve(RG, 58368, [128, 3, 512], F32)
        tcn, tkr, twuq, twukv, twout, tcraw = [Tr(n) for n in ("cnT", "kropeT", "wuq", "wukv", "wout", "craw")]
        P.alias([tcn, tkr, twuq, twukv, twout, tcraw], self.tRG)
        self.dma("gpsimd", mwin, I["mwin"].rearrange("(k p) c -> p k c", p=128), w=[twout])
        for kc in range(3):
            self.dma("gpsimd", wuq[:, kc, :], I["mwuq"][kc * 128:(kc + 1) * 128, :], w=[twuq])
        for kc in range(2):
            for hh in range(2):
                self.dma("gpsimd", wukv[:, kc, hh * 1024:(hh + 1) * 1024], I["mwukv"][kc * 128:(kc + 1) * 128, hh * 1024:(hh + 1) * 1024], w=[twukv])
        self.dma("sync", gains[:, 0:3], I["mqn"], w=[tgain])
        self.dma("sync", gains[:, 3:5], I["mkvn"], w=[tgain])
        self.cp("vector", mwsw[:, :, 0:32], mwin[:, :, 672:704], r=[twout], w=[tsw])
        self.cp("vector", mwsw[:, :, 32:64], mwin[:, :, 640:672], r=[twout], w=[tsw])
        wuqv = wuq.rearrange("p k (h c) -> p k h c", h=8)
        self.cp("vector", wuqsw[:, :, :, 0:32], wuqv[:, :, :, 160:192], r=[twuq], w=[tsw])
        self.cp("vector", wuqsw[:, :, :, 32:64], wuqv[:, :, :, 128:160], r=[twuq], w=[tsw])
        self.v("gpsimd", lambda e: e.memset(kropeT[64:128, :], 0.0), w=[tkr])
        MS = int(os.environ.get("MLA_STOP", "99"))
        if MS <= 1:
            return
        for tq in range(4):
            tsl = slice(tq * 512, (tq + 1) * 512)
            hts = [tHT[tq * 4 + i] for i in range(4)]
            for (fcs, sumbank, nfeat, goff) in (((0, 1, 2), 2, 384.0, 0), ((3, 4), 3, 256.0, 3)):
                for i_, fc in enumerate(fcs):
                    pb = fc % 2
                    for kc in range(8):
                        self.mm(PB[pb][:], mwin[:, kc, fc * 128:(fc + 1) * 128], HT[:, kc, tsl], kc == 0, kc == 7, r=[twout] + hts, w=[tPB[pb]])
                    self.cp("vector", craw[:, i_, :], PB[pb][:], r=[tPB[pb]], w=[tcraw])
                    if "norm" in os.environ.get("MLA_SKIP", ""):
                        continue
                    self.tt("vector", tmp[pb], craw[:, i_, :], PB[pb][:], ALU.mult, r=[tPB[pb], tcraw], w=[ttmp[pb]])
                    if "ones" not in os.environ.get("MLA_SKIP", ""):
                        self.mm(PB[sumbank][:], self.onesf, tmp[pb], i_ == 0, i_ == len(fcs) - 1, r=[self.tC, ttmp[pb]], w=[tPB[sumbank]])
                if "norm" in os.environ.get("MLA_SKIP", "") or "sqrt" in os.environ.get("MLA_SKIP", ""):
                    continue
                self.act(tmp[2], PB[sumbank][:], AF.Sqrt, r=[tPB[sumbank]], w=[ttmp[2]], scale=1.0 / nfeat, bias=EPS)
                self.v("vector", lambda e: e.reciprocal(out=tmp[2], in_=tmp[2]), r=[ttmp[2]], w=[ttmp[2]])
                for i_, fc in enumerate(fcs):
                    self.stt("vector", cnT[:, fc, tsl], craw[:, i_, :], gains[:, goff + i_:goff + i_ + 1], tmp[2], ALU.mult, ALU.mult,
                             r=[tcraw, tgain, ttmp[2]], w=[tcn])
            if "rope" in os.environ.get("MLA_SKIP", ""):
                continue
            for kc in range(8):
                self.mm(PB[6][0:64, :], mwin[:, kc, 640:704], HT[:, kc, tsl], kc == 0, kc == 7, r=[twout] + hts, w=[tPB[6]])
            for kc in range(8):
                self.mm(PB[7][0:64, :], mwsw[:, kc, :], HT[:, kc, tsl], kc == 0, kc == 7, r=[tsw] + hts, w=[tPB[7]])
            self.tt("vector", tmp[0][0:64], PB[6][0:64, :], cos64[0:64, tsl], ALU.mult, r=[tPB[6], tcs], w=[ttmp[0]])
            self.tt("vector", tmp[1][0:64], PB[7][0:64, :], sin64[0:64, tsl], ALU.mult, r=[tPB[7], tcs], w=[ttmp[1]])
            self.tt("gpsimd", kropeT[0:64, tsl], tmp[0][0:64], tmp[1][0:64], ALU.add, r=[ttmp[0], ttmp[1]], w=[tkr])
        if MS <= 2:
            return
        self.dma("gpsimd", wout, I["mwout"].rearrange("(h p) c -> p h c", p=128), w=[twout])
        RHT = self.RHT
        qnT = self.carve(RHT, 0, [128, S], BF16)
        qrT = self.carve(RHT, 4096, [128, S], BF16)
        knT = self.carve(RHT, 8192, [128, S], BF16)
        vh = self.carve(RHT, 12288, [128, NTB, 132], BF16)
        PT = self.carve(RHT, 16512, [128, NTB, 256], BF16)
        tqn, tqr, tkn, tvh, tPT = [Tr(n) for n in ("qnT", "qrT", "knT", "vh", "PTm")]
        P.alias([tqn, tqr, tkn, tvh, tPT], tHT)
        self.v("vector", lambda e: e.memset(vh[:, :, 128:129], 1.0), w=[tvh])
        self.v("gpsimd", lambda e: e.memset(qrT[64:128, :], 0.0), w=[tqr])
        for h in range(8):
            self.convert_tables(1, h, 8)
            for tq in range(4):
                tsl = slice(tq * 512, (tq + 1) * 512)
                for kc in range(3):
                    self.mm(PB[0][:], wuq[:, kc, 192 * h:192 * h + 128], cnT[:, kc, tsl], kc == 0, kc == 2, r=[twuq, tcn], w=[tPB[0]])
                self.cp("scalar", qnT[:, tsl], PB[0][:], r=[tPB[0]], w=[tqn])
                for kc in range(2):
                    self.mm(PB[1][:], wukv[:, kc, 256 * h:256 * h + 128], cnT[:, 3 + kc, tsl], kc == 0, kc == 1, r=[twukv, tcn], w=[tPB[1]])
                self.cp("scalar", knT[:, tsl], PB[1][:], r=[tPB[1]], w=[tkn])
                for kc in range(3):
                    self.mm(PB[6][0:64, :], wuq[:, kc, 192 * h + 128:192 * h + 192], cnT[:, kc, tsl], kc == 0, kc == 2, r=[twuq, tcn], w=[tPB[6]])
                for kc in range(3):
                    self.mm(PB[7][0:64, :], wuqsw[:, kc, h, :], cnT[:, kc, tsl], kc == 0, kc == 2, r=[tsw, tcn], w=[tPB[7]])
                self.tt("vector", tmp[0][0:64], PB[6][0:64, :], cos64[0:64, tsl], ALU.mult, r=[tPB[6], tcs], w=[ttmp[0]])
                self.tt("vector", tmp[1][0:64], PB[7][0:64, :], sin64[0:64, tsl], ALU.mult, r=[tPB[7], tcs], w=[ttmp[1]])
                self.tt("gpsimd", qrT[0:64, tsl], tmp[0][0:64], tmp[1][0:64], ALU.add, r=[ttmp[0], ttmp[1]], w=[tqr])
            if MS <= 3:
                return
            for tb in range(NTB):
                pb = tb % 2
                for kc in range(2):
                    self.mm(PB[pb][:, 0:128], cnT[:, 3 + kc, tb * 128:(tb + 1) * 128], wukv[:, kc, 256 * h + 128:256 * h + 256], kc == 0, kc == 1,
                            r=[twukv, tcn], w=[tPB[pb]])
                self.cp("scalar", vh[:, tb, 0:128], PB[pb][:, 0:128], r=[tPB[pb]], w=[tvh])
            if MS <= 4:
                return
            pending = None
            for nb2 in range(8):
                nsl = slice(nb2 * 256, (nb2 + 1) * 256)
                for mb in range(NTB):
                    sb_ = (2, 3, 0)[mb % 3]
                    msl = slice(mb * 128, (mb + 1) * 128)
                    self.mm(PB[sb_][:, 0:256], knT[:, msl], qnT[:, nsl], True, False, r=[tkn, tqn], w=[tPB[sb_]])
                    self.mm(PB[sb_][:, 0:256], kropeT[:, msl], qrT[:, nsl], False, True, r=[tkr, tqr], w=[tPB[sb_]])
                    self.act(PT[:, mb, :], PB[sb_][:, 0:256], AF.Exp, r=[tPB[sb_]], w=[tPT], scale=SCALE)
                for sub in range(2):
                    nb = nb2 * 2 + sub
                    ob = 4 + (nb % 2)
                    par = nb % 2
                    for mb in range(NTB):
                        self.mm(PB[ob][:, 0:129], PT[:, mb, sub * 128:(sub + 1) * 128], vh[:, mb, 0:129], mb == 0, mb == NTB - 1,
                                r=[tPT, tvh], w=[tPB[ob]])
                    if pending is not None:
                        pending()

                    def tail(nb=nb, ob=ob, par=par, h=h):
                        rz_ = rz2[par]
                        self.v("vector", lambda e: e.reciprocal(out=rz_, in_=PB[ob][:, 128:129]), r=[tPB[ob]], w=[trz2[par]])
                        self.ts("vector", Ob2[par], PB[ob][:, 0:128], rz_, None, ALU.mult, r=[tPB[ob], trz2[par]], w=[tOb2[par]])
                        pbT = PB[6][:].bitcast(BF16)
                        self.tp(pbT[:, 0:128], Ob2[par], self.identb, r=[tOb2[par], self.tC], w=[tPB[6]])
                        self.cp("scalar", OT, pbT[:, 0:128], r=[tPB[6]], w=[tOT])
                        for half in range(2):
                            wb = 7 if half == 0 else 1
                            hs = slice(half * 512, (half + 1) * 512)
                            self.mm(PB[wb][:], OT, wout[:, h, hs], True, True, r=[tOT, twout], w=[tPB[wb]])
                            if h == 0:
                                self.stt("vector", self.RH[:, nb, hs], self.RH[:, nb, hs], ALPHA, PB[wb][:], ALU.mult, ALU.add,
                                         r=[self.tH[nb], tPB[wb]], w=[self.tH[nb]])
                            else:
                                self.tt("vector", self.RH[:, nb, hs], self.RH[:, nb, hs], PB[wb][:], ALU.add,
                                        r=[self.tH[nb], tPB[wb]], w=[self.tH[nb]])
                    pending = tail
            pending()
            pending = None
        for tb in range(NTB):
            self.layer_norm(tb, lng, lnb, tln, scr)
        P.alias(tHT, [tqn, tqr, tkn, tvh, tPT])
        self.tRG = [tcn, tkr, twuq, twukv, twout, tcraw]
        self.tXtmp = tOb2 + [tOT]
        self.tXall = newX + trz2


def build_nc(stop_after=None):
    nc = bass.Bass("TRN2", target_bir_lowering=False)
    k = K(nc, stop_after=stop_after)
    k.build()
    return nc


_CACHE = {}


def host_consts():
    invf = np.zeros((128, 4), np.float32)
    invf[:, 0] = (10000.0 ** (-np.arange(0, 256, 2, dtype=np.float32) / np.float32(256))).astype(np.float32)
    f32 = (10000.0 ** (-np.arange(0, 64, 2, dtype=np.float32) / np.float32(64))).astype(np.float32)
    invf[0:32, 1] = f32
    invf[32:64, 1] = f32
    invf[0:32, 2] = -1.0
    invf[32:64, 2] = 1.0
    return invf


def make_in_maps(inp):
    c = np.ascontiguousarray
    f = lambda k: np.asarray(inp[k], dtype=np.float32)
    shared = {
        "rwin": c(f("ret_w_in")[0]), "rl1d": c(f("ret_log1m_decay")[0].reshape(1, 8)),
        "rgng": c(f("ret_gn_g")[0].reshape(1, 2048)), "rgnb": c(f("ret_gn_b")[0].reshape(1, 2048)),
        "rwout": c(f("ret_w_out")[0]),
        "mwin": c(f("mla_w_in")[0]), "mqn": c(f("mla_q_norm")[0].reshape(3, 128).T),
        "mkvn": c(f("mla_kv_norm")[0].reshape(2, 128).T), "mwuq": c(f("mla_w_uq")[0]),
        "mwukv": c(f("mla_w_ukv")[0]), "mwout": c(f("mla_w_out")[0]),
        "pwq": c(f("peer_w_q")),
        "pskT": c(f("peer_sub_keys").reshape(2, 16, 128, 128).transpose(0, 3, 1, 2)),
        "puT": c(f("peer_u").reshape(2, 128, 128, 8, 128).transpose(0, 1, 4, 3, 2).reshape(2, 128, 128, 1024)),
        "pv": c(f("peer_v")),
        "lnmg": c(f("ln_mix_g")), "lnmb": c(f("ln_mix_b")), "lnfg": c(f("ln_ffn_g")), "lnfb": c(f("ln_ffn_b")),
        "cinvf": host_consts(),
    }
    x = f("x")
    pos = np.asarray(inp["positions"], dtype=np.int32)
    maps = []
    for b in range(8):
        m = dict(shared)
        m["x"] = c(x[b])
        m["pos"] = c(pos[b].reshape(1, S))
        maps.append(m)
    return maps


def kernel(**inputs):
    if "nc" not in _CACHE:
        _CACHE["nc"] = build_nc()
    nc = _CACHE["nc"]
    maps = make_in_maps(inputs)
    res = run_bass_kernel_spmd(nc, maps, core_ids=list(range(8)))
    out = np.stack([np.asarray(r["out"], dtype=np.float32) for r in res.results], axis=0)
    return out
```
